# Optimizing a Trainium2 kernel written in Bass

```python
import jax, jax.numpy as jnp
from jax import lax
import numpy as np

D_MODEL = 1024
BATCH = 2
SEQ = 8192
DEPTH = 2
DEC_BATCH = 32
DEC_SEQ = 64
PAST_LEN = 4096

CHUNK = 64
N_META = 16
D_CONV = D_MODEL
CONV_K = 31
D_ML = 2 * D_MODEL
ML_HEADS = 4
ML_HEAD_DIM = D_ML // ML_HEADS
QK_CONV_K = 4
EPS = 1e-6
SPLITS = tuple(int(s) for s in np.cumsum([D_CONV, D_CONV, D_CONV, D_ML, D_ML, D_ML, D_ML, D_ML, ML_HEADS, ML_HEADS, D_MODEL]))
D_IN = 3 * D_CONV + 5 * D_ML + 2 * ML_HEADS + 2 * D_MODEL

kernel_name = 'hybrid_conformer_mlstm_stream_step'


def rms_norm(x, g):
    xf = x.astype(jnp.float32)
    y = xf * lax.rsqrt(jnp.mean(xf * xf, axis=-1, keepdims=True) + EPS)
    return (y * g.astype(jnp.float32)).astype(x.dtype)


def layer_norm(x, g, b):
    xf = x.astype(jnp.float32)
    mu = jnp.mean(xf, axis=-1, keepdims=True)
    xc = xf - mu
    y = xc * lax.rsqrt(jnp.mean(xc * xc, axis=-1, keepdims=True) + EPS)
    return (y * g.astype(jnp.float32) + b.astype(jnp.float32)).astype(x.dtype)


def causal_dwconv(buf, x, w, b):
    xp = jnp.concatenate([buf.astype(x.dtype), x], axis=1)
    y = lax.conv_general_dilated(xp, w[:, None, :].astype(xp.dtype), window_strides=(1,), padding='VALID',
                                 dimension_numbers=('NWC', 'WIO', 'NWC'), feature_group_count=x.shape[-1])
    return y + b, xp[:, -(w.shape[0] - 1):]


def mlstm_chunk(state, q, k, v, ipre, logf):
    C, n, m = state
    L = q.shape[1]
    causal = jnp.tril(jnp.ones((L, L), dtype=bool))[None, :, :, None]
    b = jnp.cumsum(logf, axis=1)
    inter = b + m[:, None, :]
    dmat = b[:, :, None, :] - b[:, None, :, :] + ipre[:, None, :, :]
    dmat = jnp.where(causal, dmat, -jnp.inf)
    m_t = jnp.maximum(inter, jnp.max(dmat, axis=2))
    w_intra = jnp.exp(dmat - m_t[:, :, None, :])
    w_inter = jnp.exp(inter - m_t)
    s = jnp.einsum('bthd,bshd->btsh', q, k) * w_intra
    num = jnp.einsum('btsh,bshd->bthd', s, v) + w_inter[..., None] * jnp.einsum('bhed,bthd->bthe', C, q)
    den = jnp.sum(s, axis=2) + w_inter * jnp.einsum('bhd,bthd->bth', n, q)
    h = num / jnp.maximum(jnp.abs(den), jnp.exp(-m_t))[..., None]
    m_new = m_t[:, -1]
    decay = jnp.exp(b[:, -1] + m - m_new)
    w_s = jnp.exp(b[:, -1:] - b + ipre - m_new[:, None])
    C_new = decay[..., None, None] * C + jnp.einsum('bsh,bshe,bshd->bhed', w_s, v, k)
    n_new = decay[..., None] * n + jnp.einsum('bsh,bshd->bhd', w_s, k)
    return h, (C_new, n_new, m_new)


def mlstm_sequence(state, q, k, v, ipre, logf, lead):
    state = tuple(s.astype(jnp.float32) for s in state)
    outs = []
    if lead > 0:
        h0, state = mlstm_chunk(state, q[:, :lead], k[:, :lead], v[:, :lead], ipre[:, :lead], logf[:, :lead])
        outs.append(h0)
        q, k, v, ipre, logf = q[:, lead:], k[:, lead:], v[:, lead:], ipre[:, lead:], logf[:, lead:]
    bsz, T = q.shape[0], q.shape[1]
    if T <= CHUNK:
        h1, state = mlstm_chunk(state, q, k, v, ipre, logf)
    else:
        nc = T // CHUNK

        def to_blocks(a):
            return jnp.moveaxis(a.reshape((bsz, nc, CHUNK) + a.shape[2:]), 1, 0)

        def step(st, xs):
            h, st = mlstm_chunk(st, *xs)
            return st, h

        state, hs = lax.scan(step, state, (to_blocks(q), to_blocks(k), to_blocks(v), to_blocks(ipre), to_blocks(logf)))
        h1 = jnp.moveaxis(hs, 0, 1).reshape((bsz, T) + q.shape[2:])
    outs.append(h1)
    return jnp.concatenate(outs, axis=1), state


def mixer_layer(x, conv_buf, qk_buf, ml_state, lead, norm_pre, norm_post, w_in, b_in, w_dw, b_dw, ln_g, ln_b,
                w_qkc, b_qkc, f_bias, ml_norm, w_conv_out, w_ml_out, w_out):
    bsz, T, _ = x.shape
    h = rms_norm(x, norm_pre)
    proj = h @ w_in + b_in
    a, gl, z_c, q_pre, k_pre, v, o_pre, z_m, i_pre, f_pre, g_c, g_m = jnp.split(proj, SPLITS, axis=-1)
    u = a * jax.nn.sigmoid(gl)
    u, conv_buf = causal_dwconv(conv_buf, u, w_dw, b_dw)
    u = jax.nn.silu(layer_norm(u, ln_g, ln_b)) * jax.nn.silu(z_c)
    br_c = u @ w_conv_out
    qk, qk_buf = causal_dwconv(qk_buf, jnp.concatenate([q_pre, k_pre], axis=-1), w_qkc, b_qkc)
    q, k = jnp.split(jax.nn.silu(qk), 2, axis=-1)

    def heads(t):
        return t.reshape(bsz, T, ML_HEADS, ML_HEAD_DIM).astype(jnp.float32)

    logf = jax.nn.log_sigmoid((f_pre + f_bias).astype(jnp.float32))
    hc, ml_state = mlstm_sequence(ml_state, heads(q), heads(k) * (ML_HEAD_DIM ** -0.5), heads(v),
                                  i_pre.astype(jnp.float32), logf, lead)
    hc = hc * lax.rsqrt(jnp.mean(hc * hc, axis=-1, keepdims=True) + EPS)
    hc = (hc.reshape(bsz, T, D_ML) * ml_norm.astype(jnp.float32)).astype(x.dtype)
    hm = hc * jax.nn.sigmoid(o_pre) * jax.nn.silu(z_m)
    br_m = hm @ w_ml_out
    y = jax.nn.sigmoid(g_c) * br_c + jax.nn.sigmoid(g_m) * br_m
    out = y @ w_out
    return x + rms_norm(out, norm_post), conv_buf, qk_buf, ml_state


def setup_inputs(seed: int = 0) -> dict:
    key = jax.random.key(seed)
    ks = jax.random.split(key, 32)
    f32 = jnp.float32
    nrm = lambda k, s, sc: jax.random.normal(k, s, f32) * sc
    return {
        'x_prompt': nrm(ks[0], (BATCH, SEQ, D_MODEL), 1.0),
        'x_sample': nrm(ks[1], (DEC_BATCH, DEC_SEQ, D_MODEL), 1.0),
        'state_conv': nrm(ks[2], (DEPTH, DEC_BATCH, CONV_K - 1, D_CONV), 0.5),
        'state_qk_conv': nrm(ks[3], (DEPTH, DEC_BATCH, QK_CONV_K - 1, 2 * D_ML), 1.0),
        'state_C': nrm(ks[4], (DEPTH, DEC_BATCH, ML_HEADS, ML_HEAD_DIM, ML_HEAD_DIM), 0.05),
        'state_n': nrm(ks[5], (DEPTH, DEC_BATCH, ML_HEADS, ML_HEAD_DIM), 0.1),
        'state_m': jax.random.uniform(ks[6], (DEPTH, DEC_BATCH, ML_HEADS), f32, 0.0, 4.0),
        'meta_tokens': nrm(ks[7], (N_META, D_MODEL), 1.0),
        'norm_pre': 1.0 + nrm(ks[8], (DEPTH, D_MODEL), 0.05),
        'norm_post': 1.0 + nrm(ks[9], (DEPTH, D_MODEL), 0.05),
        'w_in': nrm(ks[10], (DEPTH, D_MODEL, D_IN), D_MODEL ** -0.5),
        'b_in': nrm(ks[11], (DEPTH, D_IN), 0.02),
        'w_dw': nrm(ks[12], (DEPTH, CONV_K, D_CONV), CONV_K ** -0.5),
        'b_dw': nrm(ks[13], (DEPTH, D_CONV), 0.02),
        'ln_g': 1.0 + nrm(ks[14], (DEPTH, D_CONV), 0.05),
        'ln_b': nrm(ks[15], (DEPTH, D_CONV), 0.02),
        'w_qk_conv': nrm(ks[16], (DEPTH, QK_CONV_K, 2 * D_ML), QK_CONV_K ** -0.5),
        'b_qk_conv': nrm(ks[17], (DEPTH, 2 * D_ML), 0.02),
        'f_bias': 3.0 + 3.0 * jax.random.uniform(ks[18], (DEPTH, ML_HEADS), f32),
        'ml_norm': 1.0 + nrm(ks[19], (DEPTH, D_ML), 0.05),
        'w_conv_out': nrm(ks[20], (DEPTH, D_CONV, D_MODEL), D_CONV ** -0.5),
        'w_ml_out': nrm(ks[21], (DEPTH, D_ML, D_MODEL), D_ML ** -0.5),
        'w_out': nrm(ks[22], (DEPTH, D_MODEL, D_MODEL), D_MODEL ** -0.5),
    }


def reference(x_prompt, x_sample, state_conv, state_qk_conv, state_C, state_n, state_m, meta_tokens,
              norm_pre, norm_post, w_in, b_in, w_dw, b_dw, ln_g, ln_b, w_qk_conv, b_qk_conv, f_bias,
              ml_norm, w_conv_out, w_ml_out, w_out):
    bp = x_prompt.shape[0]
    meta = jnp.broadcast_to(meta_tokens.astype(x_prompt.dtype)[None], (bp, N_META, D_MODEL))
    xp = jnp.concatenate([meta, x_prompt], axis=1)
    xs = x_sample
    p_conv, p_qk, p_C, p_n, p_m = [], [], [], [], []
    s_conv, s_qk, s_C, s_n, s_m = [], [], [], [], []
    for l in range(DEPTH):
        params = (norm_pre[l], norm_post[l], w_in[l], b_in[l], w_dw[l], b_dw[l], ln_g[l], ln_b[l],
                  w_qk_conv[l], b_qk_conv[l], f_bias[l], ml_norm[l], w_conv_out[l], w_ml_out[l], w_out[l])
        cb0 = jnp.zeros((bp, CONV_K - 1, D_CONV), xp.dtype)
        qb0 = jnp.zeros((bp, QK_CONV_K - 1, 2 * D_ML), xp.dtype)
        ml0 = (jnp.zeros((bp, ML_HEADS, ML_HEAD_DIM, ML_HEAD_DIM), jnp.float32),
               jnp.zeros((bp, ML_HEADS, ML_HEAD_DIM), jnp.float32),
               jnp.zeros((bp, ML_HEADS), jnp.float32))
        xp, cb, qb, (C, n, m) = mixer_layer(xp, cb0, qb0, ml0, N_META, *params)
        p_conv.append(cb); p_qk.append(qb); p_C.append(C); p_n.append(n); p_m.append(m)
        xs, cb, qb, (C, n, m) = mixer_layer(xs, state_conv[l], state_qk_conv[l],
                                            (state_C[l], state_n[l], state_m[l]), 0, *params)
        s_conv.append(cb); s_qk.append(qb); s_C.append(C); s_n.append(n); s_m.append(m)
    return (xp[:, N_META:], xs,
            jnp.stack(p_conv), jnp.stack(p_qk), jnp.stack(p_C), jnp.stack(p_n), jnp.stack(p_m),
            jnp.stack(s_conv), jnp.stack(s_qk), jnp.stack(s_C), jnp.stack(s_n), jnp.stack(s_m))
```

```python
import math
from contextlib import ExitStack
import numpy as np
import concourse.bass as bass
import concourse.mybir as mybir
from concourse.bass_utils import run_bass_kernel_spmd

F32 = mybir.dt.float32
BF16 = mybir.dt.bfloat16
ALU = mybir.AluOpType
AF = mybir.ActivationFunctionType
AX = mybir.AxisListType

D = 1024
DIN = 15368
DML = 2048
NH = 4
DH = 512
EPS = 1e-6
C_A, C_GL, C_ZC, C_Q, C_K, C_V, C_O, C_ZM, C_I, C_F, C_GC, C_GM = (
    0, 1024, 2048, 3072, 5120, 7168, 9216, 11264, 13312, 13316, 13320, 14344)
SELF_SYNC = True


class Stop(Exception):
    pass


class Buf:
    def __init__(s, name):
        s.name = name
        s.w = None
        s.r = {}


class TB:
    def __init__(s, t, name):
        s.t = t
        s.b = Buf(name)

    def __getitem__(s, k):
        return s.t[k]


class View:
    def __init__(s, tb, flat, off):
        s.flat = flat
        s.off = off
        s.b = tb.b

    def __getitem__(s, k):
        r, c = k
        c0 = 0 if c.start is None else c.start
        c1 = 512 if c.stop is None else c.stop
        return s.flat[r, s.off + c0:s.off + c1]


class Sch:
    def __init__(s, nc, es):
        s.nc = nc
        s.names = ['pe', 'act', 'dve', 'pool', 'sp']
        s.prog = {n: [] for n in s.names}
        s.sem = {n: es.enter_context(nc.semaphore('s_' + n)) for n in ['pe', 'act', 'dve', 'pool']}
        s.cnt = {n: 0 for n in s.sem}
        s.NQ = 8
        s.dsem = {q: [es.enter_context(nc.semaphore('d_%s%d' % (q, i))) for i in range(s.NQ)] for q in ['sp', 'pool']}
        s.dcnt = {q: 0 for q in s.dsem}
        s.dlast = {}
        s.seen = {n: {} for n in s.names}

    def _wait(s, n, ev):
        k, h, v = ev
        if k == n and (n == 'pe' or not SELF_SYNC):
            return
        if s.seen[n].get(k, 0) >= v:
            return
        s.seen[n][k] = v
        s.prog[n].append(lambda e, h=h, v=v: e.wait_ge(h, v))

    def _deps(s, n, R, W):
        evs = []
        for b in R:
            if b.w is not None:
                evs.append(b.w)
        for b in W:
            if b.w is not None:
                evs.append(b.w)
            evs.extend(b.r.values())
        for ev in evs:
            s._wait(n, ev)

    def _commit(s, ev, R, W):
        for b in R:
            o = b.r.get(ev[0])
            if o is None or o[2] < ev[2]:
                b.r[ev[0]] = ev
        for b in W:
            b.w = ev
            b.r = {}

    def op(s, n, fn, R=(), W=()):
        s._deps(n, R, W)
        s.cnt[n] += 1
        h = s.sem[n]
        s.prog[n].append(lambda e, fn=fn, h=h: fn(e).then_inc(h, 1))
        s._commit((n, h, s.cnt[n]), R, W)

    def dma(s, q, out, in_, R=(), W=(), **kw):
        i = s.dcnt[q]
        s.dcnt[q] += 1
        slot = i % s.NQ
        v = 16 * (i // s.NQ + 1)
        h = s.dsem[q][slot]
        key = (q, slot)
        if v > 16:
            s._wait(q, (key, h, v - 16))
        s._deps(q, R, W)
        s.prog[q].append(lambda e, out=out, in_=in_, h=h, kw=kw: e.dma_start(out=out, in_=in_, **kw).then_inc(h, 16))
        ev = (key, h, v)
        s.dlast[key] = ev
        s._commit(ev, R, W)

    def barrier(s, engs=('pe', 'act', 'dve')):
        for n in engs:
            for m in engs:
                if m != n and s.cnt[m] > 0:
                    s._wait(n, (m, s.sem[m], s.cnt[m]))

    def finish(s):
        for ev in s.dlast.values():
            s._wait('sp', ev)
        for n in s.sem:
            if s.cnt[n] > 0:
                s._wait('sp', (n, s.sem[n], s.cnt[n]))

    def emit(s):
        nc = s.nc
        with nc.Block() as block:
            @block.tensor
            def _(e):
                for f in s.prog['pe']:
                    f(e)

            @block.scalar
            def _(e):
                for f in s.prog['act']:
                    f(e)

            @block.vector
            def _(e):
                for f in s.prog['dve']:
                    f(e)

            @block.gpsimd
            def _(e):
                for f in s.prog['pool']:
                    f(e)

            @block.sync
            def _(e):
                for f in s.prog['sp']:
                    f(e)


def build(NPT):
    nc = bass.Bass("TRN2", target_bir_lowering=False)
    NP = NPT * 512

    def din(name, shape):
        return nc.dram_tensor(name, list(shape), F32, kind="ExternalInput").ap()

    def dout(name, shape):
        return nc.dram_tensor(name, list(shape), F32, kind="ExternalOutput").ap()

    def dint(name, shape):
        return nc.dram_tensor(name, list(shape), F32).ap()

    xs = din("xs", [256, D]); xp = din("xp", [NP, D]); meta = din("meta", [16, D])
    st_conv = din("st_conv", [2, 120, D]); st_qk = din("st_qk", [2, 12, 4096])
    st_C = din("st_C", [2, 4, 4, DH, DH]); st_n = din("st_n", [2, 4, 16, 128]); st_m = din("st_m", [2, 4, 4])
    w_in = din("w_in", [2, D, DIN]); b_in = din("b_in", [2, DIN])
    w_dw = din("w_dw", [2, 31, D]); b_dw = din("b_dw", [2, 8, 128]); ln_g = din("ln_g", [2, 8, 128]); ln_b = din("ln_b", [2, 8, 128])
    w_qkc = din("w_qkc", [2, 4, 32, 128]); b_qkc = din("b_qkc", [2, 32, 128]); f_bias = din("f_bias", [2, 4])
    ml_norm = din("ml_norm", [2, 16, 128]); w_co = din("w_co", [2, D, D]); w_ml = din("w_ml", [2, DML, D]); w_out = din("w_out", [2, D, D])
    norm_pre = din("norm_pre", [2, D]); norm_post = din("norm_post", [2, D])
    cst = din("cst", [2, 128, 128])
    ys = dout("ys", [256, D]); yp = dout("yp", [NP, D])
    o_sconv = dout("o_sconv", [2, 120, D]); o_sqk = dout("o_sqk", [2, 12, 4096])
    o_sC = dout("o_sC", [2, 4, 4, DH, DH]); o_sn = dout("o_sn", [2, 4, 16, 128]); o_sm = dout("o_sm", [2, 4, 4])
    o_pconv = dout("o_pconv", [2, 30, D]); o_pqk = dout("o_pqk", [2, 3, 4096])
    o_pC = dout("o_pC", [2, 4, DH, DH]); o_pn = dout("o_pn", [2, 16, 128]); o_pm = dout("o_pm", [2, 4])
    x1s = dint("x1s", [256, D]); x1p = dint("x1p", [NP, D]); x1l = dint("x1l", [16, D]); ydump = dint("ydump", [16, D])

    es = ExitStack()
    with es:
        S = Sch(nc, es)

        def sb(name, shape, dt=F32):
            return TB(es.enter_context(nc.sbuf_tensor(name, list(shape), dt)), name)

        def pst(name, shape, dt=F32):
            return TB(es.enter_context(nc.psum_tensor(name, list(shape), dt)), name)

        def MM(out, lhsT, rhs, st, sp, R, W):
            S.op('pe', lambda e: e.matmul(out, lhsT, rhs, start=st, stop=sp), R, W)

        def TR(out, in_, ident, R, W):
            S.op('pe', lambda e: e.transpose(out, in_, ident), R, W)

        def ACT(out, in_, func, R, W, bias=0.0, scale=1.0, accum=None):
            if accum is None:
                S.op('act', lambda e: e.activation(out, in_, func, bias=bias, scale=scale), R, W)
            else:
                S.op('act', lambda e: e.activation(out, in_, func, bias=bias, scale=scale, accum_out=accum), R, W)

        def TT(eng, out, in0, in1, op, R, W):
            S.op(eng, lambda e: e.tensor_tensor(out, in0, in1, op), R, W)

        def TS(eng, out, in0, s1, s2, op0, op1, R, W):
            if s2 is None:
                S.op(eng, lambda e: e.tensor_scalar(out, in0, s1, None, op0), R, W)
            else:
                S.op(eng, lambda e: e.tensor_scalar(out, in0, s1, s2, op0, op1), R, W)

        def STT(eng, out, in0, sc, in1, op0, op1, R, W):
            S.op(eng, lambda e: e.scalar_tensor_tensor(out, in0, sc, in1, op0, op1), R, W)

        def CP(eng, out, in_, R, W):
            if eng == 'act':
                S.op('act', lambda e: e.copy(out, in_), R, W)
            else:
                S.op(eng, lambda e: e.tensor_copy(out, in_), R, W)

        def RSQ(out, in_, b):
            S.op('act', lambda e: e.activation(out, in_, AF.Sqrt), (b,), (b,))
            S.op('dve', lambda e: e.reciprocal(out, out), (b,), (b,))

        def MS(eng, ap, val, W):
            S.op(eng, lambda e: e.memset(ap, val), (), W)

        ident_f = sb("ident_f", [128, 128]); tri_f = sb("tri_f", [128, 128])
        ident_b = sb("ident_b", [128, 128], BF16); tri_b = sb("tri_b", [128, 128], BF16)
        ones_b = sb("ones_b", [128, 128], BF16); ones_f = sb("ones_f", [128, 128])
        gpre = sb("gpre", [128, D]); gpost = sb("gpost", [128, D]); bvbc = sb("bvbc", [128, 512])
        bcol = sb("bcol", [128, 120]); wdw = sb("wdw", [128, 8, 32]); misc = sb("misc", [128, 24])
        wqk = sb("wqk", [128, 128]); bqk = sb("bqk", [128, 32]); mln = sb("mln", [128, 16])
        bi_c = sb("bi_c", [4, 1]); bf_c = sb("bf_c", [4, 1]); fb_c = sb("fb_c", [4, 1]); nbfz = sb("nbfz", [4, 1])
        wif = sb("wif", [128, 8, 8], BF16)
        stg = sb("stg", [128, D])
        xres = [sb("xres0", [128, D])] * 4
        wstg = sb("wstg", [128, 8, 512])
        hn = sb("hn", [128, D], BF16)
        hT = [sb("hT0", [128, 8, 512], BF16)]
        ss = sb("ss", [128, 8]); junk = hn
        NWB = 3
        wb = [sb("wb%d" % i, [128, 8, 512], BF16) for i in range(NWB)]
        ubufs = [sb("ubuf%d" % i, [128, 544]) for i in range(2)]; sg = sb("sg", [128, 512])
        ucv = sb("ucv", [128, 8, 512]); ubf = sb("ubf", [128, 512], BF16); usq = sb("usq", [128, 512], BF16)
        tmpa = sb("tmpa", [128, 512])
        ucb = sb("ucb", [128, 8, 512], BF16)
        Cst = sb("Cst", [128, 4, 512])
        _cf = Cst[:, :, :].rearrange("p a b -> p (a b)")
        mean = View(Cst, _cf, 0); rstd = View(Cst, _cf, 512); nmr = View(Cst, _cf, 1024); tmpb = View(Cst, _cf, 1536)
        yacc = ucv; ybf = ucb
        qkbufs = [sb("qkbuf%d" % i, [128, 520]) for i in range(2)]; qkacc = sb("qkacc", [128, 512])
        qT = sb("qT", [128, 4, 512], BF16); kT = sb("kT", [128, 4, 512], BF16); qs = sb("qs", [128, 4, 128], BF16)
        gate = sb("gate", [128, 4, 512], BF16); so = tmpa
        hmT = sb("hmT", [128, 16, 512], BF16)
        vbf = sb("vbf", [128, 512], BF16); ktb = sb("ktb", [128, 512], BF16); stb = sb("stb", [128, 128], BF16)
        hnb = sb("hnb", [128, 512], BF16)
        irow = sb("irow", [4, 512]); lfrow = sb("lfrow", [4, 512]); cum = sb("cum", [4, 128]); grow = sb("grow", [4, 128])
        arow = sb("arow", [4, 128]); flrow = sb("flrow", [4, 128]); onesrow = sb("onesrow", [4, 128])
        sc4 = sb("sc4", [4, 8]); dg4 = sb("dg4", [4, 4])
        gb = sb("gb", [128, 4, 12])
        sc1 = sb("sc1", [128, 8])
        uhP = sb("uhP", [128, 8, 30]); qkhP = sb("qkhP", [128, 4, 8, 3])
        CPs = sb("CPs", [128, 16, 512]); Cwb = sb("Cwb", [128, 4, 512], BF16)
        nP = sb("nP", [128, 16]); nPb = sb("nPb", [128, 16], BF16); mP = sb("mP", [4, 1])
        nS = sb("nS", [128, 4, 16]); nSb = sb("nSb", [128, 4, 16], BF16); mS = sb("mS", [4, 4]); mSo = sb("mSo", [4, 4])
        yst = [sb("yst0", [128, D])]
        sso = sb("sso", [128, 4])

        pg = [pst("pg%d" % i, [128, 512]) for i in range(4)]
        pT = pst("pT", [128, 1024], BF16)
        pN = pst("pN", [128, 512])
        pC = pst("pC", [128, 512])
        pS = pst("pS", [128, 512])
        pgi = [0]

        def PG():
            pgi[0] += 1
            return pg[pgi[0] % 4]

        wbi = [0]

        def WL(dram2d, r0, c0, ncols):
            t = wb[wbi[0] % NWB]
            wbi[0] += 1
            src = dram2d[r0:r0 + 1024, c0:c0 + ncols].rearrange("(kc p) n -> p kc n", p=128)
            S.dma('sp', wstg[:, :, 0:ncols], src, (), (wstg.b,))
            CP('pool', t[:, :, 0:ncols], wstg[:, :, 0:ncols], (wstg.b,), (t.b,))
            return t

        S.dma('sp', ident_f[:, :], cst[0], (), (ident_f.b,))
        S.dma('sp', tri_f[:, :], cst[1], (), (tri_f.b,))
        CP('dve', ident_b[:, :], ident_f[:, :], (ident_f.b,), (ident_b.b,))
        CP('dve', tri_b[:, :], tri_f[:, :], (tri_f.b,), (tri_b.b,))
        MS('dve', ones_b[:, :], 1.0, (ones_b.b,))
        MS('dve', ones_f[:, :], 1.0, (ones_f.b,))
        MS('dve', onesrow[:, :], 1.0, (onesrow.b,))

        def load_cols(dst_ap, dstb, src_rows_ap, nrows):
            S.dma('sp', stg[0:nrows, 0:128], src_rows_ap, (), (stg.b,))
            p = PG()
            TR(p[:, 0:nrows], stg[0:nrows, 0:128], ident_f[0:nrows, 0:nrows], (stg.b, ident_f.b), (p.b,))
            CP('dve', dst_ap, p[:, 0:nrows], (p.b,), (dstb,))

        def layer_params(l):
            S.dma('sp', gpre[:, :], norm_pre[l].partition_broadcast(128), (), (gpre.b,))
            S.dma('sp', gpost[:, :], norm_post[l].partition_broadcast(128), (), (gpost.b,))
            load_cols(bcol[:, 0:104], bcol.b, b_in[l, 0:13312].rearrange("(r c) -> r c", c=128), 104)
            load_cols(bcol[:, 104:120], bcol.b, b_in[l, C_GC:C_GC + 2048].rearrange("(r c) -> r c", c=128), 16)
            for c in range(8):
                load_cols(wdw[:, c, 0:31], wdw.b, w_dw[l, :, c * 128:(c + 1) * 128], 31)
            load_cols(misc[:, 0:8], misc.b, b_dw[l], 8)
            load_cols(misc[:, 8:16], misc.b, ln_g[l], 8)
            load_cols(misc[:, 16:24], misc.b, ln_b[l], 8)
            load_cols(wqk[:, :], wqk.b, w_qkc[l].rearrange("t c p -> (t c) p"), 128)
            load_cols(bqk[:, :], bqk.b, b_qkc[l], 32)
            load_cols(mln[:, :], mln.b, ml_norm[l], 16)
            S.dma('sp', bi_c[:, :], b_in[l, C_I:C_I + 4].rearrange("(h o) -> h o", o=1), (), (bi_c.b,))
            S.dma('sp', bf_c[:, :], b_in[l, C_F:C_F + 4].rearrange("(h o) -> h o", o=1), (), (bf_c.b,))
            S.dma('sp', fb_c[:, :], f_bias[l].rearrange("(h o) -> h o", o=1), (), (fb_c.b,))
            STT('dve', nbfz[:, :], bf_c[:, :], -1.0, fb_c[:, :], ALU.mult, ALU.subtract, (bf_c.b, fb_c.b), (nbfz.b,))
            S.dma('sp', stg[:, 0:64].rearrange("p (k n) -> p k n", n=8), w_in[l, :, C_I:C_I + 8].rearrange("(kc p) n -> p kc n", p=128), (), (stg.b,))
            CP('dve', wif[:, :, :], stg[:, 0:64].rearrange("p (k n) -> p k n", n=8), (stg.b,), (wif.b,))

        LNSC = math.log(DH ** -0.5)
        import os
        kstop = int(os.environ.get('K_STOP', '-1'))
        ckc = [0]

        def ck():
            if ckc[0] == kstop:
                raise Stop()
            ckc[0] += 1

        def tile(l, kind, tix, xsrc, ydst, xb, yb):
            if kind == 'S':
                nseg, SL, L = 4, 64, 64
            elif kind == 'LEAD':
                nseg, SL, L = 1, 16, 16
            else:
                nseg, SL, L = 1, 512, 128
            ntok = nseg * SL
            nch = ntok // L
            nsub = (ntok + 127) // 128
            h_T = hT[0]
            Win = w_in[l]

            def segv(ap2d, w):
                return ap2d.rearrange("p (s w) -> p s w", w=w)

            for st in range(nsub):
                r = min(128, ntok - st * 128)
                xt = xres[st]
                S.dma('sp', xt[0:r, :], xsrc[st * 128:st * 128 + r, :], xb, (xt.b,))
                ACT(junk[0:r, :], xt[0:r, :], AF.Square, (xt.b,), (junk.b, ss.b), accum=ss[0:r, 0:1])
                TS('dve', ss[0:r, 1:2], ss[0:r, 0:1], 1.0 / D, EPS, ALU.mult, ALU.add, (ss.b,), (ss.b,))
                RSQ(ss[0:r, 2:3], ss[0:r, 1:2], ss.b)
                STT('dve', hn[0:r, :], xt[0:r, :], ss[0:r, 2:3], gpre[0:r, :], ALU.mult, ALU.mult, (xt.b, ss.b, gpre.b), (hn.b,))
                for kc in range(8):
                    TR(pT[:, kc * 128:kc * 128 + r], hn[0:r, kc * 128:(kc + 1) * 128], ident_b[0:r, 0:r], (hn.b, ident_b.b), (pT.b,))
                CP('act', h_T[:, :, st * 128:st * 128 + r], pT[:, :].rearrange("p (k t) -> p k t", t=128)[:, :, 0:r], (pT.b,), (h_T.b,))

            ck()

            def fm_proj(wt, coff, R_extra=()):
                p = PG()
                for kc in range(8):
                    MM(p[:, 0:ntok], wt[:, kc, coff:coff + 128], h_T[:, kc, 0:ntok], kc == 0, kc == 7, (wt.b, h_T.b), (p.b,))
                return p

            W30 = 30 + SL
            if kind == 'S':
                S.dma('sp', stg[0:120, :], st_conv[l], (), (stg.b,))
            CstF = Cst[:, :, :].rearrange("p a b -> p (a b)")
            for cg in range(2):
                wa = WL(Win, 0, C_A + cg * 512, 512)
                wg = WL(Win, 0, C_GL + cg * 512, 512)
                for ci in range(4):
                    c = cg * 4 + ci
                    ub = ubufs[c % 2]
                    uvc = segv(ub[:, 0:nseg * W30], W30)
                    if kind == 'S':
                        p = PG()
                        TR(p[:, 0:120], stg[0:120, c * 128:(c + 1) * 128], ident_f[0:120, 0:120], (stg.b, ident_f.b), (p.b,))
                        CP('dve', uvc[:, :, 0:30], segv(p[:, 0:120], 30), (p.b,), (ub.b,))
                    elif kind == 'LEAD':
                        MS('dve', ub[:, 0:30], 0.0, (ub.b,))
                    else:
                        CP('dve', ub[:, 0:30], uhP[:, c, :], (uhP.b,), (ub.b,))
                    pa = fm_proj(wa, ci * 128)
                    pgl = fm_proj(wg, ci * 128)
                    ACT(sg[:, 0:ntok], pgl[:, 0:ntok], AF.Sigmoid, (pgl.b, bcol.b), (sg.b,), bias=bcol[:, 8 + c:9 + c])
                    STT('dve', uvc[:, :, 30:W30], segv(pa[:, 0:ntok], SL), bcol[:, c:c + 1], segv(sg[:, 0:ntok], SL),
                        ALU.add, ALU.mult, (pa.b, sg.b, bcol.b), (ub.b,))
                    eng = 'dve'
                    acc = segv(ucv[:, c, 0:ntok], SL)
                    TS(eng, acc, uvc[:, :, 0:SL], wdw[:, c, 0:1], misc[:, c:c + 1], ALU.mult, ALU.add, (ub.b, wdw.b, misc.b), (ucv.b,))
                    for k in range(1, 31):
                        STT(eng, acc, uvc[:, :, k:k + SL], wdw[:, c, k:k + 1], acc, ALU.mult, ALU.add, (ub.b, wdw.b), (ucv.b,))
                    if kind == 'S':
                        p = PG()
                        CP('dve', segv(tmpa[:, 0:120], 30), uvc[:, :, SL:W30], (ub.b,), (tmpa.b,))
                        TR(p[0:120, 0:128], tmpa[:, 0:120], ident_f[:, :], (tmpa.b, ident_f.b), (p.b,))
                        CP('act', CstF[0:120, c * 128:(c + 1) * 128], p[0:120, 0:128], (p.b,), (Cst.b,))
                    else:
                        CP('dve', uhP[:, c, :], ub[:, SL:W30], (ub.b,), (uhP.b,))
            if kind == 'S':
                S.dma('sp', o_sconv[l], CstF[0:120, 0:1024], (Cst.b,), ())

            ck()
            p_sum = PG(); p_sq = PG()
            for c in range(8):
                CP('act', ubf[:, 0:ntok], ucv[:, c, 0:ntok], (ucv.b,), (ubf.b,))
                ACT(usq[:, 0:ntok], ucv[:, c, 0:ntok], AF.Square, (ucv.b,), (usq.b,))
                MM(p_sum[:, 0:ntok], ones_b[:, :], ubf[:, 0:ntok], c == 0, c == 7, (ones_b.b, ubf.b), (p_sum.b,))
                MM(p_sq[:, 0:ntok], ones_b[:, :], usq[:, 0:ntok], c == 0, c == 7, (ones_b.b, usq.b), (p_sq.b,))
            TS('dve', mean[:, 0:ntok], p_sum[:, 0:ntok], 1.0 / D, None, ALU.mult, None, (p_sum.b,), (mean.b,))
            TT('dve', tmpa[:, 0:ntok], mean[:, 0:ntok], mean[:, 0:ntok], ALU.mult, (mean.b,), (tmpa.b,))
            STT('dve', rstd[:, 0:ntok], p_sq[:, 0:ntok], 1.0 / D, tmpa[:, 0:ntok], ALU.mult, ALU.subtract, (p_sq.b, tmpa.b), (rstd.b,))
            TS('dve', rstd[:, 0:ntok], rstd[:, 0:ntok], EPS, None, ALU.add, None, (rstd.b,), (rstd.b,))
            RSQ(rstd[:, 0:ntok], rstd[:, 0:ntok], rstd.b)
            STT('dve', nmr[:, 0:ntok], mean[:, 0:ntok], -1.0, rstd[:, 0:ntok], ALU.mult, ALU.mult, (mean.b, rstd.b), (nmr.b,))
            for cg in range(2):
                wz = WL(Win, 0, C_ZC + cg * 512, 512)
                for ci in range(4):
                    c = cg * 4 + ci
                    pz = fm_proj(wz, ci * 128)
                    TT('dve', tmpa[:, 0:ntok], ucv[:, c, 0:ntok], rstd[:, 0:ntok], ALU.mult, (ucv.b, rstd.b), (tmpa.b,))
                    TT('dve', tmpa[:, 0:ntok], tmpa[:, 0:ntok], nmr[:, 0:ntok], ALU.add, (tmpa.b, nmr.b), (tmpa.b,))
                    ACT(tmpb[:, 0:ntok], tmpa[:, 0:ntok], AF.Silu, (tmpa.b, misc.b), (tmpb.b,), bias=misc[:, 16 + c:17 + c], scale=misc[:, 8 + c:9 + c])
                    ACT(sg[:, 0:ntok], pz[:, 0:ntok], AF.Silu, (pz.b, bcol.b), (sg.b,), bias=bcol[:, 16 + c:17 + c])
                    TT('dve', ucb[:, c, 0:ntok], tmpb[:, 0:ntok], sg[:, 0:ntok], ALU.mult, (tmpb.b, sg.b), (ucb.b,))
            for dg in range(2):
                wc = WL(w_co[l], 0, dg * 512, 512)
                wgc = WL(Win, 0, C_GC + dg * 512, 512)
                for di in range(4):
                    dt = dg * 4 + di
                    pb = PG()
                    for c in range(8):
                        MM(pb[:, 0:ntok], wc[:, c, di * 128:(di + 1) * 128], ucb[:, c, 0:ntok], c == 0, c == 7, (wc.b, ucb.b), (pb.b,))
                    pgc = fm_proj(wgc, di * 128)
                    ACT(sg[:, 0:ntok], pgc[:, 0:ntok], AF.Sigmoid, (pgc.b, bcol.b), (sg.b,), bias=bcol[:, 104 + dt:105 + dt])
                    TT('dve', yacc[:, dt, 0:ntok], pb[:, 0:ntok], sg[:, 0:ntok], ALU.mult, (pb.b, sg.b), (yacc.b,))

            ck()
            p_i = PG(); p_f = PG()
            for kc in range(8):
                MM(p_i[0:4, 0:ntok], wif[:, kc, 0:4], h_T[:, kc, 0:ntok], kc == 0, kc == 7, (wif.b, h_T.b), (p_i.b,))
            for kc in range(8):
                MM(p_f[0:4, 0:ntok], wif[:, kc, 4:8], h_T[:, kc, 0:ntok], kc == 0, kc == 7, (wif.b, h_T.b), (p_f.b,))
            ACT(irow[:, 0:ntok], p_i[0:4, 0:ntok], AF.Identity, (p_i.b, bi_c.b), (irow.b,), bias=bi_c[:, 0:1])
            ACT(lfrow[:, 0:ntok], p_f[0:4, 0:ntok], AF.Exp, (p_f.b, nbfz.b), (lfrow.b,), bias=nbfz[:, 0:1], scale=-1.0)
            ACT(lfrow[:, 0:ntok], lfrow[:, 0:ntok], AF.Ln, (lfrow.b,), (lfrow.b,), bias=1.0)
            if kind == 'S':
                S.dma('sp', mS[:, :], st_m[l].rearrange("s h -> h s"), (), (mS.b,), allow_slow_non_contiguous=True)
            for j in range(nch):
                t0 = j * L
                mcur = mS[:, j:j + 1] if kind == 'S' else mP[:, 0:1]
                mb = mS.b if kind == 'S' else mP.b
                mnew = mSo[:, j:j + 1] if kind == 'S' else mP[:, 0:1]
                mnb = mSo.b if kind == 'S' else mP.b
                S.op('dve', lambda e, t0=t0: e.tensor_tensor_scan(cum[:, 0:L], onesrow[:, 0:L], lfrow[:, t0:t0 + L], 0.0, ALU.mult, ALU.add),
                     (onesrow.b, lfrow.b), (cum.b,))
                TT('dve', grow[:, 0:L], irow[:, t0:t0 + L], cum[:, 0:L], ALU.add, (irow.b, cum.b), (grow.b,))
                S.op('dve', lambda e: e.reduce_max(sc4[:, 0:1], grow[:, 0:L], AX.X), (grow.b,), (sc4.b,))
                TT('dve', sc4[:, 1:2], sc4[:, 0:1], mcur, ALU.max, (sc4.b, mb), (sc4.b,))
                TS('dve', sc4[:, 2:3], sc4[:, 1:2], -1.0, LNSC, ALU.mult, ALU.add, (sc4.b,), (sc4.b,))
                TS('dve', sc4[:, 3:4], sc4[:, 1:2], -1.0, None, ALU.mult, None, (sc4.b,), (sc4.b,))
                ACT(arow[:, 0:L], grow[:, 0:L], AF.Exp, (grow.b, sc4.b), (arow.b,), bias=sc4[:, 2:3])
                ACT(flrow[:, 0:L], cum[:, 0:L], AF.Exp, (cum.b, sc4.b), (flrow.b,), bias=sc4[:, 3:4])
                ACT(sc4[:, 4:5], mcur, AF.Exp, (mb, sc4.b), (sc4.b,), bias=sc4[:, 3:4])
                TT('dve', mnew, sc4[:, 1:2], cum[:, L - 1:L], ALU.subtract, (sc4.b, cum.b), (mnb,))
                TS('dve', dg4[:, :], ident_f[0:4, 0:4], sc4[:, 4:5], None, ALU.mult, None, (ident_f.b, sc4.b), (dg4.b,))
                TR(pS[0:L, 256:260], arow[:, 0:L], ident_f[0:4, 0:4], (arow.b, ident_f.b), (pS.b,))
                TR(pS[0:L, 260:264], flrow[:, 0:L], ident_f[0:4, 0:4], (flrow.b, ident_f.b), (pS.b,))
                MM(pS[:, 264:268], ones_f[0:4, :], dg4[:, :], True, True, (ones_f.b, dg4.b), (pS.b,))
                CP('dve', gb[0:L, j, 0:8], pS[0:L, 256:264], (pS.b,), (gb.b,))
                CP('dve', gb[:, j, 8:12], pS[:, 264:268], (pS.b,), (gb.b,))
            if kind == 'S':
                S.dma('sp', o_sm[l].rearrange("s h -> h s"), mSo[:, :], (mSo.b,), (), allow_slow_non_contiguous=True)
                for sq in range(4):
                    load_cols(nS[:, sq, :], nS.b, st_n[l, sq], 16)

            ck()
            W3 = 3 + SL
            for h in range(NH):
                wq = WL(Win, 0, C_Q + h * 512, 512)
                wk = WL(Win, 0, C_K + h * 512, 512)
                S.dma('sp', bvbc[:, :], b_in[l, C_V + h * 512:C_V + (h + 1) * 512].partition_broadcast(128), (), (bvbc.b,))
                for i in range(8):
                    wt = wq if i < 4 else wk
                    cidx = (24 if i < 4 else 40) + h * 4 + (i % 4)
                    qc = (0 if i < 4 else 16) + h * 4 + (i % 4)
                    qk_off = (0 if i < 4 else 2048) + h * 512 + (i % 4) * 128
                    qb = qkbufs[i % 2]
                    qvi = segv(qb[:, 0:nseg * W3], W3)
                    if kind == 'S':
                        S.dma('sp', stg[0:12, 0:128], st_qk[l][:, qk_off:qk_off + 128], (), (stg.b,))
                        p = PG()
                        TR(p[:, 0:12], stg[0:12, 0:128], ident_f[0:12, 0:12], (stg.b, ident_f.b), (p.b,))
                        CP('dve', qvi[:, :, 0:3], segv(p[:, 0:12], 3), (p.b,), (qb.b,))
                    elif kind == 'LEAD':
                        MS('dve', qb[:, 0:3], 0.0, (qb.b,))
                    else:
                        CP('dve', qb[:, 0:3], qkhP[:, h, i, :], (qkhP.b,), (qb.b,))
                    p = fm_proj(wt, (i % 4) * 128)
                    ACT(qvi[:, :, 3:W3], segv(p[:, 0:ntok], SL), AF.Identity, (p.b, bcol.b), (qb.b,), bias=bcol[:, cidx:cidx + 1])
                    eng = 'dve'
                    acc = segv(qkacc[:, 0:ntok], SL)
                    TS(eng, acc, qvi[:, :, 0:SL], wqk[:, qc:qc + 1], bqk[:, qc:qc + 1], ALU.mult, ALU.add, (qb.b, wqk.b, bqk.b), (qkacc.b,))
                    for k in range(1, 4):
                        STT(eng, acc, qvi[:, :, k:k + SL], wqk[:, k * 32 + qc:k * 32 + qc + 1], acc, ALU.mult, ALU.add, (qb.b, wqk.b), (qkacc.b,))
                    dst = qT if i < 4 else kT
                    ACT(dst[:, i % 4, 0:ntok], qkacc[:, 0:ntok], AF.Silu, (qkacc.b,), (dst.b,))
                    if kind == 'S':
                        CP('dve', segv(tmpa[:, 0:12], 3), qvi[:, :, SL:W3], (qb.b,), (tmpa.b,))
                        p = PG()
                        TR(p[0:12, 0:128], tmpa[:, 0:12], ident_f[:, :], (tmpa.b, ident_f.b), (p.b,))
                        CP('act', stg[0:12, 128:256], p[0:12, 0:128], (p.b,), (stg.b,))
                        S.dma('sp', o_sqk[l][:, qk_off:qk_off + 128], stg[0:12, 128:256], (stg.b,), ())
                    else:
                        CP('dve', qkhP[:, h, i, :], qb[:, SL:W3], (qb.b,), (qkhP.b,))
                    if h == 0 and i in (0, 7):
                        ck()
                wo = WL(Win, 0, C_O + h * 512, 512)
                wz = WL(Win, 0, C_ZM + h * 512, 512)
                for et in range(4):
                    po = fm_proj(wo, et * 128)
                    pz = fm_proj(wz, et * 128)
                    ACT(so[:, 0:ntok], po[:, 0:ntok], AF.Sigmoid, (po.b, bcol.b), (so.b,), bias=bcol[:, 72 + h * 4 + et:73 + h * 4 + et])
                    ACT(sg[:, 0:ntok], pz[:, 0:ntok], AF.Silu, (pz.b, bcol.b), (sg.b,), bias=bcol[:, 88 + h * 4 + et:89 + h * 4 + et])
                    STT('dve', gate[:, et, 0:ntok], so[:, 0:ntok], mln[:, h * 4 + et:h * 4 + et + 1], sg[:, 0:ntok], ALU.mult, ALU.mult,
                        (so.b, sg.b, mln.b), (gate.b,))
                ck()
                wv = WL(Win, 0, C_V + h * 512, 512)
                ck()
                if kind != 'S':
                    CP('act', Cwb[:, :, :], CPs[:, h * 4:h * 4 + 4, :], (CPs.b,), (Cwb.b,))
                for j in range(int(os.environ.get('K_J0', '0')) if h == 0 else 0, nch):
                    t0 = j * L
                    if os.environ.get('K_BAR'):
                        S.barrier()
                    hb = 0 if kind == 'S' else h * 4
                    Cf = lambda dt, hb=hb: CPs[:, hb + dt, :]
                    Cb_ = lambda dt: Cwb[:, dt, :]
                    Cfb, Cbb = CPs.b, Cwb.b
                    if kind == 'S':
                        if not (os.environ.get('K_NOLOAD') and j > 0):
                            S.dma('sp', Cst[:, :, :], st_C[l, j, h].rearrange("(et p) d -> p et d", p=128), (), (Cst.b,))
                        for dt in range(4):
                            p = PG()
                            for et in range(4):
                                if not (os.environ.get('K_E2') and j > 0):
                                    TR(p[:, et * 128:(et + 1) * 128], Cst[:, et, dt * 128:(dt + 1) * 128], ident_f[:, :], (Cst.b, ident_f.b), (p.b,))
                            e3 = os.environ.get('K_E3', '') if j > 0 else ''
                            if e3 not in ('1', 'dve'):
                                CP('dve', Cf(dt), p[:, :], (p.b,), (Cfb,))
                            if e3 not in ('1', 'act'):
                                CP('act', Cb_(dt), Cf(dt), (Cfb,), (Cbb,))
                        CP('dve', nSb[:, j, h * 4:h * 4 + 4], nS[:, j, h * 4:h * 4 + 4], (nS.b,), (nSb.b,))
                        nf = lambda dt, j=j, h=h: nS[:, j, h * 4 + dt:h * 4 + dt + 1]
                        nb_ = lambda dt, j=j, h=h: nSb[:, j, h * 4 + dt:h * 4 + dt + 1]
                        nfb, nbb = nS.b, nSb.b
                    else:
                        nf = lambda dt, h=h: nP[:, h * 4 + dt:h * 4 + dt + 1]
                        nb_ = lambda dt, h=h: nPb[:, h * 4 + dt:h * 4 + dt + 1]
                        nfb, nbb = nP.b, nPb.b
                    if h == 0:
                        ck()
                    acol = gb[0:L, j, h:h + 1]
                    flcol = gb[0:L, j, 4 + h:5 + h]
                    emc = gb[:, j, 8 + h:9 + h]
                    pv = PG()
                    for kc in range(8):
                        MM(pv[0:L, :], h_T[:, kc, t0:t0 + L], wv[:, kc, :], kc == 0, kc == 7, (h_T.b, wv.b), (pv.b,))
                    TT('dve', vbf[0:L, :], pv[0:L, :], bvbc[0:L, :], ALU.add, (pv.b, bvbc.b), (vbf.b,))
                    for dt in range(4):
                        MM(pS[0:L, 0:L], kT[:, dt, t0:t0 + L], qT[:, dt, t0:t0 + L], dt == 0, dt == 3, (kT.b, qT.b), (pS.b,))
                    STT('dve', stb[0:L, 0:L], pS[0:L, 0:L], acol, tri_f[0:L, 0:L], ALU.mult, ALU.mult, (pS.b, gb.b, tri_f.b), (stb.b,))
                    if h == 0:
                        ck()
                    for dt in range(4):
                        TR(pT[0:L, dt * 128:(dt + 1) * 128], kT[:, dt, t0:t0 + L], ident_b[:, :], (kT.b, ident_b.b), (pT.b,))
                    TS('dve', ktb[0:L, :], pT[0:L, 0:512], acol, None, ALU.mult, None, (pT.b, gb.b), (ktb.b,))
                    TS('dve', qs[:, :, 0:L], qT[:, :, t0:t0 + L], emc, None, ALU.mult, None, (qT.b, gb.b), (qs.b,))
                    if h == 0:
                        ck()
                    MM(pN[0:L, :], stb[0:L, 0:L], vbf[0:L, :], True, False, (stb.b, vbf.b), (pN.b,))
                    for dt in range(4):
                        MM(pN[0:L, :], qs[:, dt, 0:L], Cb_(dt), False, dt == 3, (qs.b, Cbb), (pN.b,))
                    MM(pS[0:L, 128:129], stb[0:L, 0:L], ones_b[0:L, 0:1], True, False, (stb.b, ones_b.b), (pS.b,))
                    for dt in range(4):
                        MM(pS[0:L, 128:129], qs[:, dt, 0:L], nb_(dt), False, dt == 3, (qs.b, nbb), (pS.b,))
                    if h == 0:
                        ck()
                    CP('dve', sc1[0:L, 6:7], pS[0:L, 128:129], (pS.b,), (sc1.b,))
                    STT('dve', sc1[0:L, 7:8], sc1[0:L, 6:7], -1.0, sc1[0:L, 6:7], ALU.mult, ALU.max, (sc1.b,), (sc1.b,))
                    TT('dve', sc1[0:L, 0:1], sc1[0:L, 7:8], flcol, ALU.max, (sc1.b, gb.b), (sc1.b,))
                    S.op('dve', lambda e: e.reciprocal(sc1[0:L, 1:2], sc1[0:L, 0:1]), (sc1.b,), (sc1.b,))
                    ACT(junk[0:L, 0:512], pN[0:L, :], AF.Square, (pN.b, sc1.b), (junk.b, sc1.b), scale=sc1[0:L, 1:2], accum=sc1[0:L, 2:3])
                    TS('dve', sc1[0:L, 3:4], sc1[0:L, 2:3], 1.0 / DH, EPS, ALU.mult, ALU.add, (sc1.b,), (sc1.b,))
                    RSQ(sc1[0:L, 4:5], sc1[0:L, 3:4], sc1.b)
                    TT('dve', sc1[0:L, 5:6], sc1[0:L, 4:5], sc1[0:L, 1:2], ALU.mult, (sc1.b,), (sc1.b,))
                    ACT(hnb[0:L, :], pN[0:L, :], AF.Identity, (pN.b, sc1.b), (hnb.b,), scale=sc1[0:L, 5:6])
                    if h == 0:
                        ck()
                    for et in range(4):
                        TR(pT[:, 512 + et * 128:512 + et * 128 + L], hnb[0:L, et * 128:(et + 1) * 128], ident_b[0:L, 0:L], (hnb.b, ident_b.b), (pT.b,))
                    TT('dve', hmT[:, h * 4:h * 4 + 4, t0:t0 + L], pT[:, 512:1024].rearrange("p (k t) -> p k t", t=128)[:, :, 0:L],
                       gate[:, :, t0:t0 + L], ALU.mult, (pT.b, gate.b), (hmT.b,))
                    if h == 0:
                        ck()
                    for dt in range(4):
                        MM(pC[:, :], ktb[0:L, dt * 128:(dt + 1) * 128], vbf[0:L, :], True, True, (ktb.b, vbf.b), (pC.b,))
                        STT('dve', Cf(dt), Cf(dt), emc, pC[:, :], ALU.mult, ALU.add, (Cfb, gb.b, pC.b), (Cfb,))
                        CP('act', Cb_(dt), Cf(dt), (Cfb,), (Cbb,))
                        MM(pS[:, 132 + dt:133 + dt], ktb[0:L, dt * 128:(dt + 1) * 128], ones_b[0:L, 0:1], True, True, (ktb.b, ones_b.b), (pS.b,))
                        STT('dve', nf(dt), nf(dt), emc, pS[:, 132 + dt:133 + dt], ALU.mult, ALU.add, (nfb, gb.b, pS.b), (nfb,))
                        CP('dve', nb_(dt), nf(dt), (nfb,), (nbb,))
                    if h == 0:
                        ck()
                    if kind == 'S' and not os.environ.get('K_NOSTORE'):
                        for et in range(4):
                            p = PG()
                            for dt in range(4):
                                TR(p[:, dt * 128:(dt + 1) * 128], Cf(dt)[:, et * 128:(et + 1) * 128], ident_f[:, :], (Cfb, ident_f.b), (p.b,))
                            CP('act', Cst[:, et, :], p[:, :], (p.b,), (Cst.b,))
                        S.dma('sp', o_sC[l, j, h].rearrange("(et p) d -> p et d", p=128), Cst[:, :, :], (Cst.b,), ())
                        if h == 0:
                            ck()
            if kind == 'S':
                for sq in range(4):
                    p = PG()
                    TR(p[0:16, 0:128], nS[:, sq, :], ident_f[:, :], (nS.b, ident_f.b), (p.b,))
                    CP('act', stg[0:16, 0:128], p[0:16, 0:128], (p.b,), (stg.b,))
                    S.dma('sp', o_sn[l, sq], stg[0:16, 0:128], (stg.b,), ())

            ck()
            for dg in range(2):
                wm0 = WL(w_ml[l], 0, dg * 512, 512)
                wm1 = WL(w_ml[l], 1024, dg * 512, 512)
                wgm = WL(Win, 0, C_GM + dg * 512, 512)
                for di in range(4):
                    dt = dg * 4 + di
                    pb = PG()
                    for r in range(16):
                        wt = wm0 if r < 8 else wm1
                        MM(pb[:, 0:ntok], wt[:, r % 8, di * 128:(di + 1) * 128], hmT[:, r, 0:ntok], r == 0, r == 15, (wt.b, hmT.b), (pb.b,))
                    pgm = fm_proj(wgm, di * 128)
                    ACT(sg[:, 0:ntok], pgm[:, 0:ntok], AF.Sigmoid, (pgm.b, bcol.b), (sg.b,), bias=bcol[:, 112 + dt:113 + dt])
                    TT('dve', tmpa[:, 0:ntok], pb[:, 0:ntok], sg[:, 0:ntok], ALU.mult, (pb.b, sg.b), (tmpa.b,))
                    TT('dve', ybf[:, dt, 0:ntok], tmpa[:, 0:ntok], yacc[:, dt, 0:ntok], ALU.add, (tmpa.b, yacc.b), (ybf.b,))
            wo0 = WL(w_out[l], 0, 0, 512)
            wo1 = WL(w_out[l], 0, 512, 512)
            for st in range(nsub):
                r = min(128, ntok - st * 128)
                pa = PG(); pb = PG()
                for kc in range(8):
                    MM(pa[0:r, :], ybf[:, kc, st * 128:st * 128 + r], wo0[:, kc, :], kc == 0, kc == 7, (ybf.b, wo0.b), (pa.b,))
                for kc in range(8):
                    MM(pb[0:r, :], ybf[:, kc, st * 128:st * 128 + r], wo1[:, kc, :], kc == 0, kc == 7, (ybf.b, wo1.b), (pb.b,))
                ACT(junk[0:r, 0:512], pa[0:r, :], AF.Square, (pa.b,), (junk.b, sso.b), accum=sso[0:r, 0:1])
                ACT(junk[0:r, 512:1024], pb[0:r, :], AF.Square, (pb.b,), (junk.b, sso.b), accum=sso[0:r, 1:2])
                TT('dve', sso[0:r, 2:3], sso[0:r, 0:1], sso[0:r, 1:2], ALU.add, (sso.b,), (sso.b,))
                TS('dve', sso[0:r, 2:3], sso[0:r, 2:3], 1.0 / D, EPS, ALU.mult, ALU.add, (sso.b,), (sso.b,))
                RSQ(sso[0:r, 3:4], sso[0:r, 2:3], sso.b)
                yt = yst[0]
                STT('dve', yt[0:r, 0:512], pa[0:r, :], sso[0:r, 3:4], gpost[0:r, 0:512], ALU.mult, ALU.mult, (pa.b, sso.b, gpost.b), (yt.b,))
                STT('dve', yt[0:r, 512:1024], pb[0:r, :], sso[0:r, 3:4], gpost[0:r, 512:1024], ALU.mult, ALU.mult, (pb.b, sso.b, gpost.b), (yt.b,))
                S.dma('sp', xres[0][0:r, :], xsrc[st * 128:st * 128 + r, :], xb, (xres[0].b,))
                TT('pool', yt[0:r, :], yt[0:r, :], xres[0][0:r, :], ALU.add, (yt.b, xres[0].b), (yt.b,))
                S.dma('sp', ydst[st * 128:st * 128 + r, :], yt[0:r, :], (yt.b,), yb)

        try:
            tcount = [0]
            dS, dL = Buf("x1s"), Buf("x1l")
            dP = [Buf("x1p%d" % i) for i in range(NPT)]
            for l in range(2):
                ck()
                layer_params(l)
                ck()
                srcS, srcL, srcP = (xs, meta, xp) if l == 0 else (x1s, x1l, x1p)
                dstS, dstL, dstP = (x1s, x1l, x1p) if l == 0 else (ys, ydump, yp)
                tile(l, 'S', tcount[0], srcS, dstS, () if l == 0 else (dS,), (dS,) if l == 0 else ()); tcount[0] += 1
                MS('dve', CPs[:, :, :], 0.0, (CPs.b,))
                MS('dve', nP[:, :], 0.0, (nP.b,)); MS('dve', nPb[:, :], 0.0, (nPb.b,)); MS('dve', mP[:, :], 0.0, (mP.b,))
                tile(l, 'LEAD', tcount[0], srcL, dstL, () if l == 0 else (dL,), (dL,) if l == 0 else ()); tcount[0] += 1
                for pi in range(NPT):
                    tile(l, 'P', tcount[0], srcP[pi * 512:(pi + 1) * 512, :], dstP[pi * 512:(pi + 1) * 512, :],
                         () if l == 0 else (dP[pi],), (dP[pi],) if l == 0 else ()); tcount[0] += 1
                for c in range(8):
                    p = PG()
                    TR(p[0:30, 0:128], uhP[:, c, :], ident_f[:, :], (uhP.b, ident_f.b), (p.b,))
                    CP('act', stg[0:30, c * 128:(c + 1) * 128], p[0:30, 0:128], (p.b,), (stg.b,))
                S.dma('sp', o_pconv[l], stg[0:30, :], (stg.b,), ())
                for h in range(NH):
                    for i in range(8):
                        qk_off = (0 if i < 4 else 2048) + h * 512 + (i % 4) * 128
                        p = PG()
                        TR(p[0:3, 0:128], qkhP[:, h, i, :], ident_f[:, :], (qkhP.b, ident_f.b), (p.b,))
                        CP('act', stg[0:3, 0:128], p[0:3, 0:128], (p.b,), (stg.b,))
                        S.dma('sp', o_pqk[l][:, qk_off:qk_off + 128], stg[0:3, 0:128], (stg.b,), ())
                    for et in range(4):
                        p = PG()
                        for dt in range(4):
                            TR(p[:, dt * 128:(dt + 1) * 128], CPs[:, h * 4 + dt, et * 128:(et + 1) * 128], ident_f[:, :], (CPs.b, ident_f.b), (p.b,))
                        CP('act', Cst[:, et, :], p[:, :], (p.b,), (Cst.b,))
                    S.dma('sp', o_pC[l, h].rearrange("(et p) d -> p et d", p=128), Cst[:, :, :], (Cst.b,), ())
                p = PG()
                TR(p[0:16, 0:128], nP[:, :], ident_f[:, :], (nP.b, ident_f.b), (p.b,))
                CP('act', stg[0:16, 0:128], p[0:16, 0:128], (p.b,), (stg.b,))
                S.dma('sp', o_pn[l], stg[0:16, 0:128], (stg.b,), ())
                S.dma('sp', o_pm[l].rearrange("(h o) -> h o", o=1), mP[:, :], (mP.b,), ())
        except Stop:
            pass
        for _ in range(int(os.environ.get('K_PAD', '0'))):
            if os.environ.get('K_PADENG', 'dve') == 'pe':
                MM(pS[0:4, 300:304], ones_b[0:4, 0:4], ones_b[0:4, 0:4], True, True, (ones_b.b,), (pS.b,))
            elif os.environ.get('K_PADENG', 'dve') == 'act':
                CP('act', sc4[:, 7:8], sc4[:, 6:7], (sc4.b,), (sc4.b,))
            else:
                MS(os.environ.get('K_PADENG', 'dve'), sc4[:, 7:8], 0.0, (sc4.b,))
        S.finish()
        S.emit()
        import os
        if os.environ.get('K_DEBUG'):
            print('CNT', S.cnt, S.dcnt, {k: len(v) for k, v in S.prog.items()})
    return nc


_CACHE = {}


def _consts():
    c = np.zeros((2, 128, 128), np.float32)
    c[0] = np.eye(128, dtype=np.float32)
    c[1] = np.triu(np.ones((128, 128), np.float32))
    return c


def run(inputs, NPT):
    f = lambda a: np.ascontiguousarray(np.asarray(a, dtype=np.float32))
    I = {k: f(v) for k, v in inputs.items()}
    if NPT not in _CACHE:
        _CACHE[NPT] = build(NPT)
    nc = _CACHE[NPT]
    NP = NPT * 512
    in_maps = []
    for c in range(8):
        b = c // 4
        sq = slice(4 * c, 4 * c + 4)
        m = {
            "xs": I["x_sample"][sq].reshape(256, D), "xp": I["x_prompt"][b, :NP], "meta": I["meta_tokens"],
            "st_conv": I["state_conv"][:, sq].reshape(2, 120, D), "st_qk": I["state_qk_conv"][:, sq].reshape(2, 12, 4096),
            "st_C": I["state_C"][:, sq], "st_n": I["state_n"][:, sq].reshape(2, 4, 16, 128), "st_m": I["state_m"][:, sq],
            "w_in": I["w_in"], "b_in": I["b_in"], "w_dw": I["w_dw"], "b_dw": I["b_dw"].reshape(2, 8, 128),
            "ln_g": I["ln_g"].reshape(2, 8, 128), "ln_b": I["ln_b"].reshape(2, 8, 128),
            "w_qkc": I["w_qk_conv"].reshape(2, 4, 32, 128), "b_qkc": I["b_qk_conv"].reshape(2, 32, 128), "f_bias": I["f_bias"],
            "ml_norm": I["ml_norm"].reshape(2, 16, 128), "w_co": I["w_conv_out"], "w_ml": I["w_ml_out"], "w_out": I["w_out"],
            "norm_pre": I["norm_pre"], "norm_post": I["norm_post"], "cst": _consts(),
        }
        in_maps.append({k: np.ascontiguousarray(v) for k, v in m.items()})
    res = run_bass_kernel_spmd(nc, in_maps, core_ids=list(range(8)))
    R = res.results
    g = lambda c, k: np.asarray(R[c][k], dtype=np.float32)
    y_prompt = np.stack([g(0, "yp"), g(4, "yp")], 0)
    y_sample = np.concatenate([g(c, "ys").reshape(4, 64, D) for c in range(8)], 0)
    pc = lambda k, shp: np.stack([g(0, k), g(4, k)], 1).reshape(shp)
    sc = lambda k, shp: np.concatenate([g(c, k).reshape((2, 4) + shp) for c in range(8)], 1)
    return (y_prompt, y_sample,
            pc("o_pconv", (2, 2, 30, D)), pc("o_pqk", (2, 2, 3, 4096)), pc("o_pC", (2, 2, 4, DH, DH)),
            pc("o_pn", (2, 2, 4, DH)), pc("o_pm", (2, 2, 4)),
            sc("o_sconv", (30, D)), sc("o_sqk", (3, 4096)), sc("o_sC", (4, DH, DH)), sc("o_sn", (4, DH)), sc("o_sm", (4,)))


def kernel(**inputs):
    return run(inputs, 16)
```

```python
import math
from contextlib import ExitStack
import numpy as np
import concourse.bass as bass
import concourse.mybir as mybir
from concourse.bass_utils import run_bass_kernel_spmd

F32 = mybir.dt.float32
BF16 = mybir.dt.bfloat16
ALU = mybir.AluOpType
AF = mybir.ActivationFunctionType
AX = mybir.AxisListType

D = 1024
DIN = 15368
DML = 2048
NH = 4
DH = 512
EPS = 1e-6
C_A, C_GL, C_ZC, C_Q, C_K, C_V, C_O, C_ZM, C_I, C_F, C_GC, C_GM = (
    0, 1024, 2048, 3072, 5120, 7168, 9216, 11264, 13312, 13316, 13320, 14344)
SELF_SYNC = True


class Stop(Exception):
    pass


class Buf:
    def __init__(s, name):
        s.name = name
        s.w = None
        s.r = {}


class TB:
    def __init__(s, t, name):
        s.t = t
        s.b = Buf(name)

    def __getitem__(s, k):
        return s.t[k]


class View:
    def __init__(s, tb, flat, off):
        s.flat = flat
        s.off = off
        s.b = tb.b

    def __getitem__(s, k):
        r, c = k
        c0 = 0 if c.start is None else c.start
        c1 = 512 if c.stop is None else c.stop
        return s.flat[r, s.off + c0:s.off + c1]


class Sch:
    def __init__(s, nc, es):
        s.nc = nc
        s.names = ['pe', 'act', 'dve', 'pool', 'sp']
        s.prog = {n: [] for n in s.names}
        s.sem = {n: es.enter_context(nc.semaphore('s_' + n)) for n in ['pe', 'act', 'dve', 'pool']}
        s.cnt = {n: 0 for n in s.sem}
        s.NQ = 8
        s.dsem = {q: [es.enter_context(nc.semaphore('d_%s%d' % (q, i))) for i in range(s.NQ)] for q in ['sp', 'pool']}
        s.dcnt = {q: 0 for q in s.dsem}
        s.dlast = {}
        s.seen = {n: {} for n in s.names}

    def _wait(s, n, ev):
        k, h, v = ev
        if k == n and (n == 'pe' or not SELF_SYNC):
            return
        if s.seen[n].get(k, 0) >= v:
            return
        s.seen[n][k] = v
        s.prog[n].append(lambda e, h=h, v=v: e.wait_ge(h, v))

    def _deps(s, n, R, W):
        evs = []
        for b in R:
            if b.w is not None:
                evs.append(b.w)
        for b in W:
            if b.w is not None:
                evs.append(b.w)
            evs.extend(b.r.values())
        for ev in evs:
            s._wait(n, ev)

    def _commit(s, ev, R, W):
        for b in R:
            o = b.r.get(ev[0])
            if o is None or o[2] < ev[2]:
                b.r[ev[0]] = ev
        for b in W:
            b.w = ev
            b.r = {}

    def op(s, n, fn, R=(), W=()):
        s._deps(n, R, W)
        s.cnt[n] += 1
        h = s.sem[n]
        s.prog[n].append(lambda e, fn=fn, h=h: fn(e).then_inc(h, 1))
        s._commit((n, h, s.cnt[n]), R, W)

    def dma(s, q, out, in_, R=(), W=(), **kw):
        i = s.dcnt[q]
        s.dcnt[q] += 1
        slot = i % s.NQ
        v = 16 * (i // s.NQ + 1)
        h = s.dsem[q][slot]
        key = (q, slot)
        if v > 16:
            s._wait(q, (key, h, v - 16))
        s._deps(q, R, W)
        s.prog[q].append(lambda e, out=out, in_=in_, h=h, kw=kw: e.dma_start(out=out, in_=in_, **kw).then_inc(h, 16))
        ev = (key, h, v)
        s.dlast[key] = ev
        s._commit(ev, R, W)

    def barrier(s, engs=('pe', 'act', 'dve')):
        for n in engs:
            for m in engs:
                if m != n and s.cnt[m] > 0:
                    s._wait(n, (m, s.sem[m], s.cnt[m]))

    def finish(s):
        for ev in s.dlast.values():
            s._wait('sp', ev)
        for n in s.sem:
            if s.cnt[n] > 0:
                s._wait('sp', (n, s.sem[n], s.cnt[n]))

    def emit(s):
        nc = s.nc
        with nc.Block() as block:
            @block.tensor
            def _(e):
                for f in s.prog['pe']:
                    f(e)

            @block.scalar
            def _(e):
                for f in s.prog['act']:
                    f(e)

            @block.vector
            def _(e):
                for f in s.prog['dve']:
                    f(e)

            @block.gpsimd
            def _(e):
                for f in s.prog['pool']:
                    f(e)

            @block.sync
            def _(e):
                for f in s.prog['sp']:
                    f(e)


def build(NPT):
    nc = bass.Bass("TRN2", target_bir_lowering=False)
    NP = NPT * 512

    def din(name, shape):
        return nc.dram_tensor(name, list(shape), F32, kind="ExternalInput").ap()

    def dout(name, shape):
        return nc.dram_tensor(name, list(shape), F32, kind="ExternalOutput").ap()

    def dint(name, shape):
        return nc.dram_tensor(name, list(shape), F32).ap()

    xs = din("xs", [256, D]); xp = din("xp", [NP, D]); meta = din("meta", [16, D])
    st_conv = din("st_conv", [2, 120, D]); st_qk = din("st_qk", [2, 12, 4096])
    st_C = din("st_C", [2, 4, 4, DH, DH]); st_n = din("st_n", [2, 4, 16, 128]); st_m = din("st_m", [2, 4, 4])
    w_in = din("w_in", [2, D, DIN]); b_in = din("b_in", [2, DIN])
    w_dw = din("w_dw", [2, 31, D]); b_dw = din("b_dw", [2, 8, 128]); ln_g = din("ln_g", [2, 8, 128]); ln_b = din("ln_b", [2, 8, 128])
    w_qkc = din("w_qkc", [2, 4, 32, 128]); b_qkc = din("b_qkc", [2, 32, 128]); f_bias = din("f_bias", [2, 4])
    ml_norm = din("ml_norm", [2, 16, 128]); w_co = din("w_co", [2, D, D]); w_ml = din("w_ml", [2, DML, D]); w_out = din("w_out", [2, D, D])
    norm_pre = din("norm_pre", [2, D]); norm_post = din("norm_post", [2, D])
    cst = din("cst", [2, 128, 128])
    ys = dout("ys", [256, D]); yp = dout("yp", [NP, D])
    o_sconv = dout("o_sconv", [2, 120, D]); o_sqk = dout("o_sqk", [2, 12, 4096])
    o_sC = dout("o_sC", [2, 4, 4, DH, DH]); o_sn = dout("o_sn", [2, 4, 16, 128]); o_sm = dout("o_sm", [2, 4, 4])
    o_pconv = dout("o_pconv", [2, 30, D]); o_pqk = dout("o_pqk", [2, 3, 4096])
    o_pC = dout("o_pC", [2, 4, DH, DH]); o_pn = dout("o_pn", [2, 16, 128]); o_pm = dout("o_pm", [2, 4])
    x1s = dint("x1s", [256, D]); x1p = dint("x1p", [NP, D]); x1l = dint("x1l", [16, D]); ydump = dint("ydump", [16, D])

    es = ExitStack()
    with es:
        S = Sch(nc, es)

        def sb(name, shape, dt=F32):
            return TB(es.enter_context(nc.sbuf_tensor(name, list(shape), dt)), name)

        def pst(name, shape, dt=F32):
            return TB(es.enter_context(nc.psum_tensor(name, list(shape), dt)), name)

        def MM(out, lhsT, rhs, st, sp, R, W):
            S.op('pe', lambda e: e.matmul(out, lhsT, rhs, start=st, stop=sp), R, W)

        def TR(out, in_, ident, R, W):
            S.op('pe', lambda e: e.transpose(out, in_, ident), R, W)

        def ACT(out, in_, func, R, W, bias=0.0, scale=1.0, accum=None):
            if accum is None:
                S.op('act', lambda e: e.activation(out, in_, func, bias=bias, scale=scale), R, W)
            else:
                S.op('act', lambda e: e.activation(out, in_, func, bias=bias, scale=scale, accum_out=accum), R, W)

        def TT(eng, out, in0, in1, op, R, W):
            S.op(eng, lambda e: e.tensor_tensor(out, in0, in1, op), R, W)

        def TS(eng, out, in0, s1, s2, op0, op1, R, W):
            if s2 is None:
                S.op(eng, lambda e: e.tensor_scalar(out, in0, s1, None, op0), R, W)
            else:
                S.op(eng, lambda e: e.tensor_scalar(out, in0, s1, s2, op0, op1), R, W)

        def STT(eng, out, in0, sc, in1, op0, op1, R, W):
            S.op(eng, lambda e: e.scalar_tensor_tensor(out, in0, sc, in1, op0, op1), R, W)

        def CP(eng, out, in_, R, W):
            if eng == 'act':
                S.op('act', lambda e: e.copy(out, in_), R, W)
            else:
                S.op(eng, lambda e: e.tensor_copy(out, in_), R, W)

        def RSQ(out, in_, b):
            S.op('act', lambda e: e.activation(out, in_, AF.Sqrt), (b,), (b,))
            S.op('dve', lambda e: e.reciprocal(out, out), (b,), (b,))

        def MS(eng, ap, val, W):
            S.op(eng, lambda e: e.memset(ap, val), (), W)

        ident_f = sb("ident_f", [128, 128]); tri_f = sb("tri_f", [128, 128])
        ident_b = sb("ident_b", [128, 128], BF16); tri_b = sb("tri_b", [128, 128], BF16)
        ones_b = sb("ones_b", [128, 128], BF16); ones_f = sb("ones_f", [128, 128])
        gpre = sb("gpre", [128, D]); gpost = sb("gpost", [128, D]); bvbc = sb("bvbc", [128, 512])
        bcol = sb("bcol", [128, 120]); wdw = sb("wdw", [128, 8, 32]); misc = sb("misc", [128, 24])
        wqk = sb("wqk", [128, 128]); bqk = sb("bqk", [128, 32]); mln = sb("mln", [128, 16])
        bi_c = sb("bi_c", [4, 1]); bf_c = sb("bf_c", [4, 1]); fb_c = sb("fb_c", [4, 1]); nbfz = sb("nbfz", [4, 1])
        wif = sb("wif", [128, 8, 8], BF16)
        stg = sb("stg", [128, D])
        xres = [sb("xres0", [128, D])] * 4
        wstg = sb("wstg", [128, 8, 512])
        hn = sb("hn", [128, D], BF16)
        hT = [sb("hT0", [128, 8, 512], BF16)]
        ss = sb("ss", [128, 8]); junk = hn
        NWB = 3
        wb = [sb("wb%d" % i, [128, 8, 512], BF16) for i in range(NWB)]
        ubufs = [sb("ubuf%d" % i, [128, 544]) for i in range(2)]; sg = sb("sg", [128, 512])
        ucv = sb("ucv", [128, 8, 512]); ubf = sb("ubf", [128, 512], BF16); usq = sb("usq", [128, 512], BF16)
        tmpa = sb("tmpa", [128, 512])
        ucb = sb("ucb", [128, 8, 512], BF16)
        Cst = sb("Cst", [128, 4, 512])
        _cf = Cst[:, :, :].rearrange("p a b -> p (a b)")
        mean = View(Cst, _cf, 0); rstd = View(Cst, _cf, 512); nmr = View(Cst, _cf, 1024); tmpb = View(Cst, _cf, 1536)
        yacc = ucv; ybf = ucb
        qkbufs = [sb("qkbuf%d" % i, [128, 520]) for i in range(2)]; qkacc = sb("qkacc", [128, 512])
        qT = sb("qT", [128, 4, 512], BF16); kT = sb("kT", [128, 4, 512], BF16); qs = sb("qs", [128, 4, 128], BF16)
        gate = sb("gate", [128, 4, 512], BF16); so = tmpa
        hmT = sb("hmT", [128, 16, 512], BF16)
        vbf = sb("vbf", [128, 512], BF16); ktb = sb("ktb", [128, 512], BF16); stb = sb("stb", [128, 128], BF16)
        hnb = sb("hnb", [128, 512], BF16)
        irow = sb("irow", [4, 512]); lfrow = sb("lfrow", [4, 512]); cum = sb("cum", [4, 128]); grow = sb("grow", [4, 128])
        arow = sb("arow", [4, 128]); flrow = sb("flrow", [4, 128]); onesrow = sb("onesrow", [4, 128])
        sc4 = sb("sc4", [4, 8]); dg4 = sb("dg4", [4, 4])
        gb = sb("gb", [128, 4, 12])
        sc1 = sb("sc1", [128, 8])
        uhP = sb("uhP", [128, 8, 30]); qkhP = sb("qkhP", [128, 4, 8, 3])
        CPs = sb("CPs", [128, 16, 512]); Cwb = sb("Cwb", [128, 4, 512], BF16)
        nP = sb("nP", [128, 16]); nPb = sb("nPb", [128, 16], BF16); mP = sb("mP", [4, 1])
        nS = sb("nS", [128, 4, 16]); nSb = sb("nSb", [128, 4, 16], BF16); mS = sb("mS", [4, 4]); mSo = sb("mSo", [4, 4])
        yst = [sb("yst0", [128, D])]
        sso = sb("sso", [128, 4])

        pg = [pst("pg%d" % i, [128, 512]) for i in range(4)]
        pT = pst("pT", [128, 1024], BF16)
        pN = pst("pN", [128, 512])
        pC = pst("pC", [128, 512])
        pS = pst("pS", [128, 512])
        pgi = [0]

        def PG():
            pgi[0] += 1
            return pg[pgi[0] % 4]

        wbi = [0]

        NBLK = 80
        wsc = nc.dram_tensor("wsc", [NBLK, 128, 8 * 512], BF16).ap()
        wmap = {}
        wbufs = {}
        cur_l = [0]

        def WL(tag, dram2d, r0, c0, ncols):
            t = wb[wbi[0] % NWB]
            wbi[0] += 1
            key = (cur_l[0], tag, r0, c0)
            if key not in wmap:
                idx = len(wmap)
                assert idx < NBLK
                wmap[key] = idx
                wbufs[key] = Buf("wsc%d" % idx)
                src = dram2d[r0:r0 + 1024, c0:c0 + ncols].rearrange("(kc p) n -> p kc n", p=128)
                S.dma('sp', wstg[:, :, 0:ncols], src, (), (wstg.b,))
                CP('pool', t[:, :, 0:ncols], wstg[:, :, 0:ncols], (wstg.b,), (t.b,))
                S.dma('sp', wsc[idx].rearrange("p (k n) -> p k n", n=512), t[:, :, :], (t.b,), (wbufs[key],))
            else:
                idx = wmap[key]
                S.dma('sp', t[:, :, :], wsc[idx].rearrange("p (k n) -> p k n", n=512), (wbufs[key],), (t.b,))
            return t

        S.dma('sp', ident_f[:, :], cst[0], (), (ident_f.b,))
        S.dma('sp', tri_f[:, :], cst[1], (), (tri_f.b,))
        CP('dve', ident_b[:, :], ident_f[:, :], (ident_f.b,), (ident_b.b,))
        CP('dve', tri_b[:, :], tri_f[:, :], (tri_f.b,), (tri_b.b,))
        MS('dve', ones_b[:, :], 1.0, (ones_b.b,))
        MS('dve', ones_f[:, :], 1.0, (ones_f.b,))
        MS('dve', onesrow[:, :], 1.0, (onesrow.b,))

        def load_cols(dst_ap, dstb, src_rows_ap, nrows):
            S.dma('sp', stg[0:nrows, 0:128], src_rows_ap, (), (stg.b,))
            p = PG()
            TR(p[:, 0:nrows], stg[0:nrows, 0:128], ident_f[0:nrows, 0:nrows], (stg.b, ident_f.b), (p.b,))
            CP('dve', dst_ap, p[:, 0:nrows], (p.b,), (dstb,))

        def layer_params(l):
            S.dma('sp', gpre[:, :], norm_pre[l].partition_broadcast(128), (), (gpre.b,))
            S.dma('sp', gpost[:, :], norm_post[l].partition_broadcast(128), (), (gpost.b,))
            load_cols(bcol[:, 0:104], bcol.b, b_in[l, 0:13312].rearrange("(r c) -> r c", c=128), 104)
            load_cols(bcol[:, 104:120], bcol.b, b_in[l, C_GC:C_GC + 2048].rearrange("(r c) -> r c", c=128), 16)
            for c in range(8):
                load_cols(wdw[:, c, 0:31], wdw.b, w_dw[l, :, c * 128:(c + 1) * 128], 31)
            load_cols(misc[:, 0:8], misc.b, b_dw[l], 8)
            load_cols(misc[:, 8:16], misc.b, ln_g[l], 8)
            load_cols(misc[:, 16:24], misc.b, ln_b[l], 8)
            load_cols(wqk[:, :], wqk.b, w_qkc[l].rearrange("t c p -> (t c) p"), 128)
            load_cols(bqk[:, :], bqk.b, b_qkc[l], 32)
            load_cols(mln[:, :], mln.b, ml_norm[l], 16)
            S.dma('sp', bi_c[:, :], b_in[l, C_I:C_I + 4].rearrange("(h o) -> h o", o=1), (), (bi_c.b,))
            S.dma('sp', bf_c[:, :], b_in[l, C_F:C_F + 4].rearrange("(h o) -> h o", o=1), (), (bf_c.b,))
            S.dma('sp', fb_c[:, :], f_bias[l].rearrange("(h o) -> h o", o=1), (), (fb_c.b,))
            STT('dve', nbfz[:, :], bf_c[:, :], -1.0, fb_c[:, :], ALU.mult, ALU.subtract, (bf_c.b, fb_c.b), (nbfz.b,))
            S.dma('sp', stg[:, 0:64].rearrange("p (k n) -> p k n", n=8), w_in[l, :, C_I:C_I + 8].rearrange("(kc p) n -> p kc n", p=128), (), (stg.b,))
            CP('dve', wif[:, :, :], stg[:, 0:64].rearrange("p (k n) -> p k n", n=8), (stg.b,), (wif.b,))

        LNSC = math.log(DH ** -0.5)
        import os
        kstop = int(os.environ.get('K_STOP', '-1'))
        ckc = [0]

        def ck():
            if ckc[0] == kstop:
                raise Stop()
            ckc[0] += 1

        def tile(l, kind, tix, xsrc, ydst, xb, yb):
            if kind == 'S':
                nseg, SL, L = 4, 64, 64
            elif kind == 'LEAD':
                nseg, SL, L = 1, 16, 16
            else:
                nseg, SL, L = 1, 512, 128
            ntok = nseg * SL
            nch = ntok // L
            nsub = (ntok + 127) // 128
            h_T = hT[0]
            cur_l[0] = l
            Win = w_in[l]

            def segv(ap2d, w):
                return ap2d.rearrange("p (s w) -> p s w", w=w)

            for st in range(nsub):
                r = min(128, ntok - st * 128)
                xt = xres[st]
                S.dma('sp', xt[0:r, :], xsrc[st * 128:st * 128 + r, :], xb, (xt.b,))
                ACT(junk[0:r, :], xt[0:r, :], AF.Square, (xt.b,), (junk.b, ss.b), accum=ss[0:r, 0:1])
                TS('dve', ss[0:r, 1:2], ss[0:r, 0:1], 1.0 / D, EPS, ALU.mult, ALU.add, (ss.b,), (ss.b,))
                RSQ(ss[0:r, 2:3], ss[0:r, 1:2], ss.b)
                STT('dve', hn[0:r, :], xt[0:r, :], ss[0:r, 2:3], gpre[0:r, :], ALU.mult, ALU.mult, (xt.b, ss.b, gpre.b), (hn.b,))
                for kc in range(8):
                    TR(pT[:, kc * 128:kc * 128 + r], hn[0:r, kc * 128:(kc + 1) * 128], ident_b[0:r, 0:r], (hn.b, ident_b.b), (pT.b,))
                CP('act', h_T[:, :, st * 128:st * 128 + r], pT[:, :].rearrange("p (k t) -> p k t", t=128)[:, :, 0:r], (pT.b,), (h_T.b,))

            ck()

            def fm_proj(wt, coff, R_extra=()):
                p = PG()
                for kc in range(8):
                    MM(p[:, 0:ntok], wt[:, kc, coff:coff + 128], h_T[:, kc, 0:ntok], kc == 0, kc == 7, (wt.b, h_T.b), (p.b,))
                return p

            W30 = 30 + SL
            if kind == 'S':
                S.dma('sp', stg[0:120, :], st_conv[l], (), (stg.b,))
            CstF = Cst[:, :, :].rearrange("p a b -> p (a b)")
            for cg in range(2):
                wa = WL('in', Win, 0, C_A + cg * 512, 512)
                wg = WL('in', Win, 0, C_GL + cg * 512, 512)
                for ci in range(4):
                    c = cg * 4 + ci
                    ub = ubufs[c % 2]
                    uvc = segv(ub[:, 0:nseg * W30], W30)
                    if kind == 'S':
                        p = PG()
                        TR(p[:, 0:120], stg[0:120, c * 128:(c + 1) * 128], ident_f[0:120, 0:120], (stg.b, ident_f.b), (p.b,))
                        CP('dve', uvc[:, :, 0:30], segv(p[:, 0:120], 30), (p.b,), (ub.b,))
                    elif kind == 'LEAD':
                        MS('dve', ub[:, 0:30], 0.0, (ub.b,))
                    else:
                        CP('dve', ub[:, 0:30], uhP[:, c, :], (uhP.b,), (ub.b,))
                    pa = fm_proj(wa, ci * 128)
                    pgl = fm_proj(wg, ci * 128)
                    ACT(sg[:, 0:ntok], pgl[:, 0:ntok], AF.Sigmoid, (pgl.b, bcol.b), (sg.b,), bias=bcol[:, 8 + c:9 + c])
                    STT('dve', uvc[:, :, 30:W30], segv(pa[:, 0:ntok], SL), bcol[:, c:c + 1], segv(sg[:, 0:ntok], SL),
                        ALU.add, ALU.mult, (pa.b, sg.b, bcol.b), (ub.b,))
                    eng = 'dve'
                    acc = segv(ucv[:, c, 0:ntok], SL)
                    TS(eng, acc, uvc[:, :, 0:SL], wdw[:, c, 0:1], misc[:, c:c + 1], ALU.mult, ALU.add, (ub.b, wdw.b, misc.b), (ucv.b,))
                    for k in range(1, 31):
                        STT(eng, acc, uvc[:, :, k:k + SL], wdw[:, c, k:k + 1], acc, ALU.mult, ALU.add, (ub.b, wdw.b), (ucv.b,))
                    if kind == 'S':
                        p = PG()
                        CP('dve', segv(tmpa[:, 0:120], 30), uvc[:, :, SL:W30], (ub.b,), (tmpa.b,))
                        TR(p[0:120, 0:128], tmpa[:, 0:120], ident_f[:, :], (tmpa.b, ident_f.b), (p.b,))
                        CP('act', CstF[0:120, c * 128:(c + 1) * 128], p[0:120, 0:128], (p.b,), (Cst.b,))
                    else:
                        CP('dve', uhP[:, c, :], ub[:, SL:W30], (ub.b,), (uhP.b,))
            if kind == 'S':
                S.dma('sp', o_sconv[l], CstF[0:120, 0:1024], (Cst.b,), ())

            ck()
            p_sum = PG(); p_sq = PG()
            for c in range(8):
                CP('act', ubf[:, 0:ntok], ucv[:, c, 0:ntok], (ucv.b,), (ubf.b,))
                ACT(usq[:, 0:ntok], ucv[:, c, 0:ntok], AF.Square, (ucv.b,), (usq.b,))
                MM(p_sum[:, 0:ntok], ones_b[:, :], ubf[:, 0:ntok], c == 0, c == 7, (ones_b.b, ubf.b), (p_sum.b,))
                MM(p_sq[:, 0:ntok], ones_b[:, :], usq[:, 0:ntok], c == 0, c == 7, (ones_b.b, usq.b), (p_sq.b,))
            TS('dve', mean[:, 0:ntok], p_sum[:, 0:ntok], 1.0 / D, None, ALU.mult, None, (p_sum.b,), (mean.b,))
            TT('dve', tmpa[:, 0:ntok], mean[:, 0:ntok], mean[:, 0:ntok], ALU.mult, (mean.b,), (tmpa.b,))
            STT('dve', rstd[:, 0:ntok], p_sq[:, 0:ntok], 1.0 / D, tmpa[:, 0:ntok], ALU.mult, ALU.subtract, (p_sq.b, tmpa.b), (rstd.b,))
            TS('dve', rstd[:, 0:ntok], rstd[:, 0:ntok], EPS, None, ALU.add, None, (rstd.b,), (rstd.b,))
            RSQ(rstd[:, 0:ntok], rstd[:, 0:ntok], rstd.b)
            STT('dve', nmr[:, 0:ntok], mean[:, 0:ntok], -1.0, rstd[:, 0:ntok], ALU.mult, ALU.mult, (mean.b, rstd.b), (nmr.b,))
            for cg in range(2):
                wz = WL('in', Win, 0, C_ZC + cg * 512, 512)
                for ci in range(4):
                    c = cg * 4 + ci
                    pz = fm_proj(wz, ci * 128)
                    TT('dve', tmpa[:, 0:ntok], ucv[:, c, 0:ntok], rstd[:, 0:ntok], ALU.mult, (ucv.b, rstd.b), (tmpa.b,))
                    TT('dve', tmpa[:, 0:ntok], tmpa[:, 0:ntok], nmr[:, 0:ntok], ALU.add, (tmpa.b, nmr.b), (tmpa.b,))
                    ACT(tmpb[:, 0:ntok], tmpa[:, 0:ntok], AF.Silu, (tmpa.b, misc.b), (tmpb.b,), bias=misc[:, 16 + c:17 + c], scale=misc[:, 8 + c:9 + c])
                    ACT(sg[:, 0:ntok], pz[:, 0:ntok], AF.Silu, (pz.b, bcol.b), (sg.b,), bias=bcol[:, 16 + c:17 + c])
                    TT('dve', ucb[:, c, 0:ntok], tmpb[:, 0:ntok], sg[:, 0:ntok], ALU.mult, (tmpb.b, sg.b), (ucb.b,))
            for dg in range(2):
                wc = WL('co', w_co[l], 0, dg * 512, 512)
                wgc = WL('in', Win, 0, C_GC + dg * 512, 512)
                for di in range(4):
                    dt = dg * 4 + di
                    pb = PG()
                    for c in range(8):
                        MM(pb[:, 0:ntok], wc[:, c, di * 128:(di + 1) * 128], ucb[:, c, 0:ntok], c == 0, c == 7, (wc.b, ucb.b), (pb.b,))
                    pgc = fm_proj(wgc, di * 128)
                    ACT(sg[:, 0:ntok], pgc[:, 0:ntok], AF.Sigmoid, (pgc.b, bcol.b), (sg.b,), bias=bcol[:, 104 + dt:105 + dt])
                    TT('dve', yacc[:, dt, 0:ntok], pb[:, 0:ntok], sg[:, 0:ntok], ALU.mult, (pb.b, sg.b), (yacc.b,))

            ck()
            p_i = PG(); p_f = PG()
            for kc in range(8):
                MM(p_i[0:4, 0:ntok], wif[:, kc, 0:4], h_T[:, kc, 0:ntok], kc == 0, kc == 7, (wif.b, h_T.b), (p_i.b,))
            for kc in range(8):
                MM(p_f[0:4, 0:ntok], wif[:, kc, 4:8], h_T[:, kc, 0:ntok], kc == 0, kc == 7, (wif.b, h_T.b), (p_f.b,))
            ACT(irow[:, 0:ntok], p_i[0:4, 0:ntok], AF.Identity, (p_i.b, bi_c.b), (irow.b,), bias=bi_c[:, 0:1])
            ACT(lfrow[:, 0:ntok], p_f[0:4, 0:ntok], AF.Exp, (p_f.b, nbfz.b), (lfrow.b,), bias=nbfz[:, 0:1], scale=-1.0)
            ACT(lfrow[:, 0:ntok], lfrow[:, 0:ntok], AF.Ln, (lfrow.b,), (lfrow.b,), bias=1.0)
            if kind == 'S':
                S.dma('sp', mS[:, :], st_m[l].rearrange("s h -> h s"), (), (mS.b,), allow_slow_non_contiguous=True)
            for j in range(nch):
                t0 = j * L
                mcur = mS[:, j:j + 1] if kind == 'S' else mP[:, 0:1]
                mb = mS.b if kind == 'S' else mP.b
                mnew = mSo[:, j:j + 1] if kind == 'S' else mP[:, 0:1]
                mnb = mSo.b if kind == 'S' else mP.b
                S.op('dve', lambda e, t0=t0: e.tensor_tensor_scan(cum[:, 0:L], onesrow[:, 0:L], lfrow[:, t0:t0 + L], 0.0, ALU.mult, ALU.add),
                     (onesrow.b, lfrow.b), (cum.b,))
                TT('dve', grow[:, 0:L], irow[:, t0:t0 + L], cum[:, 0:L], ALU.add, (irow.b, cum.b), (grow.b,))
                S.op('dve', lambda e: e.reduce_max(sc4[:, 0:1], grow[:, 0:L], AX.X), (grow.b,), (sc4.b,))
                TT('dve', sc4[:, 1:2], sc4[:, 0:1], mcur, ALU.max, (sc4.b, mb), (sc4.b,))
                TS('dve', sc4[:, 2:3], sc4[:, 1:2], -1.0, LNSC, ALU.mult, ALU.add, (sc4.b,), (sc4.b,))
                TS('dve', sc4[:, 3:4], sc4[:, 1:2], -1.0, None, ALU.mult, None, (sc4.b,), (sc4.b,))
                ACT(arow[:, 0:L], grow[:, 0:L], AF.Exp, (grow.b, sc4.b), (arow.b,), bias=sc4[:, 2:3])
                ACT(flrow[:, 0:L], cum[:, 0:L], AF.Exp, (cum.b, sc4.b), (flrow.b,), bias=sc4[:, 3:4])
                ACT(sc4[:, 4:5], mcur, AF.Exp, (mb, sc4.b), (sc4.b,), bias=sc4[:, 3:4])
                TT('dve', mnew, sc4[:, 1:2], cum[:, L - 1:L], ALU.subtract, (sc4.b, cum.b), (mnb,))
                TS('dve', dg4[:, :], ident_f[0:4, 0:4], sc4[:, 4:5], None, ALU.mult, None, (ident_f.b, sc4.b), (dg4.b,))
                TR(pS[0:L, 256:260], arow[:, 0:L], ident_f[0:4, 0:4], (arow.b, ident_f.b), (pS.b,))
                TR(pS[0:L, 260:264], flrow[:, 0:L], ident_f[0:4, 0:4], (flrow.b, ident_f.b), (pS.b,))
                MM(pS[:, 264:268], ones_f[0:4, :], dg4[:, :], True, True, (ones_f.b, dg4.b), (pS.b,))
                CP('dve', gb[0:L, j, 0:8], pS[0:L, 256:264], (pS.b,), (gb.b,))
                CP('dve', gb[:, j, 8:12], pS[:, 264:268], (pS.b,), (gb.b,))
            if kind == 'S':
                S.dma('sp', o_sm[l].rearrange("s h -> h s"), mSo[:, :], (mSo.b,), (), allow_slow_non_contiguous=True)
                for sq in range(4):
                    load_cols(nS[:, sq, :], nS.b, st_n[l, sq], 16)

            ck()
            W3 = 3 + SL
            for h in range(NH):
                wq = WL('in', Win, 0, C_Q + h * 512, 512)
                wk = WL('in', Win, 0, C_K + h * 512, 512)
                S.dma('sp', bvbc[:, :], b_in[l, C_V + h * 512:C_V + (h + 1) * 512].partition_broadcast(128), (), (bvbc.b,))
                for i in range(8):
                    wt = wq if i < 4 else wk
                    cidx = (24 if i < 4 else 40) + h * 4 + (i % 4)
                    qc = (0 if i < 4 else 16) + h * 4 + (i % 4)
                    qk_off = (0 if i < 4 else 2048) + h * 512 + (i % 4) * 128
                    qb = qkbufs[i % 2]
                    qvi = segv(qb[:, 0:nseg * W3], W3)
                    if kind == 'S':
                        S.dma('sp', stg[0:12, 0:128], st_qk[l][:, qk_off:qk_off + 128], (), (stg.b,))
                        p = PG()
                        TR(p[:, 0:12], stg[0:12, 0:128], ident_f[0:12, 0:12], (stg.b, ident_f.b), (p.b,))
                        CP('dve', qvi[:, :, 0:3], segv(p[:, 0:12], 3), (p.b,), (qb.b,))
                    elif kind == 'LEAD':
                        MS('dve', qb[:, 0:3], 0.0, (qb.b,))
                    else:
                        CP('dve', qb[:, 0:3], qkhP[:, h, i, :], (qkhP.b,), (qb.b,))
                    p = fm_proj(wt, (i % 4) * 128)
                    ACT(qvi[:, :, 3:W3], segv(p[:, 0:ntok], SL), AF.Identity, (p.b, bcol.b), (qb.b,), bias=bcol[:, cidx:cidx + 1])
                    eng = 'dve'
                    acc = segv(qkacc[:, 0:ntok], SL)
                    TS(eng, acc, qvi[:, :, 0:SL], wqk[:, qc:qc + 1], bqk[:, qc:qc + 1], ALU.mult, ALU.add, (qb.b, wqk.b, bqk.b), (qkacc.b,))
                    for k in range(1, 4):
                        STT(eng, acc, qvi[:, :, k:k + SL], wqk[:, k * 32 + qc:k * 32 + qc + 1], acc, ALU.mult, ALU.add, (qb.b, wqk.b), (qkacc.b,))
                    dst = qT if i < 4 else kT
                    ACT(dst[:, i % 4, 0:ntok], qkacc[:, 0:ntok], AF.Silu, (qkacc.b,), (dst.b,))
                    if kind == 'S':
                        CP('dve', segv(tmpa[:, 0:12], 3), qvi[:, :, SL:W3], (qb.b,), (tmpa.b,))
                        p = PG()
                        TR(p[0:12, 0:128], tmpa[:, 0:12], ident_f[:, :], (tmpa.b, ident_f.b), (p.b,))
                        CP('act', stg[0:12, 128:256], p[0:12, 0:128], (p.b,), (stg.b,))
                        S.dma('sp', o_sqk[l][:, qk_off:qk_off + 128], stg[0:12, 128:256], (stg.b,), ())
                    else:
                        CP('dve', qkhP[:, h, i, :], qb[:, SL:W3], (qb.b,), (qkhP.b,))
                    if h == 0 and i in (0, 7):
                        ck()
                wo = WL('in', Win, 0, C_O + h * 512, 512)
                wz = WL('in', Win, 0, C_ZM + h * 512, 512)
                for et in range(4):
                    po = fm_proj(wo, et * 128)
                    pz = fm_proj(wz, et * 128)
                    ACT(so[:, 0:ntok], po[:, 0:ntok], AF.Sigmoid, (po.b, bcol.b), (so.b,), bias=bcol[:, 72 + h * 4 + et:73 + h * 4 + et])
                    ACT(sg[:, 0:ntok], pz[:, 0:ntok], AF.Silu, (pz.b, bcol.b), (sg.b,), bias=bcol[:, 88 + h * 4 + et:89 + h * 4 + et])
                    STT('dve', gate[:, et, 0:ntok], so[:, 0:ntok], mln[:, h * 4 + et:h * 4 + et + 1], sg[:, 0:ntok], ALU.mult, ALU.mult,
                        (so.b, sg.b, mln.b), (gate.b,))
                ck()
                wv = WL('in', Win, 0, C_V + h * 512, 512)
                ck()
                if kind != 'S':
                    CP('act', Cwb[:, :, :], CPs[:, h * 4:h * 4 + 4, :], (CPs.b,), (Cwb.b,))
                for j in range(int(os.environ.get('K_J0', '0')) if h == 0 else 0, nch):
                    t0 = j * L
                    if os.environ.get('K_BAR'):
                        S.barrier()
                    hb = 0 if kind == 'S' else h * 4
                    Cf = lambda dt, hb=hb: CPs[:, hb + dt, :]
                    Cb_ = lambda dt: Cwb[:, dt, :]
                    Cfb, Cbb = CPs.b, Cwb.b
                    if kind == 'S':
                        if not (os.environ.get('K_NOLOAD') and j > 0):
                            S.dma('sp', Cst[:, :, :], st_C[l, j, h].rearrange("(et p) d -> p et d", p=128), (), (Cst.b,))
                        for dt in range(4):
                            p = PG()
                            for et in range(4):
                                if not (os.environ.get('K_E2') and j > 0):
                                    TR(p[:, et * 128:(et + 1) * 128], Cst[:, et, dt * 128:(dt + 1) * 128], ident_f[:, :], (Cst.b, ident_f.b), (p.b,))
                            e3 = os.environ.get('K_E3', '') if j > 0 else ''
                            if e3 not in ('1', 'dve'):
                                CP('dve', Cf(dt), p[:, :], (p.b,), (Cfb,))
                            if e3 not in ('1', 'act'):
                                CP('act', Cb_(dt), Cf(dt), (Cfb,), (Cbb,))
                        CP('dve', nSb[:, j, h * 4:h * 4 + 4], nS[:, j, h * 4:h * 4 + 4], (nS.b,), (nSb.b,))
                        nf = lambda dt, j=j, h=h: nS[:, j, h * 4 + dt:h * 4 + dt + 1]
                        nb_ = lambda dt, j=j, h=h: nSb[:, j, h * 4 + dt:h * 4 + dt + 1]
                        nfb, nbb = nS.b, nSb.b
                    else:
                        nf = lambda dt, h=h: nP[:, h * 4 + dt:h * 4 + dt + 1]
                        nb_ = lambda dt, h=h: nPb[:, h * 4 + dt:h * 4 + dt + 1]
                        nfb, nbb = nP.b, nPb.b
                    if h == 0:
                        ck()
                    acol = gb[0:L, j, h:h + 1]
                    flcol = gb[0:L, j, 4 + h:5 + h]
                    emc = gb[:, j, 8 + h:9 + h]
                    pv = PG()
                    for kc in range(8):
                        MM(pv[0:L, :], h_T[:, kc, t0:t0 + L], wv[:, kc, :], kc == 0, kc == 7, (h_T.b, wv.b), (pv.b,))
                    TT('dve', vbf[0:L, :], pv[0:L, :], bvbc[0:L, :], ALU.add, (pv.b, bvbc.b), (vbf.b,))
                    for dt in range(4):
                        MM(pS[0:L, 0:L], kT[:, dt, t0:t0 + L], qT[:, dt, t0:t0 + L], dt == 0, dt == 3, (kT.b, qT.b), (pS.b,))
                    STT('dve', stb[0:L, 0:L], pS[0:L, 0:L], acol, tri_f[0:L, 0:L], ALU.mult, ALU.mult, (pS.b, gb.b, tri_f.b), (stb.b,))
                    if h == 0:
                        ck()
                    for dt in range(4):
                        TR(pT[0:L, dt * 128:(dt + 1) * 128], kT[:, dt, t0:t0 + L], ident_b[:, :], (kT.b, ident_b.b), (pT.b,))
                    TS('dve', ktb[0:L, :], pT[0:L, 0:512], acol, None, ALU.mult, None, (pT.b, gb.b), (ktb.b,))
                    TS('dve', qs[:, :, 0:L], qT[:, :, t0:t0 + L], emc, None, ALU.mult, None, (qT.b, gb.b), (qs.b,))
                    if h == 0:
                        ck()
                    MM(pN[0:L, :], stb[0:L, 0:L], vbf[0:L, :], True, False, (stb.b, vbf.b), (pN.b,))
                    for dt in range(4):
                        MM(pN[0:L, :], qs[:, dt, 0:L], Cb_(dt), False, dt == 3, (qs.b, Cbb), (pN.b,))
                    MM(pS[0:L, 128:129], stb[0:L, 0:L], ones_b[0:L, 0:1], True, False, (stb.b, ones_b.b), (pS.b,))
                    for dt in range(4):
                        MM(pS[0:L, 128:129], qs[:, dt, 0:L], nb_(dt), False, dt == 3, (qs.b, nbb), (pS.b,))
                    if h == 0:
                        ck()
                    CP('dve', sc1[0:L, 6:7], pS[0:L, 128:129], (pS.b,), (sc1.b,))
                    STT('dve', sc1[0:L, 7:8], sc1[0:L, 6:7], -1.0, sc1[0:L, 6:7], ALU.mult, ALU.max, (sc1.b,), (sc1.b,))
                    TT('dve', sc1[0:L, 0:1], sc1[0:L, 7:8], flcol, ALU.max, (sc1.b, gb.b), (sc1.b,))
                    S.op('dve', lambda e: e.reciprocal(sc1[0:L, 1:2], sc1[0:L, 0:1]), (sc1.b,), (sc1.b,))
                    ACT(junk[0:L, 0:512], pN[0:L, :], AF.Square, (pN.b, sc1.b), (junk.b, sc1.b), scale=sc1[0:L, 1:2], accum=sc1[0:L, 2:3])
                    TS('dve', sc1[0:L, 3:4], sc1[0:L, 2:3], 1.0 / DH, EPS, ALU.mult, ALU.add, (sc1.b,), (sc1.b,))
                    RSQ(sc1[0:L, 4:5], sc1[0:L, 3:4], sc1.b)
                    TT('dve', sc1[0:L, 5:6], sc1[0:L, 4:5], sc1[0:L, 1:2], ALU.mult, (sc1.b,), (sc1.b,))
                    ACT(hnb[0:L, :], pN[0:L, :], AF.Identity, (pN.b, sc1.b), (hnb.b,), scale=sc1[0:L, 5:6])
                    if h == 0:
                        ck()
                    for et in range(4):
                        TR(pT[:, 512 + et * 128:512 + et * 128 + L], hnb[0:L, et * 128:(et + 1) * 128], ident_b[0:L, 0:L], (hnb.b, ident_b.b), (pT.b,))
                    TT('dve', hmT[:, h * 4:h * 4 + 4, t0:t0 + L], pT[:, 512:1024].rearrange("p (k t) -> p k t", t=128)[:, :, 0:L],
                       gate[:, :, t0:t0 + L], ALU.mult, (pT.b, gate.b), (hmT.b,))
                    if h == 0:
                        ck()
                    for dt in range(4):
                        MM(pC[:, :], ktb[0:L, dt * 128:(dt + 1) * 128], vbf[0:L, :], True, True, (ktb.b, vbf.b), (pC.b,))
                        STT('dve', Cf(dt), Cf(dt), emc, pC[:, :], ALU.mult, ALU.add, (Cfb, gb.b, pC.b), (Cfb,))
                        CP('act', Cb_(dt), Cf(dt), (Cfb,), (Cbb,))
                        MM(pS[:, 132 + dt:133 + dt], ktb[0:L, dt * 128:(dt + 1) * 128], ones_b[0:L, 0:1], True, True, (ktb.b, ones_b.b), (pS.b,))
                        STT('dve', nf(dt), nf(dt), emc, pS[:, 132 + dt:133 + dt], ALU.mult, ALU.add, (nfb, gb.b, pS.b), (nfb,))
                        CP('dve', nb_(dt), nf(dt), (nfb,), (nbb,))
                    if h == 0:
                        ck()
                    if kind == 'S' and not os.environ.get('K_NOSTORE'):
                        for et in range(4):
                            p = PG()
                            for dt in range(4):
                                TR(p[:, dt * 128:(dt + 1) * 128], Cf(dt)[:, et * 128:(et + 1) * 128], ident_f[:, :], (Cfb, ident_f.b), (p.b,))
                            CP('act', Cst[:, et, :], p[:, :], (p.b,), (Cst.b,))
                        S.dma('sp', o_sC[l, j, h].rearrange("(et p) d -> p et d", p=128), Cst[:, :, :], (Cst.b,), ())
                        if h == 0:
                            ck()
            if kind == 'S':
                for sq in range(4):
                    p = PG()
                    TR(p[0:16, 0:128], nS[:, sq, :], ident_f[:, :], (nS.b, ident_f.b), (p.b,))
                    CP('act', stg[0:16, 0:128], p[0:16, 0:128], (p.b,), (stg.b,))
                    S.dma('sp', o_sn[l, sq], stg[0:16, 0:128], (stg.b,), ())

            ck()
            for dg in range(2):
                wm0 = WL('ml', w_ml[l], 0, dg * 512, 512)
                wm1 = WL('ml', w_ml[l], 1024, dg * 512, 512)
                wgm = WL('in', Win, 0, C_GM + dg * 512, 512)
                for di in range(4):
                    dt = dg * 4 + di
                    pb = PG()
                    for r in range(16):
                        wt = wm0 if r < 8 else wm1
                        MM(pb[:, 0:ntok], wt[:, r % 8, di * 128:(di + 1) * 128], hmT[:, r, 0:ntok], r == 0, r == 15, (wt.b, hmT.b), (pb.b,))
                    pgm = fm_proj(wgm, di * 128)
                    ACT(sg[:, 0:ntok], pgm[:, 0:ntok], AF.Sigmoid, (pgm.b, bcol.b), (sg.b,), bias=bcol[:, 112 + dt:113 + dt])
                    TT('dve', tmpa[:, 0:ntok], pb[:, 0:ntok], sg[:, 0:ntok], ALU.mult, (pb.b, sg.b), (tmpa.b,))
                    TT('dve', ybf[:, dt, 0:ntok], tmpa[:, 0:ntok], yacc[:, dt, 0:ntok], ALU.add, (tmpa.b, yacc.b), (ybf.b,))
            wo0 = WL('out', w_out[l], 0, 0, 512)
            wo1 = WL('out', w_out[l], 0, 512, 512)
            for st in range(nsub):
                r = min(128, ntok - st * 128)
                pa = PG(); pb = PG()
                for kc in range(8):
                    MM(pa[0:r, :], ybf[:, kc, st * 128:st * 128 + r], wo0[:, kc, :], kc == 0, kc == 7, (ybf.b, wo0.b), (pa.b,))
                for kc in range(8):
                    MM(pb[0:r, :], ybf[:, kc, st * 128:st * 128 + r], wo1[:, kc, :], kc == 0, kc == 7, (ybf.b, wo1.b), (pb.b,))
                ACT(junk[0:r, 0:512], pa[0:r, :], AF.Square, (pa.b,), (junk.b, sso.b), accum=sso[0:r, 0:1])
                ACT(junk[0:r, 512:1024], pb[0:r, :], AF.Square, (pb.b,), (junk.b, sso.b), accum=sso[0:r, 1:2])
                TT('dve', sso[0:r, 2:3], sso[0:r, 0:1], sso[0:r, 1:2], ALU.add, (sso.b,), (sso.b,))
                TS('dve', sso[0:r, 2:3], sso[0:r, 2:3], 1.0 / D, EPS, ALU.mult, ALU.add, (sso.b,), (sso.b,))
                RSQ(sso[0:r, 3:4], sso[0:r, 2:3], sso.b)
                yt = yst[0]
                STT('dve', yt[0:r, 0:512], pa[0:r, :], sso[0:r, 3:4], gpost[0:r, 0:512], ALU.mult, ALU.mult, (pa.b, sso.b, gpost.b), (yt.b,))
                STT('dve', yt[0:r, 512:1024], pb[0:r, :], sso[0:r, 3:4], gpost[0:r, 512:1024], ALU.mult, ALU.mult, (pb.b, sso.b, gpost.b), (yt.b,))
                S.dma('sp', xres[0][0:r, :], xsrc[st * 128:st * 128 + r, :], xb, (xres[0].b,))
                TT('pool', yt[0:r, :], yt[0:r, :], xres[0][0:r, :], ALU.add, (yt.b, xres[0].b), (yt.b,))
                S.dma('sp', ydst[st * 128:st * 128 + r, :], yt[0:r, :], (yt.b,), yb)

        try:
            tcount = [0]
            dS, dL = Buf("x1s"), Buf("x1l")
            dP = [Buf("x1p%d" % i) for i in range(NPT)]
            for l in range(2):
                ck()
                layer_params(l)
                ck()
                srcS, srcL, srcP = (xs, meta, xp) if l == 0 else (x1s, x1l, x1p)
                dstS, dstL, dstP = (x1s, x1l, x1p) if l == 0 else (ys, ydump, yp)
                tile(l, 'S', tcount[0], srcS, dstS, () if l == 0 else (dS,), (dS,) if l == 0 else ()); tcount[0] += 1
                MS('dve', CPs[:, :, :], 0.0, (CPs.b,))
                MS('dve', nP[:, :], 0.0, (nP.b,)); MS('dve', nPb[:, :], 0.0, (nPb.b,)); MS('dve', mP[:, :], 0.0, (mP.b,))
                tile(l, 'LEAD', tcount[0], srcL, dstL, () if l == 0 else (dL,), (dL,) if l == 0 else ()); tcount[0] += 1
                for pi in range(NPT):
                    tile(l, 'P', tcount[0], srcP[pi * 512:(pi + 1) * 512, :], dstP[pi * 512:(pi + 1) * 512, :],
                         () if l == 0 else (dP[pi],), (dP[pi],) if l == 0 else ()); tcount[0] += 1
                for c in range(8):
                    p = PG()
                    TR(p[0:30, 0:128], uhP[:, c, :], ident_f[:, :], (uhP.b, ident_f.b), (p.b,))
                    CP('act', stg[0:30, c * 128:(c + 1) * 128], p[0:30, 0:128], (p.b,), (stg.b,))
                S.dma('sp', o_pconv[l], stg[0:30, :], (stg.b,), ())
                for h in range(NH):
                    for i in range(8):
                        qk_off = (0 if i < 4 else 2048) + h * 512 + (i % 4) * 128
                        p = PG()
                        TR(p[0:3, 0:128], qkhP[:, h, i, :], ident_f[:, :], (qkhP.b, ident_f.b), (p.b,))
                        CP('act', stg[0:3, 0:128], p[0:3, 0:128], (p.b,), (stg.b,))
                        S.dma('sp', o_pqk[l][:, qk_off:qk_off + 128], stg[0:3, 0:128], (stg.b,), ())
                    for et in range(4):
                        p = PG()
                        for dt in range(4):
                            TR(p[:, dt * 128:(dt + 1) * 128], CPs[:, h * 4 + dt, et * 128:(et + 1) * 128], ident_f[:, :], (CPs.b, ident_f.b), (p.b,))
                        CP('act', Cst[:, et, :], p[:, :], (p.b,), (Cst.b,))
                    S.dma('sp', o_pC[l, h].rearrange("(et p) d -> p et d", p=128), Cst[:, :, :], (Cst.b,), ())
                p = PG()
                TR(p[0:16, 0:128], nP[:, :], ident_f[:, :], (nP.b, ident_f.b), (p.b,))
                CP('act', stg[0:16, 0:128], p[0:16, 0:128], (p.b,), (stg.b,))
                S.dma('sp', o_pn[l], stg[0:16, 0:128], (stg.b,), ())
                S.dma('sp', o_pm[l].rearrange("(h o) -> h o", o=1), mP[:, :], (mP.b,), ())
        except Stop:
            pass
        for _ in range(int(os.environ.get('K_PAD', '0'))):
            if os.environ.get('K_PADENG', 'dve') == 'pe':
                MM(pS[0:4, 300:304], ones_b[0:4, 0:4], ones_b[0:4, 0:4], True, True, (ones_b.b,), (pS.b,))
            elif os.environ.get('K_PADENG', 'dve') == 'act':
                CP('act', sc4[:, 7:8], sc4[:, 6:7], (sc4.b,), (sc4.b,))
            else:
                MS(os.environ.get('K_PADENG', 'dve'), sc4[:, 7:8], 0.0, (sc4.b,))
        S.finish()
        S.emit()
        import os
        if os.environ.get('K_DEBUG'):
            print('CNT', S.cnt, S.dcnt, {k: len(v) for k, v in S.prog.items()})
    return nc


_CACHE = {}


def _consts():
    c = np.zeros((2, 128, 128), np.float32)
    c[0] = np.eye(128, dtype=np.float32)
    c[1] = np.triu(np.ones((128, 128), np.float32))
    return c


def run(inputs, NPT):
    f = lambda a: np.ascontiguousarray(np.asarray(a, dtype=np.float32))
    I = {k: f(v) for k, v in inputs.items()}
    if NPT not in _CACHE:
        _CACHE[NPT] = build(NPT)
    nc = _CACHE[NPT]
    NP = NPT * 512
    in_maps = []
    for c in range(8):
        b = c // 4
        sq = slice(4 * c, 4 * c + 4)
        m = {
            "xs": I["x_sample"][sq].reshape(256, D), "xp": I["x_prompt"][b, :NP], "meta": I["meta_tokens"],
            "st_conv": I["state_conv"][:, sq].reshape(2, 120, D), "st_qk": I["state_qk_conv"][:, sq].reshape(2, 12, 4096),
            "st_C": I["state_C"][:, sq], "st_n": I["state_n"][:, sq].reshape(2, 4, 16, 128), "st_m": I["state_m"][:, sq],
            "w_in": I["w_in"], "b_in": I["b_in"], "w_dw": I["w_dw"], "b_dw": I["b_dw"].reshape(2, 8, 128),
            "ln_g": I["ln_g"].reshape(2, 8, 128), "ln_b": I["ln_b"].reshape(2, 8, 128),
            "w_qkc": I["w_qk_conv"].reshape(2, 4, 32, 128), "b_qkc": I["b_qk_conv"].reshape(2, 32, 128), "f_bias": I["f_bias"],
            "ml_norm": I["ml_norm"].reshape(2, 16, 128), "w_co": I["w_conv_out"], "w_ml": I["w_ml_out"], "w_out": I["w_out"],
            "norm_pre": I["norm_pre"], "norm_post": I["norm_post"], "cst": _consts(),
        }
        in_maps.append({k: np.ascontiguousarray(v) for k, v in m.items()})
    res = run_bass_kernel_spmd(nc, in_maps, core_ids=list(range(8)))
    R = res.results
    g = lambda c, k: np.asarray(R[c][k], dtype=np.float32)
    y_prompt = np.stack([g(0, "yp"), g(4, "yp")], 0)
    y_sample = np.concatenate([g(c, "ys").reshape(4, 64, D) for c in range(8)], 0)
    pc = lambda k, shp: np.stack([g(0, k), g(4, k)], 1).reshape(shp)
    sc = lambda k, shp: np.concatenate([g(c, k).reshape((2, 4) + shp) for c in range(8)], 1)
    return (y_prompt, y_sample,
            pc("o_pconv", (2, 2, 30, D)), pc("o_pqk", (2, 2, 3, 4096)), pc("o_pC", (2, 2, 4, DH, DH)),
            pc("o_pn", (2, 2, 4, DH)), pc("o_pm", (2, 2, 4)),
            sc("o_sconv", (30, D)), sc("o_sqk", (3, 4096)), sc("o_sC", (4, DH, DH)), sc("o_sn", (4, DH)), sc("o_sm", (4,)))


def kernel(**inputs):
    return run(inputs, 16)
```

```python
import math
from contextlib import ExitStack
import numpy as np
import concourse.bass as bass
import concourse.mybir as mybir
from concourse.bass_utils import run_bass_kernel_spmd

F32 = mybir.dt.float32
BF16 = mybir.dt.bfloat16
ALU = mybir.AluOpType
AF = mybir.ActivationFunctionType
AX = mybir.AxisListType

D = 1024
DIN = 15368
DML = 2048
NH = 4
DH = 512
EPS = 1e-6
C_A, C_GL, C_ZC, C_Q, C_K, C_V, C_O, C_ZM, C_I, C_F, C_GC, C_GM = (
    0, 1024, 2048, 3072, 5120, 7168, 9216, 11264, 13312, 13316, 13320, 14344)
SELF_SYNC = True


class Stop(Exception):
    pass


class Buf:
    def __init__(s, name):
        s.name = name
        s.w = None
        s.r = {}


class TB:
    def __init__(s, t, name):
        s.t = t
        s.b = Buf(name)

    def __getitem__(s, k):
        return s.t[k]


class View:
    def __init__(s, tb, flat, off):
        s.flat = flat
        s.off = off
        s.b = tb.b

    def __getitem__(s, k):
        r, c = k
        c0 = 0 if c.start is None else c.start
        c1 = 512 if c.stop is None else c.stop
        return s.flat[r, s.off + c0:s.off + c1]


class Sch:
    def __init__(s, nc, es):
        s.nc = nc
        s.names = ['pe', 'act', 'dve', 'pool', 'sp']
        s.prog = {n: [] for n in s.names}
        s.sem = {n: es.enter_context(nc.semaphore('s_' + n)) for n in ['pe', 'act', 'dve', 'pool']}
        s.cnt = {n: 0 for n in s.sem}
        s.NQ = 8
        s.dsem = {q: [es.enter_context(nc.semaphore('d_%s%d' % (q, i))) for i in range(s.NQ)] for q in ['sp', 'pool']}
        s.dcnt = {q: 0 for q in s.dsem}
        s.dlast = {}
        s.seen = {n: {} for n in s.names}

    def _wait(s, n, ev):
        k, h, v = ev
        if k == n and (n == 'pe' or not SELF_SYNC):
            return
        if s.seen[n].get(k, 0) >= v:
            return
        s.seen[n][k] = v
        s.prog[n].append(lambda e, h=h, v=v: e.wait_ge(h, v))

    def _deps(s, n, R, W):
        evs = []
        for b in R:
            if b.w is not None:
                evs.append(b.w)
        for b in W:
            if b.w is not None:
                evs.append(b.w)
            evs.extend(b.r.values())
        for ev in evs:
            s._wait(n, ev)

    def _commit(s, ev, R, W):
        for b in R:
            o = b.r.get(ev[0])
            if o is None or o[2] < ev[2]:
                b.r[ev[0]] = ev
        for b in W:
            b.w = ev
            b.r = {}

    def op(s, n, fn, R=(), W=()):
        s._deps(n, R, W)
        s.cnt[n] += 1
        h = s.sem[n]
        s.prog[n].append(lambda e, fn=fn, h=h: fn(e).then_inc(h, 1))
        s._commit((n, h, s.cnt[n]), R, W)

    def dma(s, q, out, in_, R=(), W=(), **kw):
        i = s.dcnt[q]
        s.dcnt[q] += 1
        slot = i % s.NQ
        v = 16 * (i // s.NQ + 1)
        h = s.dsem[q][slot]
        key = (q, slot)
        if v > 16:
            s._wait(q, (key, h, v - 16))
        s._deps(q, R, W)
        s.prog[q].append(lambda e, out=out, in_=in_, h=h, kw=kw: e.dma_start(out=out, in_=in_, **kw).then_inc(h, 16))
        ev = (key, h, v)
        s.dlast[key] = ev
        s._commit(ev, R, W)

    def barrier(s, engs=('pe', 'act', 'dve')):
        for n in engs:
            for m in engs:
                if m != n and s.cnt[m] > 0:
                    s._wait(n, (m, s.sem[m], s.cnt[m]))

    def finish(s):
        for ev in s.dlast.values():
            s._wait('sp', ev)
        for n in s.sem:
            if s.cnt[n] > 0:
                s._wait('sp', (n, s.sem[n], s.cnt[n]))

    def emit(s):
        nc = s.nc
        with nc.Block() as block:
            @block.tensor
            def _(e):
                for f in s.prog['pe']:
                    f(e)

            @block.scalar
            def _(e):
                for f in s.prog['act']:
                    f(e)

            @block.vector
            def _(e):
                for f in s.prog['dve']:
                    f(e)

            @block.gpsimd
            def _(e):
                for f in s.prog['pool']:
                    f(e)

            @block.sync
            def _(e):
                for f in s.prog['sp']:
                    f(e)


def build(NPT):
    nc = bass.Bass("TRN2", target_bir_lowering=False)
    NP = NPT * 512

    def din(name, shape):
        return nc.dram_tensor(name, list(shape), F32, kind="ExternalInput").ap()

    def dout(name, shape):
        return nc.dram_tensor(name, list(shape), F32, kind="ExternalOutput").ap()

    def dint(name, shape):
        return nc.dram_tensor(name, list(shape), F32).ap()

    xs = din("xs", [256, D]); xp = din("xp", [NP, D]); meta = din("meta", [16, D])
    st_conv = din("st_conv", [2, 120, D]); st_qk = din("st_qk", [2, 12, 4096])
    st_C = din("st_C", [2, 4, 4, DH, DH]); st_n = din("st_n", [2, 4, 16, 128]); st_m = din("st_m", [2, 4, 4])
    w_in = din("w_in", [2, D, DIN]); b_in = din("b_in", [2, DIN])
    w_dw = din("w_dw", [2, 31, D]); b_dw = din("b_dw", [2, 8, 128]); ln_g = din("ln_g", [2, 8, 128]); ln_b = din("ln_b", [2, 8, 128])
    w_qkc = din("w_qkc", [2, 4, 32, 128]); b_qkc = din("b_qkc", [2, 32, 128]); f_bias = din("f_bias", [2, 4])
    ml_norm = din("ml_norm", [2, 16, 128]); w_co = din("w_co", [2, D, D]); w_ml = din("w_ml", [2, DML, D]); w_out = din("w_out", [2, D, D])
    norm_pre = din("norm_pre", [2, D]); norm_post = din("norm_post", [2, D])
    cst = din("cst", [2, 128, 128])
    ys = dout("ys", [256, D]); yp = dout("yp", [NP, D])
    o_sconv = dout("o_sconv", [2, 120, D]); o_sqk = dout("o_sqk", [2, 12, 4096])
    o_sC = dout("o_sC", [2, 4, 4, DH, DH]); o_sn = dout("o_sn", [2, 4, 16, 128]); o_sm = dout("o_sm", [2, 4, 4])
    o_pconv = dout("o_pconv", [2, 30, D]); o_pqk = dout("o_pqk", [2, 3, 4096])
    o_pC = dout("o_pC", [2, 4, DH, DH]); o_pn = dout("o_pn", [2, 16, 128]); o_pm = dout("o_pm", [2, 4])
    x1s = dint("x1s", [256, D]); x1p = dint("x1p", [NP, D]); x1l = dint("x1l", [16, D]); ydump = dint("ydump", [16, D])

    es = ExitStack()
    with es:
        S = Sch(nc, es)

        def sb(name, shape, dt=F32):
            return TB(es.enter_context(nc.sbuf_tensor(name, list(shape), dt)), name)

        def pst(name, shape, dt=F32):
            return TB(es.enter_context(nc.psum_tensor(name, list(shape), dt)), name)

        def MM(out, lhsT, rhs, st, sp, R, W):
            S.op('pe', lambda e: e.matmul(out, lhsT, rhs, start=st, stop=sp), R, W)

        def TR(out, in_, ident, R, W):
            S.op('pe', lambda e: e.transpose(out, in_, ident), R, W)

        def ACT(out, in_, func, R, W, bias=0.0, scale=1.0, accum=None):
            if accum is None:
                S.op('act', lambda e: e.activation(out, in_, func, bias=bias, scale=scale), R, W)
            else:
                S.op('act', lambda e: e.activation(out, in_, func, bias=bias, scale=scale, accum_out=accum), R, W)

        def TT(eng, out, in0, in1, op, R, W):
            S.op(eng, lambda e: e.tensor_tensor(out, in0, in1, op), R, W)

        def TS(eng, out, in0, s1, s2, op0, op1, R, W):
            if s2 is None:
                S.op(eng, lambda e: e.tensor_scalar(out, in0, s1, None, op0), R, W)
            else:
                S.op(eng, lambda e: e.tensor_scalar(out, in0, s1, s2, op0, op1), R, W)

        def STT(eng, out, in0, sc, in1, op0, op1, R, W):
            S.op(eng, lambda e: e.scalar_tensor_tensor(out, in0, sc, in1, op0, op1), R, W)

        def CP(eng, out, in_, R, W):
            if eng == 'act':
                S.op('act', lambda e: e.copy(out, in_), R, W)
            else:
                S.op(eng, lambda e: e.tensor_copy(out, in_), R, W)

        def RSQ(out, in_, b):
            S.op('act', lambda e: e.activation(out, in_, AF.Sqrt), (b,), (b,))
            S.op('dve', lambda e: e.reciprocal(out, out), (b,), (b,))

        def MS(eng, ap, val, W):
            S.op(eng, lambda e: e.memset(ap, val), (), W)

        ident_f = sb("ident_f", [128, 128]); tri_f = sb("tri_f", [128, 128])
        ident_b = sb("ident_b", [128, 128], BF16)
        ones_b = sb("ones_b", [128, 128], BF16); ones_f = sb("ones_f", [128, 128])
        gpre = sb("gpre", [128, D]); gpost = sb("gpost", [128, D]); bvbc = sb("bvbc", [128, 512])
        bcol = sb("bcol", [128, 120]); wdw = sb("wdw", [128, 8, 32]); misc = sb("misc", [128, 24])
        wqk = sb("wqk", [128, 128]); bqk = sb("bqk", [128, 32]); mln = sb("mln", [128, 16])
        bi_c = sb("bi_c", [4, 1]); bf_c = sb("bf_c", [4, 1]); fb_c = sb("fb_c", [4, 1]); nbfz = sb("nbfz", [4, 1])
        wif = sb("wif", [128, 8, 8], BF16)
        stg = sb("stg", [128, D])
        xres = [sb("xres0", [128, D])] * 4
        wstg = sb("wstg", [128, 8, 512])
        hn = sb("hn", [128, D], BF16)
        hT = [sb("hT0", [128, 8, 512], BF16)]
        ss = sb("ss", [128, 8]); junk = hn
        NWB = 3
        wb = [sb("wb%d" % i, [128, 8, 512], BF16) for i in range(NWB)]
        ubufs = [sb("ubuf%d" % i, [128, 544]) for i in range(2)]; sg = sb("sg", [128, 512])
        ucv = sb("ucv", [128, 8, 512]); ubf = sb("ubf", [128, 512], BF16); usq = sb("usq", [128, 512], BF16)
        tmpa = sb("tmpa", [128, 512])
        ucb = sb("ucb", [128, 8, 512], BF16)
        Cst = sb("Cst", [128, 4, 512])
        _cf = Cst[:, :, :].rearrange("p a b -> p (a b)")
        mean = View(Cst, _cf, 0); rstd = View(Cst, _cf, 512); nmr = View(Cst, _cf, 1024); tmpb = View(Cst, _cf, 1536)
        yacc = ucv; ybf = ucb
        qkbufs = [sb("qkbuf%d" % i, [128, 520]) for i in range(2)]; qkacc = sb("qkacc", [128, 512])
        qT = sb("qT", [128, 4, 512], BF16); kT = sb("kT", [128, 4, 512], BF16)
        gate = sb("gate", [128, 4, 512], BF16); so = tmpa
        hmT = sb("hmT", [128, 16, 512], BF16)
        stb4 = sb("stb4", [128, 4, 128], BF16)
        hnb = sb("hnb", [128, 512], BF16)
        irow = sb("irow", [4, 512]); lfrow = sb("lfrow", [4, 512]); cum = sb("cum", [4, 128]); grow = sb("grow", [4, 128])
        arow = sb("arow", [4, 128]); flrow = sb("flrow", [4, 128]); onesrow = sb("onesrow", [4, 128])
        sc4 = sb("sc4", [4, 8]); dg4 = sb("dg4", [4, 4])
        gb = sb("gb", [128, 4, 12])
        sc1 = sb("sc1", [128, 8])
        uhP = sb("uhP", [128, 8, 30]); qkhP = sb("qkhP", [128, 4, 8, 3])
        CPs = sb("CPs", [128, 16, 512]); Cwb = sb("Cwb", [128, 4, 512], BF16)
        nP = sb("nP", [128, 16]); nPb = sb("nPb", [128, 16], BF16); mP = sb("mP", [4, 1])
        nS = sb("nS", [128, 4, 16]); nSb = sb("nSb", [128, 4, 16], BF16); mS = sb("mS", [4, 4]); mSo = sb("mSo", [4, 4])
        yst = [sb("yst0", [128, D])]
        sso = sb("sso", [128, 4])
        vbf4 = yst[0][:, :].bitcast(BF16).rearrange("p (j e) -> p j e", e=512)

        pg = [pst("pg%d" % i, [128, 512]) for i in range(4)]
        pT = pst("pT", [128, 1024], BF16)
        pN = pst("pN", [128, 512])
        pC = pst("pC", [128, 512])
        pS = pst("pS", [128, 512])
        pgi = [0]

        def PG():
            pgi[0] += 1
            return pg[pgi[0] % 4]

        wbi = [0]

        NBLK = 80
        wsc = nc.dram_tensor("wsc", [NBLK, 128, 8 * 512], BF16).ap()
        wmap = {}
        wbufs = {}
        cur_l = [0]

        def WL(tag, dram2d, r0, c0, ncols):
            t = wb[wbi[0] % NWB]
            wbi[0] += 1
            key = (cur_l[0], tag, r0, c0)
            if key not in wmap:
                idx = len(wmap)
                assert idx < NBLK
                wmap[key] = idx
                wbufs[key] = Buf("wsc%d" % idx)
                src = dram2d[r0:r0 + 1024, c0:c0 + ncols].rearrange("(kc p) n -> p kc n", p=128)
                S.dma('sp', wstg[:, :, 0:ncols], src, (), (wstg.b,))
                CP('pool', t[:, :, 0:ncols], wstg[:, :, 0:ncols], (wstg.b,), (t.b,))
                S.dma('sp', wsc[idx].rearrange("p (k n) -> p k n", n=512), t[:, :, :], (t.b,), (wbufs[key],))
            else:
                idx = wmap[key]
                S.dma('sp', t[:, :, :], wsc[idx].rearrange("p (k n) -> p k n", n=512), (wbufs[key],), (t.b,))
            return t

        S.dma('sp', ident_f[:, :], cst[0], (), (ident_f.b,))
        S.dma('sp', tri_f[:, :], cst[1], (), (tri_f.b,))
        CP('dve', ident_b[:, :], ident_f[:, :], (ident_f.b,), (ident_b.b,))
        MS('dve', ones_b[:, :], 1.0, (ones_b.b,))
        MS('dve', ones_f[:, :], 1.0, (ones_f.b,))
        MS('dve', onesrow[:, :], 1.0, (onesrow.b,))

        def load_cols(dst_ap, dstb, src_rows_ap, nrows):
            S.dma('sp', stg[0:nrows, 0:128], src_rows_ap, (), (stg.b,))
            p = PG()
            TR(p[:, 0:nrows], stg[0:nrows, 0:128], ident_f[0:nrows, 0:nrows], (stg.b, ident_f.b), (p.b,))
            CP('dve', dst_ap, p[:, 0:nrows], (p.b,), (dstb,))

        def layer_params(l):
            S.dma('sp', gpre[:, :], norm_pre[l].partition_broadcast(128), (), (gpre.b,))
            S.dma('sp', gpost[:, :], norm_post[l].partition_broadcast(128), (), (gpost.b,))
            load_cols(bcol[:, 0:104], bcol.b, b_in[l, 0:13312].rearrange("(r c) -> r c", c=128), 104)
            load_cols(bcol[:, 104:120], bcol.b, b_in[l, C_GC:C_GC + 2048].rearrange("(r c) -> r c", c=128), 16)
            for c in range(8):
                load_cols(wdw[:, c, 0:31], wdw.b, w_dw[l, :, c * 128:(c + 1) * 128], 31)
            load_cols(misc[:, 0:8], misc.b, b_dw[l], 8)
            load_cols(misc[:, 8:16], misc.b, ln_g[l], 8)
            load_cols(misc[:, 16:24], misc.b, ln_b[l], 8)
            load_cols(wqk[:, :], wqk.b, w_qkc[l].rearrange("t c p -> (t c) p"), 128)
            load_cols(bqk[:, :], bqk.b, b_qkc[l], 32)
            load_cols(mln[:, :], mln.b, ml_norm[l], 16)
            S.dma('sp', bi_c[:, :], b_in[l, C_I:C_I + 4].rearrange("(h o) -> h o", o=1), (), (bi_c.b,))
            S.dma('sp', bf_c[:, :], b_in[l, C_F:C_F + 4].rearrange("(h o) -> h o", o=1), (), (bf_c.b,))
            S.dma('sp', fb_c[:, :], f_bias[l].rearrange("(h o) -> h o", o=1), (), (fb_c.b,))
            STT('dve', nbfz[:, :], bf_c[:, :], -1.0, fb_c[:, :], ALU.mult, ALU.subtract, (bf_c.b, fb_c.b), (nbfz.b,))
            S.dma('sp', stg[:, 0:64].rearrange("p (k n) -> p k n", n=8), w_in[l, :, C_I:C_I + 8].rearrange("(kc p) n -> p kc n", p=128), (), (stg.b,))
            CP('dve', wif[:, :, :], stg[:, 0:64].rearrange("p (k n) -> p k n", n=8), (stg.b,), (wif.b,))

        LNSC = math.log(DH ** -0.5)
        import os
        kstop = int(os.environ.get('K_STOP', '-1'))
        ckc = [0]

        def ck():
            if ckc[0] == kstop:
                raise Stop()
            ckc[0] += 1

        def tile(l, kind, tix, xsrc, ydst, xb, yb):
            if kind == 'S':
                nseg, SL, L = 4, 64, 64
            elif kind == 'LEAD':
                nseg, SL, L = 1, 16, 16
            else:
                nseg, SL, L = 1, 512, 128
            ntok = nseg * SL
            nch = ntok // L
            nsub = (ntok + 127) // 128
            h_T = hT[0]
            cur_l[0] = l
            Win = w_in[l]

            def segv(ap2d, w):
                return ap2d.rearrange("p (s w) -> p s w", w=w)

            for st in range(nsub):
                r = min(128, ntok - st * 128)
                xt = xres[st]
                S.dma('sp', xt[0:r, :], xsrc[st * 128:st * 128 + r, :], xb, (xt.b,))
                ACT(junk[0:r, :], xt[0:r, :], AF.Square, (xt.b,), (junk.b, ss.b), accum=ss[0:r, 0:1])
                TS('dve', ss[0:r, 1:2], ss[0:r, 0:1], 1.0 / D, EPS, ALU.mult, ALU.add, (ss.b,), (ss.b,))
                RSQ(ss[0:r, 2:3], ss[0:r, 1:2], ss.b)
                STT('dve', hn[0:r, :], xt[0:r, :], ss[0:r, 2:3], gpre[0:r, :], ALU.mult, ALU.mult, (xt.b, ss.b, gpre.b), (hn.b,))
                for kc in range(8):
                    TR(pT[:, kc * 128:kc * 128 + r], hn[0:r, kc * 128:(kc + 1) * 128], ident_b[0:r, 0:r], (hn.b, ident_b.b), (pT.b,))
                CP('act', h_T[:, :, st * 128:st * 128 + r], pT[:, :].rearrange("p (k t) -> p k t", t=128)[:, :, 0:r], (pT.b,), (h_T.b,))

            ck()

            def fm_proj(wt, coff, R_extra=()):
                p = PG()
                for kc in range(8):
                    MM(p[:, 0:ntok], wt[:, kc, coff:coff + 128], h_T[:, kc, 0:ntok], kc == 0, kc == 7, (wt.b, h_T.b), (p.b,))
                return p

            W30 = 30 + SL
            if kind == 'S':
                S.dma('sp', stg[0:120, :], st_conv[l], (), (stg.b,))
            CstF = Cst[:, :, :].rearrange("p a b -> p (a b)")
            for cg in range(2):
                wa = WL('in', Win, 0, C_A + cg * 512, 512)
                wg = WL('in', Win, 0, C_GL + cg * 512, 512)
                for ci in range(4):
                    c = cg * 4 + ci
                    ub = ubufs[c % 2]
                    uvc = segv(ub[:, 0:nseg * W30], W30)
                    if kind == 'S':
                        p = PG()
                        TR(p[:, 0:120], stg[0:120, c * 128:(c + 1) * 128], ident_f[0:120, 0:120], (stg.b, ident_f.b), (p.b,))
                        CP('dve', uvc[:, :, 0:30], segv(p[:, 0:120], 30), (p.b,), (ub.b,))
                    elif kind == 'LEAD':
                        MS('dve', ub[:, 0:30], 0.0, (ub.b,))
                    else:
                        CP('dve', ub[:, 0:30], uhP[:, c, :], (uhP.b,), (ub.b,))
                    pa = fm_proj(wa, ci * 128)
                    pgl = fm_proj(wg, ci * 128)
                    ACT(sg[:, 0:ntok], pgl[:, 0:ntok], AF.Sigmoid, (pgl.b, bcol.b), (sg.b,), bias=bcol[:, 8 + c:9 + c])
                    STT('dve', uvc[:, :, 30:W30], segv(pa[:, 0:ntok], SL), bcol[:, c:c + 1], segv(sg[:, 0:ntok], SL),
                        ALU.add, ALU.mult, (pa.b, sg.b, bcol.b), (ub.b,))
                    eng = 'dve'
                    acc = segv(ucv[:, c, 0:ntok], SL)
                    TS(eng, acc, uvc[:, :, 0:SL], wdw[:, c, 0:1], misc[:, c:c + 1], ALU.mult, ALU.add, (ub.b, wdw.b, misc.b), (ucv.b,))
                    for k in range(1, 31):
                        STT(eng, acc, uvc[:, :, k:k + SL], wdw[:, c, k:k + 1], acc, ALU.mult, ALU.add, (ub.b, wdw.b), (ucv.b,))
                    if kind == 'S':
                        p = PG()
                        CP('dve', segv(tmpa[:, 0:120], 30), uvc[:, :, SL:W30], (ub.b,), (tmpa.b,))
                        TR(p[0:120, 0:128], tmpa[:, 0:120], ident_f[:, :], (tmpa.b, ident_f.b), (p.b,))
                        CP('act', CstF[0:120, c * 128:(c + 1) * 128], p[0:120, 0:128], (p.b,), (Cst.b,))
                    else:
                        CP('dve', uhP[:, c, :], ub[:, SL:W30], (ub.b,), (uhP.b,))
            if kind == 'S':
                S.dma('sp', o_sconv[l], CstF[0:120, 0:1024], (Cst.b,), ())

            ck()
            p_sum = PG(); p_sq = PG()
            for c in range(8):
                CP('act', ubf[:, 0:ntok], ucv[:, c, 0:ntok], (ucv.b,), (ubf.b,))
                ACT(usq[:, 0:ntok], ucv[:, c, 0:ntok], AF.Square, (ucv.b,), (usq.b,))
                MM(p_sum[:, 0:ntok], ones_b[:, :], ubf[:, 0:ntok], c == 0, c == 7, (ones_b.b, ubf.b), (p_sum.b,))
                MM(p_sq[:, 0:ntok], ones_b[:, :], usq[:, 0:ntok], c == 0, c == 7, (ones_b.b, usq.b), (p_sq.b,))
            TS('dve', mean[:, 0:ntok], p_sum[:, 0:ntok], 1.0 / D, None, ALU.mult, None, (p_sum.b,), (mean.b,))
            TT('dve', tmpa[:, 0:ntok], mean[:, 0:ntok], mean[:, 0:ntok], ALU.mult, (mean.b,), (tmpa.b,))
            STT('dve', rstd[:, 0:ntok], p_sq[:, 0:ntok], 1.0 / D, tmpa[:, 0:ntok], ALU.mult, ALU.subtract, (p_sq.b, tmpa.b), (rstd.b,))
            TS('dve', rstd[:, 0:ntok], rstd[:, 0:ntok], EPS, None, ALU.add, None, (rstd.b,), (rstd.b,))
            RSQ(rstd[:, 0:ntok], rstd[:, 0:ntok], rstd.b)
            STT('dve', nmr[:, 0:ntok], mean[:, 0:ntok], -1.0, rstd[:, 0:ntok], ALU.mult, ALU.mult, (mean.b, rstd.b), (nmr.b,))
            for cg in range(2):
                wz = WL('in', Win, 0, C_ZC + cg * 512, 512)
                for ci in range(4):
                    c = cg * 4 + ci
                    pz = fm_proj(wz, ci * 128)
                    TT('dve', tmpa[:, 0:ntok], ucv[:, c, 0:ntok], rstd[:, 0:ntok], ALU.mult, (ucv.b, rstd.b), (tmpa.b,))
                    TT('dve', tmpa[:, 0:ntok], tmpa[:, 0:ntok], nmr[:, 0:ntok], ALU.add, (tmpa.b, nmr.b), (tmpa.b,))
                    ACT(tmpb[:, 0:ntok], tmpa[:, 0:ntok], AF.Silu, (tmpa.b, misc.b), (tmpb.b,), bias=misc[:, 16 + c:17 + c], scale=misc[:, 8 + c:9 + c])
                    ACT(sg[:, 0:ntok], pz[:, 0:ntok], AF.Silu, (pz.b, bcol.b), (sg.b,), bias=bcol[:, 16 + c:17 + c])
                    TT('dve', ucb[:, c, 0:ntok], tmpb[:, 0:ntok], sg[:, 0:ntok], ALU.mult, (tmpb.b, sg.b), (ucb.b,))
            for dg in range(2):
                wc = WL('co', w_co[l], 0, dg * 512, 512)
                wgc = WL('in', Win, 0, C_GC + dg * 512, 512)
                for di in range(4):
                    dt = dg * 4 + di
                    pb = PG()
                    for c in range(8):
                        MM(pb[:, 0:ntok], wc[:, c, di * 128:(di + 1) * 128], ucb[:, c, 0:ntok], c == 0, c == 7, (wc.b, ucb.b), (pb.b,))
                    pgc = fm_proj(wgc, di * 128)
                    ACT(sg[:, 0:ntok], pgc[:, 0:ntok], AF.Sigmoid, (pgc.b, bcol.b), (sg.b,), bias=bcol[:, 104 + dt:105 + dt])
                    TT('dve', yacc[:, dt, 0:ntok], pb[:, 0:ntok], sg[:, 0:ntok], ALU.mult, (pb.b, sg.b), (yacc.b,))

            ck()
            p_i = PG(); p_f = PG()
            for kc in range(8):
                MM(p_i[0:4, 0:ntok], wif[:, kc, 0:4], h_T[:, kc, 0:ntok], kc == 0, kc == 7, (wif.b, h_T.b), (p_i.b,))
            for kc in range(8):
                MM(p_f[0:4, 0:ntok], wif[:, kc, 4:8], h_T[:, kc, 0:ntok], kc == 0, kc == 7, (wif.b, h_T.b), (p_f.b,))
            ACT(irow[:, 0:ntok], p_i[0:4, 0:ntok], AF.Identity, (p_i.b, bi_c.b), (irow.b,), bias=bi_c[:, 0:1])
            ACT(lfrow[:, 0:ntok], p_f[0:4, 0:ntok], AF.Exp, (p_f.b, nbfz.b), (lfrow.b,), bias=nbfz[:, 0:1], scale=-1.0)
            ACT(lfrow[:, 0:ntok], lfrow[:, 0:ntok], AF.Ln, (lfrow.b,), (lfrow.b,), bias=1.0)
            if kind == 'S':
                S.dma('sp', mS[:, :], st_m[l].rearrange("s h -> h s"), (), (mS.b,), allow_slow_non_contiguous=True)
            for j in range(nch):
                t0 = j * L
                mcur = mS[:, j:j + 1] if kind == 'S' else mP[:, 0:1]
                mb = mS.b if kind == 'S' else mP.b
                mnew = mSo[:, j:j + 1] if kind == 'S' else mP[:, 0:1]
                mnb = mSo.b if kind == 'S' else mP.b
                S.op('dve', lambda e, t0=t0: e.tensor_tensor_scan(cum[:, 0:L], onesrow[:, 0:L], lfrow[:, t0:t0 + L], 0.0, ALU.mult, ALU.add),
                     (onesrow.b, lfrow.b), (cum.b,))
                TT('dve', grow[:, 0:L], irow[:, t0:t0 + L], cum[:, 0:L], ALU.add, (irow.b, cum.b), (grow.b,))
                S.op('dve', lambda e: e.reduce_max(sc4[:, 0:1], grow[:, 0:L], AX.X), (grow.b,), (sc4.b,))
                TT('dve', sc4[:, 1:2], sc4[:, 0:1], mcur, ALU.max, (sc4.b, mb), (sc4.b,))
                TS('dve', sc4[:, 2:3], sc4[:, 1:2], -1.0, LNSC, ALU.mult, ALU.add, (sc4.b,), (sc4.b,))
                TS('dve', sc4[:, 3:4], sc4[:, 1:2], -1.0, None, ALU.mult, None, (sc4.b,), (sc4.b,))
                ACT(arow[:, 0:L], grow[:, 0:L], AF.Exp, (grow.b, sc4.b), (arow.b,), bias=sc4[:, 2:3])
                ACT(flrow[:, 0:L], cum[:, 0:L], AF.Exp, (cum.b, sc4.b), (flrow.b,), bias=sc4[:, 3:4])
                ACT(sc4[:, 4:5], mcur, AF.Exp, (mb, sc4.b), (sc4.b,), bias=sc4[:, 3:4])
                TT('dve', mnew, sc4[:, 1:2], cum[:, L - 1:L], ALU.subtract, (sc4.b, cum.b), (mnb,))
                TS('dve', dg4[:, :], ident_f[0:4, 0:4], sc4[:, 4:5], None, ALU.mult, None, (ident_f.b, sc4.b), (dg4.b,))
                TR(pS[0:L, 256:260], arow[:, 0:L], ident_f[0:4, 0:4], (arow.b, ident_f.b), (pS.b,))
                TR(pS[0:L, 260:264], flrow[:, 0:L], ident_f[0:4, 0:4], (flrow.b, ident_f.b), (pS.b,))
                MM(pS[:, 264:268], ones_f[0:4, :], dg4[:, :], True, True, (ones_f.b, dg4.b), (pS.b,))
                CP('dve', gb[0:L, j, 0:8], pS[0:L, 256:264], (pS.b,), (gb.b,))
                CP('dve', gb[:, j, 8:12], pS[:, 264:268], (pS.b,), (gb.b,))
            if kind == 'S':
                S.dma('sp', o_sm[l].rearrange("s h -> h s"), mSo[:, :], (mSo.b,), (), allow_slow_non_contiguous=True)
                for sq in range(4):
                    load_cols(nS[:, sq, :], nS.b, st_n[l, sq], 16)

            ck()
            W3 = 3 + SL
            for h in range(NH):
                wq = WL('in', Win, 0, C_Q + h * 512, 512)
                wk = WL('in', Win, 0, C_K + h * 512, 512)
                S.dma('sp', bvbc[:, :], b_in[l, C_V + h * 512:C_V + (h + 1) * 512].partition_broadcast(128), (), (bvbc.b,))
                for i in range(8):
                    wt = wq if i < 4 else wk
                    cidx = (24 if i < 4 else 40) + h * 4 + (i % 4)
                    qc = (0 if i < 4 else 16) + h * 4 + (i % 4)
                    qk_off = (0 if i < 4 else 2048) + h * 512 + (i % 4) * 128
                    qb = qkbufs[i % 2]
                    qvi = segv(qb[:, 0:nseg * W3], W3)
                    if kind == 'S':
                        S.dma('sp', stg[0:12, 0:128], st_qk[l][:, qk_off:qk_off + 128], (), (stg.b,))
                        p = PG()
                        TR(p[:, 0:12], stg[0:12, 0:128], ident_f[0:12, 0:12], (stg.b, ident_f.b), (p.b,))
                        CP('dve', qvi[:, :, 0:3], segv(p[:, 0:12], 3), (p.b,), (qb.b,))
                    elif kind == 'LEAD':
                        MS('dve', qb[:, 0:3], 0.0, (qb.b,))
                    else:
                        CP('dve', qb[:, 0:3], qkhP[:, h, i, :], (qkhP.b,), (qb.b,))
                    p = fm_proj(wt, (i % 4) * 128)
                    ACT(qvi[:, :, 3:W3], segv(p[:, 0:ntok], SL), AF.Identity, (p.b, bcol.b), (qb.b,), bias=bcol[:, cidx:cidx + 1])
                    eng = 'dve'
                    acc = segv(qkacc[:, 0:ntok], SL)
                    TS(eng, acc, qvi[:, :, 0:SL], wqk[:, qc:qc + 1], bqk[:, qc:qc + 1], ALU.mult, ALU.add, (qb.b, wqk.b, bqk.b), (qkacc.b,))
                    for k in range(1, 4):
                        STT(eng, acc, qvi[:, :, k:k + SL], wqk[:, k * 32 + qc:k * 32 + qc + 1], acc, ALU.mult, ALU.add, (qb.b, wqk.b), (qkacc.b,))
                    dst = qT if i < 4 else kT
                    ACT(dst[:, i % 4, 0:ntok], qkacc[:, 0:ntok], AF.Silu, (qkacc.b,), (dst.b,))
                    if kind == 'S':
                        CP('dve', segv(tmpa[:, 0:12], 3), qvi[:, :, SL:W3], (qb.b,), (tmpa.b,))
                        p = PG()
                        TR(p[0:12, 0:128], tmpa[:, 0:12], ident_f[:, :], (tmpa.b, ident_f.b), (p.b,))
                        CP('act', stg[0:12, 128:256], p[0:12, 0:128], (p.b,), (stg.b,))
                        S.dma('sp', o_sqk[l][:, qk_off:qk_off + 128], stg[0:12, 128:256], (stg.b,), ())
                    else:
                        CP('dve', qkhP[:, h, i, :], qb[:, SL:W3], (qb.b,), (qkhP.b,))
                    if h == 0 and i in (0, 7):
                        ck()
                wo = WL('in', Win, 0, C_O + h * 512, 512)
                wz = WL('in', Win, 0, C_ZM + h * 512, 512)
                for et in range(4):
                    po = fm_proj(wo, et * 128)
                    pz = fm_proj(wz, et * 128)
                    ACT(so[:, 0:ntok], po[:, 0:ntok], AF.Sigmoid, (po.b, bcol.b), (so.b,), bias=bcol[:, 72 + h * 4 + et:73 + h * 4 + et])
                    ACT(sg[:, 0:ntok], pz[:, 0:ntok], AF.Silu, (pz.b, bcol.b), (sg.b,), bias=bcol[:, 88 + h * 4 + et:89 + h * 4 + et])
                    STT('dve', gate[:, et, 0:ntok], so[:, 0:ntok], mln[:, h * 4 + et:h * 4 + et + 1], sg[:, 0:ntok], ALU.mult, ALU.mult,
                        (so.b, sg.b, mln.b), (gate.b,))
                ck()
                wv = WL('in', Win, 0, C_V + h * 512, 512)
                ck()
                if kind != 'S':
                    CP('act', Cwb[:, :, :], CPs[:, h * 4:h * 4 + 4, :], (CPs.b,), (Cwb.b,))
                for j in range(nch):
                    t0 = j * L
                    acol = gb[0:L, j, h:h + 1]
                    emc = gb[:, j, 8 + h:9 + h]
                    pv = PG()
                    for kc in range(8):
                        MM(pv[0:L, :], h_T[:, kc, t0:t0 + L], wv[:, kc, :], kc == 0, kc == 7, (h_T.b, wv.b), (pv.b,))
                    TT('dve', vbf4[0:L, j, :], pv[0:L, :], bvbc[0:L, :], ALU.add, (pv.b, bvbc.b), (yst[0].b,))
                    for dt in range(4):
                        MM(pS[0:L, 0:L], kT[:, dt, t0:t0 + L], qT[:, dt, t0:t0 + L], dt == 0, dt == 3, (kT.b, qT.b), (pS.b,))
                    STT('dve', stb4[0:L, j, 0:L], pS[0:L, 0:L], acol, tri_f[0:L, 0:L], ALU.mult, ALU.mult, (pS.b, gb.b, tri_f.b), (stb4.b,))
                    for dt in range(4):
                        TR(pT[0:L, dt * 128:(dt + 1) * 128], kT[:, dt, t0:t0 + L], ident_b[:, :], (kT.b, ident_b.b), (pT.b,))
                    TS('dve', ucb[0:L, j, :], pT[0:L, 0:512], acol, None, ALU.mult, None, (pT.b, gb.b), (ucb.b,))
                    TS('dve', ucb[:, 4 + j, :].rearrange("p (d t) -> p d t", t=128)[:, :, 0:L], qT[:, :, t0:t0 + L], emc, None, ALU.mult, None,
                       (qT.b, gb.b), (ucb.b,))
                for j in range(int(os.environ.get('K_J0', '0')) if h == 0 else 0, nch):
                    t0 = j * L
                    hb = 0 if kind == 'S' else h * 4
                    Cf = lambda dt, hb=hb: CPs[:, hb + dt, :]
                    Cb_ = lambda dt: Cwb[:, dt, :]
                    Cfb, Cbb = CPs.b, Cwb.b
                    if kind == 'S':
                        S.dma('sp', Cst[:, :, :], st_C[l, j, h].rearrange("(et p) d -> p et d", p=128), (), (Cst.b,))
                        for dt in range(4):
                            p = PG()
                            for et in range(4):
                                TR(p[:, et * 128:(et + 1) * 128], Cst[:, et, dt * 128:(dt + 1) * 128], ident_f[:, :], (Cst.b, ident_f.b), (p.b,))
                            CP('dve', Cf(dt), p[:, :], (p.b,), (Cfb,))
                            CP('act', Cb_(dt), Cf(dt), (Cfb,), (Cbb,))
                        CP('dve', nSb[:, j, h * 4:h * 4 + 4], nS[:, j, h * 4:h * 4 + 4], (nS.b,), (nSb.b,))
                        nf = lambda dt, j=j, h=h: nS[:, j, h * 4 + dt:h * 4 + dt + 1]
                        nb_ = lambda dt, j=j, h=h: nSb[:, j, h * 4 + dt:h * 4 + dt + 1]
                        nfb, nbb = nS.b, nSb.b
                    else:
                        nf = lambda dt, h=h: nP[:, h * 4 + dt:h * 4 + dt + 1]
                        nb_ = lambda dt, h=h: nPb[:, h * 4 + dt:h * 4 + dt + 1]
                        nfb, nbb = nP.b, nPb.b
                    flcol = gb[0:L, j, 4 + h:5 + h]
                    emc = gb[:, j, 8 + h:9 + h]
                    stj = stb4[0:L, j, 0:L]
                    vj = vbf4[0:L, j, :]
                    qsj = lambda dt, j=j: ucb[:, 4 + j, dt * 128:dt * 128 + L]
                    MM(pN[0:L, :], stj, vj, True, False, (stb4.b, yst[0].b), (pN.b,))
                    for dt in range(4):
                        MM(pN[0:L, :], qsj(dt), Cb_(dt), False, dt == 3, (ucb.b, Cbb), (pN.b,))
                    MM(pS[0:L, 128:129], stj, ones_b[0:L, 0:1], True, False, (stb4.b, ones_b.b), (pS.b,))
                    for dt in range(4):
                        MM(pS[0:L, 128:129], qsj(dt), nb_(dt), False, dt == 3, (ucb.b, nbb), (pS.b,))
                    CP('dve', sc1[0:L, 6:7], pS[0:L, 128:129], (pS.b,), (sc1.b,))
                    for dt in range(4):
                        MM(pC[:, :], ucb[0:L, j, dt * 128:(dt + 1) * 128], vj, True, True, (ucb.b, yst[0].b), (pC.b,))
                        STT('dve', Cf(dt), Cf(dt), emc, pC[:, :], ALU.mult, ALU.add, (Cfb, gb.b, pC.b), (Cfb,))
                        CP('act', Cb_(dt), Cf(dt), (Cfb,), (Cbb,))
                        MM(pS[:, 132 + dt:133 + dt], ucb[0:L, j, dt * 128:(dt + 1) * 128], ones_b[0:L, 0:1], True, True, (ucb.b, ones_b.b), (pS.b,))
                        STT('dve', nf(dt), nf(dt), emc, pS[:, 132 + dt:133 + dt], ALU.mult, ALU.add, (nfb, gb.b, pS.b), (nfb,))
                        CP('dve', nb_(dt), nf(dt), (nfb,), (nbb,))
                    STT('dve', sc1[0:L, 7:8], sc1[0:L, 6:7], -1.0, sc1[0:L, 6:7], ALU.mult, ALU.max, (sc1.b,), (sc1.b,))
                    TT('dve', sc1[0:L, 0:1], sc1[0:L, 7:8], flcol, ALU.max, (sc1.b, gb.b), (sc1.b,))
                    S.op('dve', lambda e: e.reciprocal(sc1[0:L, 1:2], sc1[0:L, 0:1]), (sc1.b,), (sc1.b,))
                    ACT(junk[0:L, 0:512], pN[0:L, :], AF.Square, (pN.b, sc1.b), (junk.b, sc1.b), scale=sc1[0:L, 1:2], accum=sc1[0:L, 2:3])
                    TS('dve', sc1[0:L, 3:4], sc1[0:L, 2:3], 1.0 / DH, EPS, ALU.mult, ALU.add, (sc1.b,), (sc1.b,))
                    RSQ(sc1[0:L, 4:5], sc1[0:L, 3:4], sc1.b)
                    TT('dve', sc1[0:L, 5:6], sc1[0:L, 4:5], sc1[0:L, 1:2], ALU.mult, (sc1.b,), (sc1.b,))
                    ACT(hnb[0:L, :], pN[0:L, :], AF.Identity, (pN.b, sc1.b), (hnb.b,), scale=sc1[0:L, 5:6])
                    for et in range(4):
                        TR(pT[:, 512 + et * 128:512 + et * 128 + L], hnb[0:L, et * 128:(et + 1) * 128], ident_b[0:L, 0:L], (hnb.b, ident_b.b), (pT.b,))
                    TT('dve', hmT[:, h * 4:h * 4 + 4, t0:t0 + L], pT[:, 512:1024].rearrange("p (k t) -> p k t", t=128)[:, :, 0:L],
                       gate[:, :, t0:t0 + L], ALU.mult, (pT.b, gate.b), (hmT.b,))
                    if kind == 'S':
                        for et in range(4):
                            p = PG()
                            for dt in range(4):
                                TR(p[:, dt * 128:(dt + 1) * 128], Cf(dt)[:, et * 128:(et + 1) * 128], ident_f[:, :], (Cfb, ident_f.b), (p.b,))
                            CP('act', Cst[:, et, :], p[:, :], (p.b,), (Cst.b,))
                        S.dma('sp', o_sC[l, j, h].rearrange("(et p) d -> p et d", p=128), Cst[:, :, :], (Cst.b,), ())
            if kind == 'S':
                for sq in range(4):
                    p = PG()
                    TR(p[0:16, 0:128], nS[:, sq, :], ident_f[:, :], (nS.b, ident_f.b), (p.b,))
                    CP('act', stg[0:16, 0:128], p[0:16, 0:128], (p.b,), (stg.b,))
                    S.dma('sp', o_sn[l, sq], stg[0:16, 0:128], (stg.b,), ())

            ck()
            for dg in range(2):
                wm0 = WL('ml', w_ml[l], 0, dg * 512, 512)
                wm1 = WL('ml', w_ml[l], 1024, dg * 512, 512)
                wgm = WL('in', Win, 0, C_GM + dg * 512, 512)
                for di in range(4):
                    dt = dg * 4 + di
                    pb = PG()
                    for r in range(16):
                        wt = wm0 if r < 8 else wm1
                        MM(pb[:, 0:ntok], wt[:, r % 8, di * 128:(di + 1) * 128], hmT[:, r, 0:ntok], r == 0, r == 15, (wt.b, hmT.b), (pb.b,))
                    pgm = fm_proj(wgm, di * 128)
                    ACT(sg[:, 0:ntok], pgm[:, 0:ntok], AF.Sigmoid, (pgm.b, bcol.b), (sg.b,), bias=bcol[:, 112 + dt:113 + dt])
                    TT('dve', tmpa[:, 0:ntok], pb[:, 0:ntok], sg[:, 0:ntok], ALU.mult, (pb.b, sg.b), (tmpa.b,))
                    TT('dve', ybf[:, dt, 0:ntok], tmpa[:, 0:ntok], yacc[:, dt, 0:ntok], ALU.add, (tmpa.b, yacc.b), (ybf.b,))
            wo0 = WL('out', w_out[l], 0, 0, 512)
            wo1 = WL('out', w_out[l], 0, 512, 512)
            for st in range(nsub):
                r = min(128, ntok - st * 128)
                pa = PG(); pb = PG()
                for kc in range(8):
                    MM(pa[0:r, :], ybf[:, kc, st * 128:st * 128 + r], wo0[:, kc, :], kc == 0, kc == 7, (ybf.b, wo0.b), (pa.b,))
                for kc in range(8):
                    MM(pb[0:r, :], ybf[:, kc, st * 128:st * 128 + r], wo1[:, kc, :], kc == 0, kc == 7, (ybf.b, wo1.b), (pb.b,))
                ACT(junk[0:r, 0:512], pa[0:r, :], AF.Square, (pa.b,), (junk.b, sso.b), accum=sso[0:r, 0:1])
                ACT(junk[0:r, 512:1024], pb[0:r, :], AF.Square, (pb.b,), (junk.b, sso.b), accum=sso[0:r, 1:2])
                TT('dve', sso[0:r, 2:3], sso[0:r, 0:1], sso[0:r, 1:2], ALU.add, (sso.b,), (sso.b,))
                TS('dve', sso[0:r, 2:3], sso[0:r, 2:3], 1.0 / D, EPS, ALU.mult, ALU.add, (sso.b,), (sso.b,))
                RSQ(sso[0:r, 3:4], sso[0:r, 2:3], sso.b)
                yt = yst[0]
                STT('dve', yt[0:r, 0:512], pa[0:r, :], sso[0:r, 3:4], gpost[0:r, 0:512], ALU.mult, ALU.mult, (pa.b, sso.b, gpost.b), (yt.b,))
                STT('dve', yt[0:r, 512:1024], pb[0:r, :], sso[0:r, 3:4], gpost[0:r, 512:1024], ALU.mult, ALU.mult, (pb.b, sso.b, gpost.b), (yt.b,))
                S.dma('sp', xres[0][0:r, :], xsrc[st * 128:st * 128 + r, :], xb, (xres[0].b,))
                TT('pool', yt[0:r, :], yt[0:r, :], xres[0][0:r, :], ALU.add, (yt.b, xres[0].b), (yt.b,))
                S.dma('sp', ydst[st * 128:st * 128 + r, :], yt[0:r, :], (yt.b,), yb)

        try:
            tcount = [0]
            dS, dL = Buf("x1s"), Buf("x1l")
            dP = [Buf("x1p%d" % i) for i in range(NPT)]
            for l in range(2):
                ck()
                layer_params(l)
                ck()
                srcS, srcL, srcP = (xs, meta, xp) if l == 0 else (x1s, x1l, x1p)
                dstS, dstL, dstP = (x1s, x1l, x1p) if l == 0 else (ys, ydump, yp)
                tile(l, 'S', tcount[0], srcS, dstS, () if l == 0 else (dS,), (dS,) if l == 0 else ()); tcount[0] += 1
                MS('dve', CPs[:, :, :], 0.0, (CPs.b,))
                MS('dve', nP[:, :], 0.0, (nP.b,)); MS('dve', nPb[:, :], 0.0, (nPb.b,)); MS('dve', mP[:, :], 0.0, (mP.b,))
                tile(l, 'LEAD', tcount[0], srcL, dstL, () if l == 0 else (dL,), (dL,) if l == 0 else ()); tcount[0] += 1
                for pi in range(NPT):
                    tile(l, 'P', tcount[0], srcP[pi * 512:(pi + 1) * 512, :], dstP[pi * 512:(pi + 1) * 512, :],
                         () if l == 0 else (dP[pi],), (dP[pi],) if l == 0 else ()); tcount[0] += 1
                for c in range(8):
                    p = PG()
                    TR(p[0:30, 0:128], uhP[:, c, :], ident_f[:, :], (uhP.b, ident_f.b), (p.b,))
                    CP('act', stg[0:30, c * 128:(c + 1) * 128], p[0:30, 0:128], (p.b,), (stg.b,))
                S.dma('sp', o_pconv[l], stg[0:30, :], (stg.b,), ())
                for h in range(NH):
                    for i in range(8):
                        qk_off = (0 if i < 4 else 2048) + h * 512 + (i % 4) * 128
                        p = PG()
                        TR(p[0:3, 0:128], qkhP[:, h, i, :], ident_f[:, :], (qkhP.b, ident_f.b), (p.b,))
                        CP('act', stg[0:3, 0:128], p[0:3, 0:128], (p.b,), (stg.b,))
                        S.dma('sp', o_pqk[l][:, qk_off:qk_off + 128], stg[0:3, 0:128], (stg.b,), ())
                    for et in range(4):
                        p = PG()
                        for dt in range(4):
                            TR(p[:, dt * 128:(dt + 1) * 128], CPs[:, h * 4 + dt, et * 128:(et + 1) * 128], ident_f[:, :], (CPs.b, ident_f.b), (p.b,))
                        CP('act', Cst[:, et, :], p[:, :], (p.b,), (Cst.b,))
                    S.dma('sp', o_pC[l, h].rearrange("(et p) d -> p et d", p=128), Cst[:, :, :], (Cst.b,), ())
                p = PG()
                TR(p[0:16, 0:128], nP[:, :], ident_f[:, :], (nP.b, ident_f.b), (p.b,))
                CP('act', stg[0:16, 0:128], p[0:16, 0:128], (p.b,), (stg.b,))
                S.dma('sp', o_pn[l], stg[0:16, 0:128], (stg.b,), ())
                S.dma('sp', o_pm[l].rearrange("(h o) -> h o", o=1), mP[:, :], (mP.b,), ())
        except Stop:
            pass
        for _ in range(int(os.environ.get('K_PAD', '0'))):
            if os.environ.get('K_PADENG', 'dve') == 'pe':
                MM(pS[0:4, 300:304], ones_b[0:4, 0:4], ones_b[0:4, 0:4], True, True, (ones_b.b,), (pS.b,))
            elif os.environ.get('K_PADENG', 'dve') == 'act':
                CP('act', sc4[:, 7:8], sc4[:, 6:7], (sc4.b,), (sc4.b,))
            else:
                MS(os.environ.get('K_PADENG', 'dve'), sc4[:, 7:8], 0.0, (sc4.b,))
        S.finish()
        S.emit()
        import os
        if os.environ.get('K_DEBUG'):
            print('CNT', S.cnt, S.dcnt, {k: len(v) for k, v in S.prog.items()})
    return nc


_CACHE = {}


def _consts():
    c = np.zeros((2, 128, 128), np.float32)
    c[0] = np.eye(128, dtype=np.float32)
    c[1] = np.triu(np.ones((128, 128), np.float32))
    return c


def run(inputs, NPT):
    f = lambda a: np.ascontiguousarray(np.asarray(a, dtype=np.float32))
    I = {k: f(v) for k, v in inputs.items()}
    if NPT not in _CACHE:
        _CACHE[NPT] = build(NPT)
    nc = _CACHE[NPT]
    NP = NPT * 512
    in_maps = []
    for c in range(8):
        b = c // 4
        sq = slice(4 * c, 4 * c + 4)
        m = {
            "xs": I["x_sample"][sq].reshape(256, D), "xp": I["x_prompt"][b, :NP], "meta": I["meta_tokens"],
            "st_conv": I["state_conv"][:, sq].reshape(2, 120, D), "st_qk": I["state_qk_conv"][:, sq].reshape(2, 12, 4096),
            "st_C": I["state_C"][:, sq], "st_n": I["state_n"][:, sq].reshape(2, 4, 16, 128), "st_m": I["state_m"][:, sq],
            "w_in": I["w_in"], "b_in": I["b_in"], "w_dw": I["w_dw"], "b_dw": I["b_dw"].reshape(2, 8, 128),
            "ln_g": I["ln_g"].reshape(2, 8, 128), "ln_b": I["ln_b"].reshape(2, 8, 128),
            "w_qkc": I["w_qk_conv"].reshape(2, 4, 32, 128), "b_qkc": I["b_qk_conv"].reshape(2, 32, 128), "f_bias": I["f_bias"],
            "ml_norm": I["ml_norm"].reshape(2, 16, 128), "w_co": I["w_conv_out"], "w_ml": I["w_ml_out"], "w_out": I["w_out"],
            "norm_pre": I["norm_pre"], "norm_post": I["norm_post"], "cst": _consts(),
        }
        in_maps.append({k: np.ascontiguousarray(v) for k, v in m.items()})
    res = run_bass_kernel_spmd(nc, in_maps, core_ids=list(range(8)))
    R = res.results
    g = lambda c, k: np.asarray(R[c][k], dtype=np.float32)
    y_prompt = np.stack([g(0, "yp"), g(4, "yp")], 0)
    y_sample = np.concatenate([g(c, "ys").reshape(4, 64, D) for c in range(8)], 0)
    pc = lambda k, shp: np.stack([g(0, k), g(4, k)], 1).reshape(shp)
    sc = lambda k, shp: np.concatenate([g(c, k).reshape((2, 4) + shp) for c in range(8)], 1)
    return (y_prompt, y_sample,
            pc("o_pconv", (2, 2, 30, D)), pc("o_pqk", (2, 2, 3, 4096)), pc("o_pC", (2, 2, 4, DH, DH)),
            pc("o_pn", (2, 2, 4, DH)), pc("o_pm", (2, 2, 4)),
            sc("o_sconv", (30, D)), sc("o_sqk", (3, 4096)), sc("o_sC", (4, DH, DH)), sc("o_sn", (4, DH)), sc("o_sm", (4,)))


def kernel(**inputs):
    return run(inputs, 16)
```

```python
import math
from contextlib import ExitStack
import numpy as np
import concourse.bass as bass
import concourse.mybir as mybir
from concourse.bass_utils import run_bass_kernel_spmd

F32 = mybir.dt.float32
BF16 = mybir.dt.bfloat16
ALU = mybir.AluOpType
AF = mybir.ActivationFunctionType
AX = mybir.AxisListType

D = 1024
DIN = 15368
DML = 2048
NH = 4
DH = 512
EPS = 1e-6
C_A, C_GL, C_ZC, C_Q, C_K, C_V, C_O, C_ZM, C_I, C_F, C_GC, C_GM = (
    0, 1024, 2048, 3072, 5120, 7168, 9216, 11264, 13312, 13316, 13320, 14344)
SELF_SYNC = True


class Stop(Exception):
    pass


class Buf:
    def __init__(s, name):
        s.name = name
        s.w = None
        s.r = {}


class TB:
    def __init__(s, t, name):
        s.t = t
        s.b = Buf(name)

    def __getitem__(s, k):
        return s.t[k]


class View:
    def __init__(s, tb, flat, off):
        s.flat = flat
        s.off = off
        s.b = tb.b

    def __getitem__(s, k):
        r, c = k
        c0 = 0 if c.start is None else c.start
        c1 = 512 if c.stop is None else c.stop
        return s.flat[r, s.off + c0:s.off + c1]


class Sch:
    def __init__(s, nc, es):
        s.nc = nc
        s.names = ['pe', 'act', 'dve', 'pool', 'sp']
        s.prog = {n: [] for n in s.names}
        s.sem = {n: es.enter_context(nc.semaphore('s_' + n)) for n in ['pe', 'act', 'dve', 'pool']}
        s.cnt = {n: 0 for n in s.sem}
        s.NQ = 8
        s.dsem = {q: [es.enter_context(nc.semaphore('d_%s%d' % (q, i))) for i in range(s.NQ)] for q in ['sp', 'pool']}
        s.dcnt = {q: 0 for q in s.dsem}
        s.dlast = {}
        s.seen = {n: {} for n in s.names}

    def _wait(s, n, ev):
        k, h, v = ev
        if k == n and (n == 'pe' or not SELF_SYNC):
            return
        if s.seen[n].get(k, 0) >= v:
            return
        s.seen[n][k] = v
        s.prog[n].append(lambda e, h=h, v=v: e.wait_ge(h, v))

    def _deps(s, n, R, W):
        evs = []
        for b in R:
            if b.w is not None:
                evs.append(b.w)
        for b in W:
            if b.w is not None:
                evs.append(b.w)
            evs.extend(b.r.values())
        for ev in evs:
            s._wait(n, ev)

    def _commit(s, ev, R, W):
        for b in R:
            o = b.r.get(ev[0])
            if o is None or o[2] < ev[2]:
                b.r[ev[0]] = ev
        for b in W:
            b.w = ev
            b.r = {}

    def op(s, n, fn, R=(), W=()):
        s._deps(n, R, W)
        s.cnt[n] += 1
        h = s.sem[n]
        s.prog[n].append(lambda e, fn=fn, h=h: fn(e).then_inc(h, 1))
        s._commit((n, h, s.cnt[n]), R, W)

    def dma(s, q, out, in_, R=(), W=(), **kw):
        i = s.dcnt[q]
        s.dcnt[q] += 1
        slot = i % s.NQ
        v = 16 * (i // s.NQ + 1)
        h = s.dsem[q][slot]
        key = (q, slot)
        if v > 16:
            s._wait(q, (key, h, v - 16))
        s._deps(q, R, W)
        s.prog[q].append(lambda e, out=out, in_=in_, h=h, kw=kw: e.dma_start(out=out, in_=in_, **kw).then_inc(h, 16))
        ev = (key, h, v)
        s.dlast[key] = ev
        s._commit(ev, R, W)

    def barrier(s, engs=('pe', 'act', 'dve')):
        for n in engs:
            for m in engs:
                if m != n and s.cnt[m] > 0:
                    s._wait(n, (m, s.sem[m], s.cnt[m]))

    def finish(s):
        for ev in s.dlast.values():
            s._wait('sp', ev)
        for n in s.sem:
            if s.cnt[n] > 0:
                s._wait('sp', (n, s.sem[n], s.cnt[n]))

    def emit(s):
        nc = s.nc
        with nc.Block() as block:
            @block.tensor
            def _(e):
                for f in s.prog['pe']:
                    f(e)

            @block.scalar
            def _(e):
                for f in s.prog['act']:
                    f(e)

            @block.vector
            def _(e):
                for f in s.prog['dve']:
                    f(e)

            @block.gpsimd
            def _(e):
                for f in s.prog['pool']:
                    f(e)

            @block.sync
            def _(e):
                for f in s.prog['sp']:
                    f(e)


def build(NPT):
    nc = bass.Bass("TRN2", target_bir_lowering=False)
    NP = NPT * 512

    def din(name, shape):
        return nc.dram_tensor(name, list(shape), F32, kind="ExternalInput").ap()

    def dout(name, shape):
        return nc.dram_tensor(name, list(shape), F32, kind="ExternalOutput").ap()

    def dint(name, shape):
        return nc.dram_tensor(name, list(shape), F32).ap()

    xs = din("xs", [256, D]); xp = din("xp", [NP, D]); meta = din("meta", [16, D])
    st_conv = din("st_conv", [2, 120, D]); st_qk = din("st_qk", [2, 12, 4096])
    st_C = din("st_C", [2, 4, 4, DH, DH]); st_n = din("st_n", [2, 4, 16, 128]); st_m = din("st_m", [2, 4, 4])
    w_in = din("w_in", [2, D, DIN]); b_in = din("b_in", [2, DIN])
    w_dw = din("w_dw", [2, 31, D]); b_dw = din("b_dw", [2, 8, 128]); ln_g = din("ln_g", [2, 8, 128]); ln_b = din("ln_b", [2, 8, 128])
    w_qkc = din("w_qkc", [2, 4, 32, 128]); b_qkc = din("b_qkc", [2, 32, 128]); f_bias = din("f_bias", [2, 4])
    ml_norm = din("ml_norm", [2, 16, 128]); w_co = din("w_co", [2, D, D]); w_ml = din("w_ml", [2, DML, D]); w_out = din("w_out", [2, D, D])
    norm_pre = din("norm_pre", [2, D]); norm_post = din("norm_post", [2, D])
    cst = din("cst", [2, 128, 128])
    ys = dout("ys", [256, D]); yp = dout("yp", [NP, D])
    o_sconv = dout("o_sconv", [2, 120, D]); o_sqk = dout("o_sqk", [2, 12, 4096])
    o_sC = dout("o_sC", [2, 4, 4, DH, DH]); o_sn = dout("o_sn", [2, 4, 16, 128]); o_sm = dout("o_sm", [2, 4, 4])
    o_pconv = dout("o_pconv", [2, 30, D]); o_pqk = dout("o_pqk", [2, 3, 4096])
    o_pC = dout("o_pC", [2, 4, DH, DH]); o_pn = dout("o_pn", [2, 16, 128]); o_pm = dout("o_pm", [2, 4])
    x1s = dint("x1s", [256, D]); x1p = dint("x1p", [NP, D]); x1l = dint("x1l", [16, D]); ydump = dint("ydump", [16, D])

    es = ExitStack()
    with es:
        S = Sch(nc, es)

        def sb(name, shape, dt=F32):
            return TB(es.enter_context(nc.sbuf_tensor(name, list(shape), dt)), name)

        def pst(name, shape, dt=F32):
            return TB(es.enter_context(nc.psum_tensor(name, list(shape), dt)), name)

        def MM(out, lhsT, rhs, st, sp, R, W):
            S.op('pe', lambda e: e.matmul(out, lhsT, rhs, start=st, stop=sp), R, W)

        def TR(out, in_, ident, R, W):
            S.op('pe', lambda e: e.transpose(out, in_, ident), R, W)

        def ACT(out, in_, func, R, W, bias=0.0, scale=1.0, accum=None):
            if accum is None:
                S.op('act', lambda e: e.activation(out, in_, func, bias=bias, scale=scale), R, W)
            else:
                S.op('act', lambda e: e.activation(out, in_, func, bias=bias, scale=scale, accum_out=accum), R, W)

        def TT(eng, out, in0, in1, op, R, W):
            S.op(eng, lambda e: e.tensor_tensor(out, in0, in1, op), R, W)

        def TS(eng, out, in0, s1, s2, op0, op1, R, W):
            if s2 is None:
                S.op(eng, lambda e: e.tensor_scalar(out, in0, s1, None, op0), R, W)
            else:
                S.op(eng, lambda e: e.tensor_scalar(out, in0, s1, s2, op0, op1), R, W)

        def STT(eng, out, in0, sc, in1, op0, op1, R, W):
            S.op(eng, lambda e: e.scalar_tensor_tensor(out, in0, sc, in1, op0, op1), R, W)

        def CP(eng, out, in_, R, W):
            if eng == 'act':
                S.op('act', lambda e: e.copy(out, in_), R, W)
            else:
                S.op(eng, lambda e: e.tensor_copy(out, in_), R, W)

        def RSQ(out, in_, b):
            S.op('act', lambda e: e.activation(out, in_, AF.Sqrt), (b,), (b,))
            S.op('dve', lambda e: e.reciprocal(out, out), (b,), (b,))

        def MS(eng, ap, val, W):
            S.op(eng, lambda e: e.memset(ap, val), (), W)

        ident_f = sb("ident_f", [128, 128]); tri_f = sb("tri_f", [128, 128])
        ident_b = sb("ident_b", [128, 128], BF16)
        ones_b = sb("ones_b", [128, 128], BF16); ones_f = sb("ones_f", [128, 128])
        gpre = sb("gpre", [128, D]); gpost = sb("gpost", [128, D]); bvbc = sb("bvbc", [128, 512])
        bcol = sb("bcol", [128, 120]); wdw = sb("wdw", [128, 8, 32]); misc = sb("misc", [128, 24])
        wqk = sb("wqk", [128, 128]); bqk = sb("bqk", [128, 32]); mln = sb("mln", [128, 16])
        bi_c = sb("bi_c", [4, 1]); bf_c = sb("bf_c", [4, 1]); fb_c = sb("fb_c", [4, 1]); nbfz = sb("nbfz", [4, 1])
        wif = sb("wif", [128, 8, 8], BF16)
        stg = sb("stg", [128, D])
        xres = [sb("xres0", [128, D])] * 4
        wstg = sb("wstg", [128, 8, 512])
        hn = sb("hn", [128, D], BF16)
        hT = [sb("hT0", [128, 8, 512], BF16)]
        ss = sb("ss", [128, 8]); junk = hn
        NWB = 3
        wb = [sb("wb%d" % i, [128, 8, 512], BF16) for i in range(NWB)]
        ubufs = [sb("ubuf%d" % i, [128, 544], BF16) for i in range(2)]
        dgs = [sb("dg%d" % i, [128, 128], BF16) for i in range(4)]; sg = sb("sg", [128, 512])
        ucv = sb("ucv", [128, 8, 512]); ubf = sb("ubf", [128, 512], BF16); usq = sb("usq", [128, 512], BF16)
        tmpa = sb("tmpa", [128, 512])
        ucb = sb("ucb", [128, 8, 512], BF16)
        Cst = sb("Cst", [128, 4, 512])
        _cf = Cst[:, :, :].rearrange("p a b -> p (a b)")
        mean = View(Cst, _cf, 0); rstd = View(Cst, _cf, 512); nmr = View(Cst, _cf, 1024); tmpb = View(Cst, _cf, 1536)
        yacc = ucv; ybf = ucb
        qkbufs = [sb("qkbuf%d" % i, [128, 520]) for i in range(2)]; qkacc = sb("qkacc", [128, 512])
        qT = sb("qT", [128, 4, 512], BF16); kT = sb("kT", [128, 4, 512], BF16)
        gate = sb("gate", [128, 4, 512], BF16); so = tmpa
        hmT = sb("hmT", [128, 16, 512], BF16)
        stb4 = sb("stb4", [128, 4, 128], BF16)
        hnb = sb("hnb", [128, 512], BF16)
        irow = sb("irow", [4, 512]); lfrow = sb("lfrow", [4, 512]); cum = sb("cum", [4, 128]); grow = sb("grow", [4, 128])
        arow = sb("arow", [4, 128]); flrow = sb("flrow", [4, 128]); onesrow = sb("onesrow", [4, 128])
        sc4 = sb("sc4", [4, 8]); dg4 = sb("dg4", [4, 4])
        gb = sb("gb", [128, 4, 12])
        sc1 = sb("sc1", [128, 8])
        uhP = sb("uhP", [128, 8, 30]); qkhP = sb("qkhP", [128, 4, 8, 3])
        CPs = sb("CPs", [128, 16, 512]); Cwb = sb("Cwb", [128, 4, 512], BF16)
        nP = sb("nP", [128, 16]); nPb = sb("nPb", [128, 16], BF16); mP = sb("mP", [4, 1])
        nS = sb("nS", [128, 4, 16]); nSb = sb("nSb", [128, 4, 16], BF16); mS = sb("mS", [4, 4]); mSo = sb("mSo", [4, 4])
        yst = [sb("yst0", [128, D])]
        sso = sb("sso", [128, 4])
        vbf4 = yst[0][:, :].bitcast(BF16).rearrange("p (j e) -> p j e", e=512)

        pg = [pst("pg%d" % i, [128, 512]) for i in range(4)]
        pT = pst("pT", [128, 1024], BF16)
        pN = pst("pN", [128, 512])
        pC = pst("pC", [128, 512])
        pS = pst("pS", [128, 512])
        pgi = [0]

        def PG():
            pgi[0] += 1
            return pg[pgi[0] % 4]

        wbi = [0]

        NBLK = 80
        wsc = nc.dram_tensor("wsc", [NBLK, 128, 8 * 512], BF16).ap()
        wmap = {}
        wbufs = {}
        cur_l = [0]

        def WL(tag, dram2d, r0, c0, ncols):
            t = wb[wbi[0] % NWB]
            wbi[0] += 1
            key = (cur_l[0], tag, r0, c0)
            if key not in wmap:
                idx = len(wmap)
                assert idx < NBLK
                wmap[key] = idx
                wbufs[key] = Buf("wsc%d" % idx)
                src = dram2d[r0:r0 + 1024, c0:c0 + ncols].rearrange("(kc p) n -> p kc n", p=128)
                S.dma('sp', wstg[:, :, 0:ncols], src, (), (wstg.b,))
                CP('pool', t[:, :, 0:ncols], wstg[:, :, 0:ncols], (wstg.b,), (t.b,))
                S.dma('sp', wsc[idx].rearrange("p (k n) -> p k n", n=512), t[:, :, :], (t.b,), (wbufs[key],))
            else:
                idx = wmap[key]
                S.dma('sp', t[:, :, :], wsc[idx].rearrange("p (k n) -> p k n", n=512), (wbufs[key],), (t.b,))
            return t

        S.dma('sp', ident_f[:, :], cst[0], (), (ident_f.b,))
        S.dma('sp', tri_f[:, :], cst[1], (), (tri_f.b,))
        CP('dve', ident_b[:, :], ident_f[:, :], (ident_f.b,), (ident_b.b,))
        MS('dve', ones_b[:, :], 1.0, (ones_b.b,))
        MS('dve', ones_f[:, :], 1.0, (ones_f.b,))
        MS('dve', onesrow[:, :], 1.0, (onesrow.b,))

        def load_cols(dst_ap, dstb, src_rows_ap, nrows):
            S.dma('sp', stg[0:nrows, 0:128], src_rows_ap, (), (stg.b,))
            p = PG()
            TR(p[:, 0:nrows], stg[0:nrows, 0:128], ident_f[0:nrows, 0:nrows], (stg.b, ident_f.b), (p.b,))
            CP('dve', dst_ap, p[:, 0:nrows], (p.b,), (dstb,))

        def layer_params(l):
            S.dma('sp', gpre[:, :], norm_pre[l].partition_broadcast(128), (), (gpre.b,))
            S.dma('sp', gpost[:, :], norm_post[l].partition_broadcast(128), (), (gpost.b,))
            load_cols(bcol[:, 0:104], bcol.b, b_in[l, 0:13312].rearrange("(r c) -> r c", c=128), 104)
            load_cols(bcol[:, 104:120], bcol.b, b_in[l, C_GC:C_GC + 2048].rearrange("(r c) -> r c", c=128), 16)
            for c in range(8):
                load_cols(wdw[:, c, 0:31], wdw.b, w_dw[l, :, c * 128:(c + 1) * 128], 31)
            load_cols(misc[:, 0:8], misc.b, b_dw[l], 8)
            load_cols(misc[:, 8:16], misc.b, ln_g[l], 8)
            load_cols(misc[:, 16:24], misc.b, ln_b[l], 8)
            load_cols(wqk[:, :], wqk.b, w_qkc[l].rearrange("t c p -> (t c) p"), 128)
            load_cols(bqk[:, :], bqk.b, b_qkc[l], 32)
            load_cols(mln[:, :], mln.b, ml_norm[l], 16)
            S.dma('sp', bi_c[:, :], b_in[l, C_I:C_I + 4].rearrange("(h o) -> h o", o=1), (), (bi_c.b,))
            S.dma('sp', bf_c[:, :], b_in[l, C_F:C_F + 4].rearrange("(h o) -> h o", o=1), (), (bf_c.b,))
            S.dma('sp', fb_c[:, :], f_bias[l].rearrange("(h o) -> h o", o=1), (), (fb_c.b,))
            STT('dve', nbfz[:, :], bf_c[:, :], -1.0, fb_c[:, :], ALU.mult, ALU.subtract, (bf_c.b, fb_c.b), (nbfz.b,))
            S.dma('sp', stg[:, 0:64].rearrange("p (k n) -> p k n", n=8), w_in[l, :, C_I:C_I + 8].rearrange("(kc p) n -> p kc n", p=128), (), (stg.b,))
            CP('dve', wif[:, :, :], stg[:, 0:64].rearrange("p (k n) -> p k n", n=8), (stg.b,), (wif.b,))

        LNSC = math.log(DH ** -0.5)
        import os
        kstop = int(os.environ.get('K_STOP', '-1'))
        ckc = [0]

        def ck():
            if ckc[0] == kstop:
                raise Stop()
            ckc[0] += 1

        def tile(l, kind, tix, xsrc, ydst, xb, yb):
            if kind == 'S':
                nseg, SL, L = 4, 64, 64
            elif kind == 'LEAD':
                nseg, SL, L = 1, 16, 16
            else:
                nseg, SL, L = 1, 512, 128
            ntok = nseg * SL
            nch = ntok // L
            nsub = (ntok + 127) // 128
            h_T = hT[0]
            cur_l[0] = l
            Win = w_in[l]

            def segv(ap2d, w):
                return ap2d.rearrange("p (s w) -> p s w", w=w)

            for st in range(nsub):
                r = min(128, ntok - st * 128)
                xt = xres[st]
                S.dma('sp', xt[0:r, :], xsrc[st * 128:st * 128 + r, :], xb, (xt.b,))
                ACT(junk[0:r, :], xt[0:r, :], AF.Square, (xt.b,), (junk.b, ss.b), accum=ss[0:r, 0:1])
                TS('dve', ss[0:r, 1:2], ss[0:r, 0:1], 1.0 / D, EPS, ALU.mult, ALU.add, (ss.b,), (ss.b,))
                RSQ(ss[0:r, 2:3], ss[0:r, 1:2], ss.b)
                STT('dve', hn[0:r, :], xt[0:r, :], ss[0:r, 2:3], gpre[0:r, :], ALU.mult, ALU.mult, (xt.b, ss.b, gpre.b), (hn.b,))
                for kc in range(8):
                    TR(pT[:, kc * 128:kc * 128 + r], hn[0:r, kc * 128:(kc + 1) * 128], ident_b[0:r, 0:r], (hn.b, ident_b.b), (pT.b,))
                CP('act', h_T[:, :, st * 128:st * 128 + r], pT[:, :].rearrange("p (k t) -> p k t", t=128)[:, :, 0:r], (pT.b,), (h_T.b,))

            ck()

            def fm_proj(wt, coff, R_extra=()):
                p = PG()
                for kc in range(8):
                    MM(p[:, 0:ntok], wt[:, kc, coff:coff + 128], h_T[:, kc, 0:ntok], kc == 0, kc == 7, (wt.b, h_T.b), (p.b,))
                return p

            W30 = 30 + SL
            if kind == 'S':
                S.dma('sp', stg[0:120, :], st_conv[l], (), (stg.b,))
            CstF = Cst[:, :, :].rearrange("p a b -> p (a b)")
            wab = [None, None]

            def conv_proj(c):
                cg, ci = divmod(c, 4)
                if ci == 0:
                    wab[0] = WL('in', Win, 0, C_A + cg * 512, 512)
                    wab[1] = WL('in', Win, 0, C_GL + cg * 512, 512)
                wa, wg = wab
                ub = ubufs[c % 2]
                uvc = segv(ub[:, 0:nseg * W30], W30)
                if kind == 'S':
                    p = PG()
                    TR(p[:, 0:120], stg[0:120, c * 128:(c + 1) * 128], ident_f[0:120, 0:120], (stg.b, ident_f.b), (p.b,))
                    CP('dve', uvc[:, :, 0:30], segv(p[:, 0:120], 30), (p.b,), (ub.b,))
                elif kind == 'LEAD':
                    MS('dve', ub[:, 0:30], 0.0, (ub.b,))
                else:
                    CP('dve', ub[:, 0:30], uhP[:, c, :], (uhP.b,), (ub.b,))
                pa = fm_proj(wa, ci * 128)
                pgl = fm_proj(wg, ci * 128)
                ACT(sg[:, 0:ntok], pgl[:, 0:ntok], AF.Sigmoid, (pgl.b, bcol.b), (sg.b,), bias=bcol[:, 8 + c:9 + c])
                STT('dve', uvc[:, :, 30:W30], segv(pa[:, 0:ntok], SL), bcol[:, c:c + 1], segv(sg[:, 0:ntok], SL),
                    ALU.add, ALU.mult, (pa.b, sg.b, bcol.b), (ub.b,))

            def conv_taps(c):
                ub = ubufs[c % 2]
                uvc = segv(ub[:, 0:nseg * W30], W30)
                pc_ = PG()
                for sgi in range(nseg):
                    for k in range(31):
                        dgk = dgs[k % 4]
                        TS('dve', dgk[:, :], ident_b[:, :], wdw[:, c, k:k + 1], None, ALU.mult, None, (ident_b.b, wdw.b), (dgk.b,))
                        MM(pc_[:, sgi * SL:(sgi + 1) * SL], dgk[:, :], ub[:, sgi * W30 + k:sgi * W30 + k + SL], k == 0, k == 30,
                           (dgk.b, ub.b), (pc_.b,))
                ACT(ucv[:, c, 0:ntok], pc_[:, 0:ntok], AF.Identity, (pc_.b, misc.b), (ucv.b,), bias=misc[:, c:c + 1])
                if kind == 'S':
                    p = PG()
                    CP('dve', segv(tmpa[:, 0:120], 30), uvc[:, :, SL:W30], (ub.b,), (tmpa.b,))
                    TR(p[0:120, 0:128], tmpa[:, 0:120], ident_f[:, :], (tmpa.b, ident_f.b), (p.b,))
                    CP('act', CstF[0:120, c * 128:(c + 1) * 128], p[0:120, 0:128], (p.b,), (Cst.b,))
                else:
                    CP('dve', uhP[:, c, :], ub[:, SL:W30], (ub.b,), (uhP.b,))

            for c in range(8):
                conv_proj(c)
                if c > 0:
                    conv_taps(c - 1)
            conv_taps(7)
            if kind == 'S':
                S.dma('sp', o_sconv[l], CstF[0:120, 0:1024], (Cst.b,), ())

            ck()
            p_sum = PG(); p_sq = PG()
            for c in range(8):
                CP('act', ubf[:, 0:ntok], ucv[:, c, 0:ntok], (ucv.b,), (ubf.b,))
                ACT(usq[:, 0:ntok], ucv[:, c, 0:ntok], AF.Square, (ucv.b,), (usq.b,))
                MM(p_sum[:, 0:ntok], ones_b[:, :], ubf[:, 0:ntok], c == 0, c == 7, (ones_b.b, ubf.b), (p_sum.b,))
                MM(p_sq[:, 0:ntok], ones_b[:, :], usq[:, 0:ntok], c == 0, c == 7, (ones_b.b, usq.b), (p_sq.b,))
            TS('dve', mean[:, 0:ntok], p_sum[:, 0:ntok], 1.0 / D, None, ALU.mult, None, (p_sum.b,), (mean.b,))
            TT('dve', tmpa[:, 0:ntok], mean[:, 0:ntok], mean[:, 0:ntok], ALU.mult, (mean.b,), (tmpa.b,))
            STT('dve', rstd[:, 0:ntok], p_sq[:, 0:ntok], 1.0 / D, tmpa[:, 0:ntok], ALU.mult, ALU.subtract, (p_sq.b, tmpa.b), (rstd.b,))
            TS('dve', rstd[:, 0:ntok], rstd[:, 0:ntok], EPS, None, ALU.add, None, (rstd.b,), (rstd.b,))
            RSQ(rstd[:, 0:ntok], rstd[:, 0:ntok], rstd.b)
            STT('dve', nmr[:, 0:ntok], mean[:, 0:ntok], -1.0, rstd[:, 0:ntok], ALU.mult, ALU.mult, (mean.b, rstd.b), (nmr.b,))
            for cg in range(2):
                wz = WL('in', Win, 0, C_ZC + cg * 512, 512)
                for ci in range(4):
                    c = cg * 4 + ci
                    pz = fm_proj(wz, ci * 128)
                    TT('dve', tmpa[:, 0:ntok], ucv[:, c, 0:ntok], rstd[:, 0:ntok], ALU.mult, (ucv.b, rstd.b), (tmpa.b,))
                    TT('dve', tmpa[:, 0:ntok], tmpa[:, 0:ntok], nmr[:, 0:ntok], ALU.add, (tmpa.b, nmr.b), (tmpa.b,))
                    ACT(tmpb[:, 0:ntok], tmpa[:, 0:ntok], AF.Silu, (tmpa.b, misc.b), (tmpb.b,), bias=misc[:, 16 + c:17 + c], scale=misc[:, 8 + c:9 + c])
                    ACT(sg[:, 0:ntok], pz[:, 0:ntok], AF.Silu, (pz.b, bcol.b), (sg.b,), bias=bcol[:, 16 + c:17 + c])
                    TT('dve', ucb[:, c, 0:ntok], tmpb[:, 0:ntok], sg[:, 0:ntok], ALU.mult, (tmpb.b, sg.b), (ucb.b,))
            for dg in range(2):
                wc = WL('co', w_co[l], 0, dg * 512, 512)
                wgc = WL('in', Win, 0, C_GC + dg * 512, 512)
                for di in range(4):
                    dt = dg * 4 + di
                    pb = PG()
                    for c in range(8):
                        MM(pb[:, 0:ntok], wc[:, c, di * 128:(di + 1) * 128], ucb[:, c, 0:ntok], c == 0, c == 7, (wc.b, ucb.b), (pb.b,))
                    pgc = fm_proj(wgc, di * 128)
                    ACT(sg[:, 0:ntok], pgc[:, 0:ntok], AF.Sigmoid, (pgc.b, bcol.b), (sg.b,), bias=bcol[:, 104 + dt:105 + dt])
                    TT('dve', yacc[:, dt, 0:ntok], pb[:, 0:ntok], sg[:, 0:ntok], ALU.mult, (pb.b, sg.b), (yacc.b,))

            ck()
            p_i = PG(); p_f = PG()
            for kc in range(8):
                MM(p_i[0:4, 0:ntok], wif[:, kc, 0:4], h_T[:, kc, 0:ntok], kc == 0, kc == 7, (wif.b, h_T.b), (p_i.b,))
            for kc in range(8):
                MM(p_f[0:4, 0:ntok], wif[:, kc, 4:8], h_T[:, kc, 0:ntok], kc == 0, kc == 7, (wif.b, h_T.b), (p_f.b,))
            ACT(irow[:, 0:ntok], p_i[0:4, 0:ntok], AF.Identity, (p_i.b, bi_c.b), (irow.b,), bias=bi_c[:, 0:1])
            ACT(lfrow[:, 0:ntok], p_f[0:4, 0:ntok], AF.Exp, (p_f.b, nbfz.b), (lfrow.b,), bias=nbfz[:, 0:1], scale=-1.0)
            ACT(lfrow[:, 0:ntok], lfrow[:, 0:ntok], AF.Ln, (lfrow.b,), (lfrow.b,), bias=1.0)
            if kind == 'S':
                S.dma('sp', mS[:, :], st_m[l].rearrange("s h -> h s"), (), (mS.b,), allow_slow_non_contiguous=True)
            for j in range(nch):
                t0 = j * L
                mcur = mS[:, j:j + 1] if kind == 'S' else mP[:, 0:1]
                mb = mS.b if kind == 'S' else mP.b
                mnew = mSo[:, j:j + 1] if kind == 'S' else mP[:, 0:1]
                mnb = mSo.b if kind == 'S' else mP.b
                S.op('dve', lambda e, t0=t0: e.tensor_tensor_scan(cum[:, 0:L], onesrow[:, 0:L], lfrow[:, t0:t0 + L], 0.0, ALU.mult, ALU.add),
                     (onesrow.b, lfrow.b), (cum.b,))
                TT('dve', grow[:, 0:L], irow[:, t0:t0 + L], cum[:, 0:L], ALU.add, (irow.b, cum.b), (grow.b,))
                S.op('dve', lambda e: e.reduce_max(sc4[:, 0:1], grow[:, 0:L], AX.X), (grow.b,), (sc4.b,))
                TT('dve', sc4[:, 1:2], sc4[:, 0:1], mcur, ALU.max, (sc4.b, mb), (sc4.b,))
                TS('dve', sc4[:, 2:3], sc4[:, 1:2], -1.0, LNSC, ALU.mult, ALU.add, (sc4.b,), (sc4.b,))
                TS('dve', sc4[:, 3:4], sc4[:, 1:2], -1.0, None, ALU.mult, None, (sc4.b,), (sc4.b,))
                ACT(arow[:, 0:L], grow[:, 0:L], AF.Exp, (grow.b, sc4.b), (arow.b,), bias=sc4[:, 2:3])
                ACT(flrow[:, 0:L], cum[:, 0:L], AF.Exp, (cum.b, sc4.b), (flrow.b,), bias=sc4[:, 3:4])
                ACT(sc4[:, 4:5], mcur, AF.Exp, (mb, sc4.b), (sc4.b,), bias=sc4[:, 3:4])
                TT('dve', mnew, sc4[:, 1:2], cum[:, L - 1:L], ALU.subtract, (sc4.b, cum.b), (mnb,))
                TS('dve', dg4[:, :], ident_f[0:4, 0:4], sc4[:, 4:5], None, ALU.mult, None, (ident_f.b, sc4.b), (dg4.b,))
                TR(pS[0:L, 256:260], arow[:, 0:L], ident_f[0:4, 0:4], (arow.b, ident_f.b), (pS.b,))
                TR(pS[0:L, 260:264], flrow[:, 0:L], ident_f[0:4, 0:4], (flrow.b, ident_f.b), (pS.b,))
                MM(pS[:, 264:268], ones_f[0:4, :], dg4[:, :], True, True, (ones_f.b, dg4.b), (pS.b,))
                CP('dve', gb[0:L, j, 0:8], pS[0:L, 256:264], (pS.b,), (gb.b,))
                CP('dve', gb[:, j, 8:12], pS[:, 264:268], (pS.b,), (gb.b,))
            if kind == 'S':
                S.dma('sp', o_sm[l].rearrange("s h -> h s"), mSo[:, :], (mSo.b,), (), allow_slow_non_contiguous=True)
                for sq in range(4):
                    load_cols(nS[:, sq, :], nS.b, st_n[l, sq], 16)

            ck()
            W3 = 3 + SL
            for h in range(NH):
                wq = WL('in', Win, 0, C_Q + h * 512, 512)
                wk = WL('in', Win, 0, C_K + h * 512, 512)
                S.dma('sp', bvbc[:, :], b_in[l, C_V + h * 512:C_V + (h + 1) * 512].partition_broadcast(128), (), (bvbc.b,))
                for i in range(8):
                    wt = wq if i < 4 else wk
                    cidx = (24 if i < 4 else 40) + h * 4 + (i % 4)
                    qc = (0 if i < 4 else 16) + h * 4 + (i % 4)
                    qk_off = (0 if i < 4 else 2048) + h * 512 + (i % 4) * 128
                    qb = qkbufs[i % 2]
                    qvi = segv(qb[:, 0:nseg * W3], W3)
                    if kind == 'S':
                        S.dma('sp', stg[0:12, 0:128], st_qk[l][:, qk_off:qk_off + 128], (), (stg.b,))
                        p = PG()
                        TR(p[:, 0:12], stg[0:12, 0:128], ident_f[0:12, 0:12], (stg.b, ident_f.b), (p.b,))
                        CP('dve', qvi[:, :, 0:3], segv(p[:, 0:12], 3), (p.b,), (qb.b,))
                    elif kind == 'LEAD':
                        MS('dve', qb[:, 0:3], 0.0, (qb.b,))
                    else:
                        CP('dve', qb[:, 0:3], qkhP[:, h, i, :], (qkhP.b,), (qb.b,))
                    p = fm_proj(wt, (i % 4) * 128)
                    ACT(qvi[:, :, 3:W3], segv(p[:, 0:ntok], SL), AF.Identity, (p.b, bcol.b), (qb.b,), bias=bcol[:, cidx:cidx + 1])
                    eng = 'dve'
                    acc = segv(qkacc[:, 0:ntok], SL)
                    TS(eng, acc, qvi[:, :, 0:SL], wqk[:, qc:qc + 1], bqk[:, qc:qc + 1], ALU.mult, ALU.add, (qb.b, wqk.b, bqk.b), (qkacc.b,))
                    for k in range(1, 4):
                        STT(eng, acc, qvi[:, :, k:k + SL], wqk[:, k * 32 + qc:k * 32 + qc + 1], acc, ALU.mult, ALU.add, (qb.b, wqk.b), (qkacc.b,))
                    dst = qT if i < 4 else kT
                    ACT(dst[:, i % 4, 0:ntok], qkacc[:, 0:ntok], AF.Silu, (qkacc.b,), (dst.b,))
                    if kind == 'S':
                        CP('dve', segv(tmpa[:, 0:12], 3), qvi[:, :, SL:W3], (qb.b,), (tmpa.b,))
                        p = PG()
                        TR(p[0:12, 0:128], tmpa[:, 0:12], ident_f[:, :], (tmpa.b, ident_f.b), (p.b,))
                        CP('act', stg[0:12, 128:256], p[0:12, 0:128], (p.b,), (stg.b,))
                        S.dma('sp', o_sqk[l][:, qk_off:qk_off + 128], stg[0:12, 128:256], (stg.b,), ())
                    else:
                        CP('dve', qkhP[:, h, i, :], qb[:, SL:W3], (qb.b,), (qkhP.b,))
                    if h == 0 and i in (0, 7):
                        ck()
                wo = WL('in', Win, 0, C_O + h * 512, 512)
                wz = WL('in', Win, 0, C_ZM + h * 512, 512)
                for et in range(4):
                    po = fm_proj(wo, et * 128)
                    pz = fm_proj(wz, et * 128)
                    ACT(so[:, 0:ntok], po[:, 0:ntok], AF.Sigmoid, (po.b, bcol.b), (so.b,), bias=bcol[:, 72 + h * 4 + et:73 + h * 4 + et])
                    ACT(sg[:, 0:ntok], pz[:, 0:ntok], AF.Silu, (pz.b, bcol.b), (sg.b,), bias=bcol[:, 88 + h * 4 + et:89 + h * 4 + et])
                    STT('dve', gate[:, et, 0:ntok], so[:, 0:ntok], mln[:, h * 4 + et:h * 4 + et + 1], sg[:, 0:ntok], ALU.mult, ALU.mult,
                        (so.b, sg.b, mln.b), (gate.b,))
                ck()
                wv = WL('in', Win, 0, C_V + h * 512, 512)
                ck()
                if kind != 'S':
                    CP('act', Cwb[:, :, :], CPs[:, h * 4:h * 4 + 4, :], (CPs.b,), (Cwb.b,))
                for j in range(nch):
                    t0 = j * L
                    acol = gb[0:L, j, h:h + 1]
                    emc = gb[:, j, 8 + h:9 + h]
                    pv = PG()
                    for kc in range(8):
                        MM(pv[0:L, :], h_T[:, kc, t0:t0 + L], wv[:, kc, :], kc == 0, kc == 7, (h_T.b, wv.b), (pv.b,))
                    TT('dve', vbf4[0:L, j, :], pv[0:L, :], bvbc[0:L, :], ALU.add, (pv.b, bvbc.b), (yst[0].b,))
                    for dt in range(4):
                        MM(pS[0:L, 0:L], kT[:, dt, t0:t0 + L], qT[:, dt, t0:t0 + L], dt == 0, dt == 3, (kT.b, qT.b), (pS.b,))
                    STT('dve', stb4[0:L, j, 0:L], pS[0:L, 0:L], acol, tri_f[0:L, 0:L], ALU.mult, ALU.mult, (pS.b, gb.b, tri_f.b), (stb4.b,))
                    for dt in range(4):
                        TR(pT[0:L, dt * 128:(dt + 1) * 128], kT[:, dt, t0:t0 + L], ident_b[:, :], (kT.b, ident_b.b), (pT.b,))
                    TS('dve', ucb[0:L, j, :], pT[0:L, 0:512], acol, None, ALU.mult, None, (pT.b, gb.b), (ucb.b,))
                    TS('dve', ucb[:, 4 + j, :].rearrange("p (d t) -> p d t", t=128)[:, :, 0:L], qT[:, :, t0:t0 + L], emc, None, ALU.mult, None,
                       (qT.b, gb.b), (ucb.b,))
                for j in range(int(os.environ.get('K_J0', '0')) if h == 0 else 0, nch):
                    t0 = j * L
                    hb = 0 if kind == 'S' else h * 4
                    Cf = lambda dt, hb=hb: CPs[:, hb + dt, :]
                    Cb_ = lambda dt: Cwb[:, dt, :]
                    Cfb, Cbb = CPs.b, Cwb.b
                    if kind == 'S':
                        S.dma('sp', Cst[:, :, :], st_C[l, j, h].rearrange("(et p) d -> p et d", p=128), (), (Cst.b,))
                        for dt in range(4):
                            p = PG()
                            for et in range(4):
                                TR(p[:, et * 128:(et + 1) * 128], Cst[:, et, dt * 128:(dt + 1) * 128], ident_f[:, :], (Cst.b, ident_f.b), (p.b,))
                            CP('dve', Cf(dt), p[:, :], (p.b,), (Cfb,))
                            CP('act', Cb_(dt), Cf(dt), (Cfb,), (Cbb,))
                        CP('dve', nSb[:, j, h * 4:h * 4 + 4], nS[:, j, h * 4:h * 4 + 4], (nS.b,), (nSb.b,))
                        nf = lambda dt, j=j, h=h: nS[:, j, h * 4 + dt:h * 4 + dt + 1]
                        nb_ = lambda dt, j=j, h=h: nSb[:, j, h * 4 + dt:h * 4 + dt + 1]
                        nfb, nbb = nS.b, nSb.b
                    else:
                        nf = lambda dt, h=h: nP[:, h * 4 + dt:h * 4 + dt + 1]
                        nb_ = lambda dt, h=h: nPb[:, h * 4 + dt:h * 4 + dt + 1]
                        nfb, nbb = nP.b, nPb.b
                    flcol = gb[0:L, j, 4 + h:5 + h]
                    emc = gb[:, j, 8 + h:9 + h]
                    stj = stb4[0:L, j, 0:L]
                    vj = vbf4[0:L, j, :]
                    qsj = lambda dt, j=j: ucb[:, 4 + j, dt * 128:dt * 128 + L]
                    MM(pN[0:L, :], stj, vj, True, False, (stb4.b, yst[0].b), (pN.b,))
                    for dt in range(4):
                        MM(pN[0:L, :], qsj(dt), Cb_(dt), False, dt == 3, (ucb.b, Cbb), (pN.b,))
                    MM(pS[0:L, 128:129], stj, ones_b[0:L, 0:1], True, False, (stb4.b, ones_b.b), (pS.b,))
                    for dt in range(4):
                        MM(pS[0:L, 128:129], qsj(dt), nb_(dt), False, dt == 3, (ucb.b, nbb), (pS.b,))
                    CP('dve', sc1[0:L, 6:7], pS[0:L, 128:129], (pS.b,), (sc1.b,))
                    for dt in range(4):
                        MM(pC[:, :], ucb[0:L, j, dt * 128:(dt + 1) * 128], vj, True, True, (ucb.b, yst[0].b), (pC.b,))
                        STT('dve', Cf(dt), Cf(dt), emc, pC[:, :], ALU.mult, ALU.add, (Cfb, gb.b, pC.b), (Cfb,))
                        CP('act', Cb_(dt), Cf(dt), (Cfb,), (Cbb,))
                        MM(pS[:, 132 + dt:133 + dt], ucb[0:L, j, dt * 128:(dt + 1) * 128], ones_b[0:L, 0:1], True, True, (ucb.b, ones_b.b), (pS.b,))
                        STT('dve', nf(dt), nf(dt), emc, pS[:, 132 + dt:133 + dt], ALU.mult, ALU.add, (nfb, gb.b, pS.b), (nfb,))
                        CP('dve', nb_(dt), nf(dt), (nfb,), (nbb,))
                    STT('dve', sc1[0:L, 7:8], sc1[0:L, 6:7], -1.0, sc1[0:L, 6:7], ALU.mult, ALU.max, (sc1.b,), (sc1.b,))
                    TT('dve', sc1[0:L, 0:1], sc1[0:L, 7:8], flcol, ALU.max, (sc1.b, gb.b), (sc1.b,))
                    S.op('dve', lambda e: e.reciprocal(sc1[0:L, 1:2], sc1[0:L, 0:1]), (sc1.b,), (sc1.b,))
                    ACT(junk[0:L, 0:512], pN[0:L, :], AF.Square, (pN.b, sc1.b), (junk.b, sc1.b), scale=sc1[0:L, 1:2], accum=sc1[0:L, 2:3])
                    TS('dve', sc1[0:L, 3:4], sc1[0:L, 2:3], 1.0 / DH, EPS, ALU.mult, ALU.add, (sc1.b,), (sc1.b,))
                    RSQ(sc1[0:L, 4:5], sc1[0:L, 3:4], sc1.b)
                    TT('dve', sc1[0:L, 5:6], sc1[0:L, 4:5], sc1[0:L, 1:2], ALU.mult, (sc1.b,), (sc1.b,))
                    ACT(hnb[0:L, :], pN[0:L, :], AF.Identity, (pN.b, sc1.b), (hnb.b,), scale=sc1[0:L, 5:6])
                    for et in range(4):
                        TR(pT[:, 512 + et * 128:512 + et * 128 + L], hnb[0:L, et * 128:(et + 1) * 128], ident_b[0:L, 0:L], (hnb.b, ident_b.b), (pT.b,))
                    TT('dve', hmT[:, h * 4:h * 4 + 4, t0:t0 + L], pT[:, 512:1024].rearrange("p (k t) -> p k t", t=128)[:, :, 0:L],
                       gate[:, :, t0:t0 + L], ALU.mult, (pT.b, gate.b), (hmT.b,))
                    if kind == 'S':
                        for et in range(4):
                            p = PG()
                            for dt in range(4):
                                TR(p[:, dt * 128:(dt + 1) * 128], Cf(dt)[:, et * 128:(et + 1) * 128], ident_f[:, :], (Cfb, ident_f.b), (p.b,))
                            CP('act', Cst[:, et, :], p[:, :], (p.b,), (Cst.b,))
                        S.dma('sp', o_sC[l, j, h].rearrange("(et p) d -> p et d", p=128), Cst[:, :, :], (Cst.b,), ())
            if kind == 'S':
                for sq in range(4):
                    p = PG()
                    TR(p[0:16, 0:128], nS[:, sq, :], ident_f[:, :], (nS.b, ident_f.b), (p.b,))
                    CP('act', stg[0:16, 0:128], p[0:16, 0:128], (p.b,), (stg.b,))
                    S.dma('sp', o_sn[l, sq], stg[0:16, 0:128], (stg.b,), ())

            ck()
            for dg in range(2):
                wm0 = WL('ml', w_ml[l], 0, dg * 512, 512)
                wm1 = WL('ml', w_ml[l], 1024, dg * 512, 512)
                wgm = WL('in', Win, 0, C_GM + dg * 512, 512)
                for di in range(4):
                    dt = dg * 4 + di
                    pb = PG()
                    for r in range(16):
                        wt = wm0 if r < 8 else wm1
                        MM(pb[:, 0:ntok], wt[:, r % 8, di * 128:(di + 1) * 128], hmT[:, r, 0:ntok], r == 0, r == 15, (wt.b, hmT.b), (pb.b,))
                    pgm = fm_proj(wgm, di * 128)
                    ACT(sg[:, 0:ntok], pgm[:, 0:ntok], AF.Sigmoid, (pgm.b, bcol.b), (sg.b,), bias=bcol[:, 112 + dt:113 + dt])
                    TT('dve', tmpa[:, 0:ntok], pb[:, 0:ntok], sg[:, 0:ntok], ALU.mult, (pb.b, sg.b), (tmpa.b,))
                    TT('dve', ybf[:, dt, 0:ntok], tmpa[:, 0:ntok], yacc[:, dt, 0:ntok], ALU.add, (tmpa.b, yacc.b), (ybf.b,))
            wo0 = WL('out', w_out[l], 0, 0, 512)
            wo1 = WL('out', w_out[l], 0, 512, 512)
            for st in range(nsub):
                r = min(128, ntok - st * 128)
                pa = PG(); pb = PG()
                for kc in range(8):
                    MM(pa[0:r, :], ybf[:, kc, st * 128:st * 128 + r], wo0[:, kc, :], kc == 0, kc == 7, (ybf.b, wo0.b), (pa.b,))
                for kc in range(8):
                    MM(pb[0:r, :], ybf[:, kc, st * 128:st * 128 + r], wo1[:, kc, :], kc == 0, kc == 7, (ybf.b, wo1.b), (pb.b,))
                ACT(junk[0:r, 0:512], pa[0:r, :], AF.Square, (pa.b,), (junk.b, sso.b), accum=sso[0:r, 0:1])
                ACT(junk[0:r, 512:1024], pb[0:r, :], AF.Square, (pb.b,), (junk.b, sso.b), accum=sso[0:r, 1:2])
                TT('dve', sso[0:r, 2:3], sso[0:r, 0:1], sso[0:r, 1:2], ALU.add, (sso.b,), (sso.b,))
                TS('dve', sso[0:r, 2:3], sso[0:r, 2:3], 1.0 / D, EPS, ALU.mult, ALU.add, (sso.b,), (sso.b,))
                RSQ(sso[0:r, 3:4], sso[0:r, 2:3], sso.b)
                yt = yst[0]
                STT('dve', yt[0:r, 0:512], pa[0:r, :], sso[0:r, 3:4], gpost[0:r, 0:512], ALU.mult, ALU.mult, (pa.b, sso.b, gpost.b), (yt.b,))
                STT('dve', yt[0:r, 512:1024], pb[0:r, :], sso[0:r, 3:4], gpost[0:r, 512:1024], ALU.mult, ALU.mult, (pb.b, sso.b, gpost.b), (yt.b,))
                S.dma('sp', xres[0][0:r, :], xsrc[st * 128:st * 128 + r, :], xb, (xres[0].b,))
                TT('pool', yt[0:r, :], yt[0:r, :], xres[0][0:r, :], ALU.add, (yt.b, xres[0].b), (yt.b,))
                S.dma('sp', ydst[st * 128:st * 128 + r, :], yt[0:r, :], (yt.b,), yb)

        try:
            tcount = [0]
            dS, dL = Buf("x1s"), Buf("x1l")
            dP = [Buf("x1p%d" % i) for i in range(NPT)]
            for l in range(2):
                ck()
                layer_params(l)
                ck()
                srcS, srcL, srcP = (xs, meta, xp) if l == 0 else (x1s, x1l, x1p)
                dstS, dstL, dstP = (x1s, x1l, x1p) if l == 0 else (ys, ydump, yp)
                tile(l, 'S', tcount[0], srcS, dstS, () if l == 0 else (dS,), (dS,) if l == 0 else ()); tcount[0] += 1
                MS('dve', CPs[:, :, :], 0.0, (CPs.b,))
                MS('dve', nP[:, :], 0.0, (nP.b,)); MS('dve', nPb[:, :], 0.0, (nPb.b,)); MS('dve', mP[:, :], 0.0, (mP.b,))
                tile(l, 'LEAD', tcount[0], srcL, dstL, () if l == 0 else (dL,), (dL,) if l == 0 else ()); tcount[0] += 1
                for pi in range(NPT):
                    tile(l, 'P', tcount[0], srcP[pi * 512:(pi + 1) * 512, :], dstP[pi * 512:(pi + 1) * 512, :],
                         () if l == 0 else (dP[pi],), (dP[pi],) if l == 0 else ()); tcount[0] += 1
                for c in range(8):
                    p = PG()
                    TR(p[0:30, 0:128], uhP[:, c, :], ident_f[:, :], (uhP.b, ident_f.b), (p.b,))
                    CP('act', stg[0:30, c * 128:(c + 1) * 128], p[0:30, 0:128], (p.b,), (stg.b,))
                S.dma('sp', o_pconv[l], stg[0:30, :], (stg.b,), ())
                for h in range(NH):
                    for i in range(8):
                        qk_off = (0 if i < 4 else 2048) + h * 512 + (i % 4) * 128
                        p = PG()
                        TR(p[0:3, 0:128], qkhP[:, h, i, :], ident_f[:, :], (qkhP.b, ident_f.b), (p.b,))
                        CP('act', stg[0:3, 0:128], p[0:3, 0:128], (p.b,), (stg.b,))
                        S.dma('sp', o_pqk[l][:, qk_off:qk_off + 128], stg[0:3, 0:128], (stg.b,), ())
                    for et in range(4):
                        p = PG()
                        for dt in range(4):
                            TR(p[:, dt * 128:(dt + 1) * 128], CPs[:, h * 4 + dt, et * 128:(et + 1) * 128], ident_f[:, :], (CPs.b, ident_f.b), (p.b,))
                        CP('act', Cst[:, et, :], p[:, :], (p.b,), (Cst.b,))
                    S.dma('sp', o_pC[l, h].rearrange("(et p) d -> p et d", p=128), Cst[:, :, :], (Cst.b,), ())
                p = PG()
                TR(p[0:16, 0:128], nP[:, :], ident_f[:, :], (nP.b, ident_f.b), (p.b,))
                CP('act', stg[0:16, 0:128], p[0:16, 0:128], (p.b,), (stg.b,))
                S.dma('sp', o_pn[l], stg[0:16, 0:128], (stg.b,), ())
                S.dma('sp', o_pm[l].rearrange("(h o) -> h o", o=1), mP[:, :], (mP.b,), ())
        except Stop:
            pass
        for _ in range(int(os.environ.get('K_PAD', '0'))):
            if os.environ.get('K_PADENG', 'dve') == 'pe':
                MM(pS[0:4, 300:304], ones_b[0:4, 0:4], ones_b[0:4, 0:4], True, True, (ones_b.b,), (pS.b,))
            elif os.environ.get('K_PADENG', 'dve') == 'act':
                CP('act', sc4[:, 7:8], sc4[:, 6:7], (sc4.b,), (sc4.b,))
            else:
                MS(os.environ.get('K_PADENG', 'dve'), sc4[:, 7:8], 0.0, (sc4.b,))
        S.finish()
        S.emit()
        import os
        if os.environ.get('K_DEBUG'):
            print('CNT', S.cnt, S.dcnt, {k: len(v) for k, v in S.prog.items()})
    return nc


_CACHE = {}


def _consts():
    c = np.zeros((2, 128, 128), np.float32)
    c[0] = np.eye(128, dtype=np.float32)
    c[1] = np.triu(np.ones((128, 128), np.float32))
    return c


def run(inputs, NPT):
    f = lambda a: np.ascontiguousarray(np.asarray(a, dtype=np.float32))
    I = {k: f(v) for k, v in inputs.items()}
    if NPT not in _CACHE:
        _CACHE[NPT] = build(NPT)
    nc = _CACHE[NPT]
    NP = NPT * 512
    in_maps = []
    for c in range(8):
        b = c // 4
        sq = slice(4 * c, 4 * c + 4)
        m = {
            "xs": I["x_sample"][sq].reshape(256, D), "xp": I["x_prompt"][b, :NP], "meta": I["meta_tokens"],
            "st_conv": I["state_conv"][:, sq].reshape(2, 120, D), "st_qk": I["state_qk_conv"][:, sq].reshape(2, 12, 4096),
            "st_C": I["state_C"][:, sq], "st_n": I["state_n"][:, sq].reshape(2, 4, 16, 128), "st_m": I["state_m"][:, sq],
            "w_in": I["w_in"], "b_in": I["b_in"], "w_dw": I["w_dw"], "b_dw": I["b_dw"].reshape(2, 8, 128),
            "ln_g": I["ln_g"].reshape(2, 8, 128), "ln_b": I["ln_b"].reshape(2, 8, 128),
            "w_qkc": I["w_qk_conv"].reshape(2, 4, 32, 128), "b_qkc": I["b_qk_conv"].reshape(2, 32, 128), "f_bias": I["f_bias"],
            "ml_norm": I["ml_norm"].reshape(2, 16, 128), "w_co": I["w_conv_out"], "w_ml": I["w_ml_out"], "w_out": I["w_out"],
            "norm_pre": I["norm_pre"], "norm_post": I["norm_post"], "cst": _consts(),
        }
        in_maps.append({k: np.ascontiguousarray(v) for k, v in m.items()})
    res = run_bass_kernel_spmd(nc, in_maps, core_ids=list(range(8)))
    R = res.results
    g = lambda c, k: np.asarray(R[c][k], dtype=np.float32)
    y_prompt = np.stack([g(0, "yp"), g(4, "yp")], 0)
    y_sample = np.concatenate([g(c, "ys").reshape(4, 64, D) for c in range(8)], 0)
    pc = lambda k, shp: np.stack([g(0, k), g(4, k)], 1).reshape(shp)
    sc = lambda k, shp: np.concatenate([g(c, k).reshape((2, 4) + shp) for c in range(8)], 1)
    return (y_prompt, y_sample,
            pc("o_pconv", (2, 2, 30, D)), pc("o_pqk", (2, 2, 3, 4096)), pc("o_pC", (2, 2, 4, DH, DH)),
            pc("o_pn", (2, 2, 4, DH)), pc("o_pm", (2, 2, 4)),
            sc("o_sconv", (30, D)), sc("o_sqk", (3, 4096)), sc("o_sC", (4, DH, DH)), sc("o_sn", (4, DH)), sc("o_sm", (4,)))


def kernel(**inputs):
    return run(inputs, 16)
```
